# Optimizing a Trainium2 kernel written in Bass

```python
import jax, jax.numpy as jnp
from jax import lax
import numpy as np

D_MODEL = 2048
BATCH = 4
SEQ = 4096
DEPTH = 4

CHUNK = 64
MIX_WIDTH = D_MODEL
POOL_WINDOWS = (2, 4, 8, 16)
POOL_GROUPS = len(POOL_WINDOWS)
POOL_WIDTH = MIX_WIDTH // 4
POOL_GROUP = POOL_WIDTH // POOL_GROUPS
CONV_WIDTH = MIX_WIDTH // 4
CONV_KERNEL = 31
ATTN_WIDTH = MIX_WIDTH - POOL_WIDTH - CONV_WIDTH
ATTN_HEADS = 8
ATTN_HEAD_DIM = ATTN_WIDTH // ATTN_HEADS
LEFT_CHUNKS = 8
BAND = (LEFT_CHUNKS + 1) * CHUNK
REL_LEFT = 128
REL_RIGHT = CHUNK - 1
REL_SIZE = REL_LEFT + REL_RIGHT + 1
OFF_CONV = POOL_WIDTH
OFF_GATE = OFF_CONV + CONV_WIDTH
OFF_Q = OFF_GATE + CONV_WIDTH
OFF_K = OFF_Q + ATTN_WIDTH
OFF_V = OFF_K + ATTN_WIDTH
IN_WIDTH = OFF_V + ATTN_WIDTH
PEER_HEADS = 8
PEER_NKEYS = 128
PEER_EXPERTS = PEER_NKEYS * PEER_NKEYS
PEER_QDIM = 256
PEER_HALF = PEER_QDIM // 2
PEER_TOPK = 16
PEER_TOKEN_BLOCK = 128
PLE_DIM = 256
LN_EPS = 1e-5
DEEPNORM_ALPHA = (2 * DEPTH) ** 0.25
DEEPNORM_BETA = (8 * DEPTH) ** -0.25

kernel_name = "hybrid_pool_conv_chunkattn_peer_deepnorm"


def layer_norm(x, g, b):
    xf = x.astype(jnp.float32)
    mu = jnp.mean(xf, axis=-1, keepdims=True)
    var = jnp.mean(jnp.square(xf - mu), axis=-1, keepdims=True)
    return ((xf - mu) * lax.rsqrt(var + LN_EPS) * g + b).astype(x.dtype)


def pool_mixer(xa, w_pool, pool_scale):
    b, s, _ = xa.shape
    xg = xa.reshape(b, s, POOL_GROUPS, POOL_GROUP).astype(jnp.float32)
    cs = jnp.concatenate([jnp.zeros_like(xg[:, :1]), jnp.cumsum(xg, axis=1)], axis=1)
    t = jnp.arange(s)
    pooled = []
    for g, w in enumerate(POOL_WINDOWS):
        lo = jnp.maximum(t + 1 - w, 0)
        win_sum = cs[:, 1:, g] - cs[:, lo, g]
        count = jnp.minimum(t + 1, w).astype(jnp.float32)
        pooled.append(win_sum / count[None, :, None])
    pooled = jnp.stack(pooled, axis=2) - xg
    y = jnp.einsum('bsgc,gcd->bsgd', pooled.astype(xa.dtype), w_pool)
    return y.reshape(b, s, POOL_WIDTH) * pool_scale


def conv_module(xv, xg, dw_w, dw_b, cn_g, cn_b):
    h = xv * jax.nn.sigmoid(xg)
    h = lax.conv_general_dilated(
        h, dw_w.astype(h.dtype), window_strides=(1,), padding=[(CONV_KERNEL - 1, 0)],
        dimension_numbers=('NWC', 'WIO', 'NWC'), feature_group_count=CONV_WIDTH) + dw_b
    return jax.nn.silu(layer_norm(h, cn_g, cn_b))


def chunk_attention(q, k, v, rel_table):
    b, s, _ = q.shape
    nc = s // CHUNK
    pad = LEFT_CHUNKS * CHUNK
    q = q.reshape(b, s, ATTN_HEADS, ATTN_HEAD_DIM)
    kp = jnp.pad(k.reshape(b, s, ATTN_HEADS, ATTN_HEAD_DIM), ((0, 0), (pad, 0), (0, 0), (0, 0)))
    vp = jnp.pad(v.reshape(b, s, ATTN_HEADS, ATTN_HEAD_DIM), ((0, 0), (pad, 0), (0, 0), (0, 0)))
    dist = jnp.arange(CHUNK)[:, None] - jnp.arange(BAND)[None, :] + pad
    rel_idx = jnp.clip(dist, -REL_RIGHT, REL_LEFT) + REL_RIGHT
    bias = rel_table[:, rel_idx].astype(jnp.float32)
    scale = ATTN_HEAD_DIM ** -0.5

    def one_chunk(c):
        start = c * CHUNK
        qc = lax.dynamic_slice_in_dim(q, start, CHUNK, axis=1)
        kb = lax.dynamic_slice_in_dim(kp, start, BAND, axis=1)
        vb = lax.dynamic_slice_in_dim(vp, start, BAND, axis=1)
        sc = jnp.einsum('bqhd,bkhd->bhqk', qc, kb).astype(jnp.float32) * scale + bias
        valid = (start - pad + jnp.arange(BAND)) >= 0
        sc = jnp.where(valid, sc, -jnp.inf)
        pr = jax.nn.softmax(sc, axis=-1).astype(vb.dtype)
        return jnp.einsum('bhqk,bkhd->bqhd', pr, vb)

    out = lax.map(one_chunk, jnp.arange(nc))
    return out.transpose(1, 0, 2, 3, 4).reshape(b, s, ATTN_WIDTH)


def peer(x, w_q, sub_keys, u_tab, v_tab):
    b, s, d = x.shape
    t = b * s
    xf = x.reshape(t, d)
    q = (xf @ w_q).reshape(t, PEER_HEADS, 2, PEER_HALF)
    sc = jnp.einsum('thpc,pkc->thpk', q, sub_keys).astype(jnp.float32)
    s1, i1 = lax.top_k(sc[:, :, 0], PEER_TOPK)
    s2, i2 = lax.top_k(sc[:, :, 1], PEER_TOPK)
    n_cand = PEER_TOPK * PEER_TOPK
    cand_s = (s1[..., :, None] + s2[..., None, :]).reshape(t, PEER_HEADS, n_cand)
    cand_i = (i1[..., :, None] * PEER_NKEYS + i2[..., None, :]).reshape(t, PEER_HEADS, n_cand)
    top_s, pos = lax.top_k(cand_s, PEER_TOPK)
    idx = jnp.take_along_axis(cand_i, pos, axis=-1)
    gate = jax.nn.softmax(top_s, axis=-1)
    nb = t // PEER_TOKEN_BLOCK
    sel = PEER_HEADS * PEER_TOPK
    xb = xf.reshape(nb, PEER_TOKEN_BLOCK, d)
    ib = idx.reshape(nb, PEER_TOKEN_BLOCK, sel)
    gb = gate.reshape(nb, PEER_TOKEN_BLOCK, sel).astype(x.dtype)

    def block(args):
        xt, it, gt = args
        u = u_tab[it]
        a = jax.nn.gelu(jnp.einsum('td,tkd->tk', xt, u), approximate=False) * gt
        return jnp.einsum('tk,tkd->td', a, v_tab[it])

    y = lax.map(block, (xb, ib, gb))
    return y.reshape(b, s, d)


def setup_inputs(seed: int = 0) -> dict:
    key = jax.random.key(seed)
    ks = jax.random.split(key, 24)

    def nrm(k, shape, scale):
        return jax.random.normal(k, shape, jnp.float32) * scale

    return {
        "x": nrm(ks[0], (BATCH, SEQ, D_MODEL), 1.0),
        "p": nrm(ks[1], (DEPTH, BATCH, SEQ, PLE_DIM), 1.0),
        "w_in": nrm(ks[2], (DEPTH, D_MODEL, IN_WIDTH), D_MODEL ** -0.5),
        "b_in": nrm(ks[3], (DEPTH, IN_WIDTH), 0.02),
        "w_pool": nrm(ks[4], (DEPTH, POOL_GROUPS, POOL_GROUP, POOL_GROUP), POOL_GROUP ** -0.5),
        "pool_scale": 1.0 + nrm(ks[5], (DEPTH, POOL_WIDTH), 0.1),
        "dw_w": nrm(ks[6], (DEPTH, CONV_KERNEL, 1, CONV_WIDTH), CONV_KERNEL ** -0.5),
        "dw_b": nrm(ks[7], (DEPTH, CONV_WIDTH), 0.02),
        "cn_g": 1.0 + nrm(ks[8], (DEPTH, CONV_WIDTH), 0.05),
        "cn_b": nrm(ks[9], (DEPTH, CONV_WIDTH), 0.02),
        "rel_bias": nrm(ks[10], (DEPTH, ATTN_HEADS, REL_SIZE), 0.5),
        "w_out": nrm(ks[11], (DEPTH, MIX_WIDTH, D_MODEL), MIX_WIDTH ** -0.5 * DEEPNORM_BETA),
        "ln1_g": 1.0 + nrm(ks[12], (DEPTH, D_MODEL), 0.05),
        "ln1_b": nrm(ks[13], (DEPTH, D_MODEL), 0.02),
        "w_q": nrm(ks[14], (DEPTH, D_MODEL, PEER_HEADS * PEER_QDIM), D_MODEL ** -0.5),
        "sub_keys": nrm(ks[15], (DEPTH, 2, PEER_NKEYS, PEER_HALF), PEER_HALF ** -0.5),
        "u_tab": nrm(ks[16], (DEPTH, PEER_EXPERTS, D_MODEL), D_MODEL ** -0.5),
        "v_tab": nrm(ks[17], (DEPTH, PEER_EXPERTS, D_MODEL), DEEPNORM_BETA * PEER_HEADS ** -0.5),
        "w_ple": nrm(ks[18], (DEPTH, PLE_DIM, D_MODEL), PLE_DIM ** -0.5 * DEEPNORM_BETA),
        "w_pg": nrm(ks[19], (DEPTH, D_MODEL, D_MODEL), D_MODEL ** -0.5),
        "ln2_g": 1.0 + nrm(ks[20], (DEPTH, D_MODEL), 0.05),
        "ln2_b": nrm(ks[21], (DEPTH, D_MODEL), 0.02),
    }


def reference(x, p, w_in, b_in, w_pool, pool_scale, dw_w, dw_b, cn_g, cn_b, rel_bias, w_out,
              ln1_g, ln1_b, w_q, sub_keys, u_tab, v_tab, w_ple, w_pg, ln2_g, ln2_b):
    for i in range(DEPTH):
        h = x @ w_in[i] + b_in[i]
        y_pool = pool_mixer(h[..., :OFF_CONV], w_pool[i], pool_scale[i])
        y_conv = conv_module(h[..., OFF_CONV:OFF_GATE], h[..., OFF_GATE:OFF_Q],
                             dw_w[i], dw_b[i], cn_g[i], cn_b[i])
        y_attn = chunk_attention(h[..., OFF_Q:OFF_K], h[..., OFF_K:OFF_V], h[..., OFF_V:], rel_bias[i])
        mixed = jnp.concatenate([y_pool, y_conv, y_attn], axis=-1) @ w_out[i]
        x = layer_norm(DEEPNORM_ALPHA * x + mixed, ln1_g[i], ln1_b[i])
        ffn = peer(x, w_q[i], sub_keys[i], u_tab[i], v_tab[i])
        ple = (p[i] @ w_ple[i]) * jax.nn.sigmoid(x @ w_pg[i])
        x = layer_norm(DEEPNORM_ALPHA * x + ffn + ple, ln2_g[i], ln2_b[i])
    return x
```

```python
import numpy as np
import concourse.bass as bass
import concourse.mybir as mybir
from concourse.bass_utils import run_bass_kernel_spmd

F32 = mybir.dt.float32
BF16 = mybir.dt.bfloat16
U32 = mybir.dt.uint32
I32 = mybir.dt.int32
AF = mybir.ActivationFunctionType
ALU = mybir.AluOpType

ENGS = ("sync", "scalar", "vector", "gpsimd", "tensor")


class Buf:
    def __init__(self, name, t=None):
        self.name = name
        self.t = t
        self.writers = {}
        self.readers = {}
        self.dsem = None
        self.dval = 0


class Prog:
    def __init__(self, nc):
        self.nc = nc
        self.q = {e: [] for e in ENGS}
        self.sems = {}
        self.waited = {}
        self._dbufs = {}
        self._named = {}
        self._sets = {}
        self.cur = None
        self.use_semset(0)

    def use_semset(self, k):
        if self.cur is not None:
            self._sets[self.cur]["cnt"] = dict(self.ecnt)
        if k not in self._sets:
            keys = {e: ("e", e, k) for e in ENGS}
            self._sets[k] = {"keys": keys, "cnt": {e: 0 for e in ENGS}}
        self.cur = k
        self.ekey = dict(self._sets[k]["keys"])
        self.ecnt = dict(self._sets[k]["cnt"])

    def sb(self, name, shape, dtype):
        return Buf(name, self.nc.alloc_sbuf_tensor("s_" + name, list(shape), dtype))

    def ps(self, name, shape, dtype=F32):
        return Buf(name, self.nc.alloc_psum_tensor(name, list(shape), dtype))

    def dram(self, name):
        return Buf(name, None)

    def _dsem(self, buf):
        if buf.dsem is None:
            key = ("d", buf.name)
            if key in self._named:
                buf.dval = self._named[key].dval
            else:
                self.sems[key] = self.nc.alloc_semaphore("ds_" + buf.name)
            self._named[key] = buf
            self._dbufs[key] = buf
            buf.dsem = key
        return buf.dsem

    def _waits(self, eng, deps, skip_same_engine=False):
        for key, val in deps.items():
            if skip_same_engine and key[0] == "e" and key[1] == eng:
                continue
            if self.waited.get((eng, key), 0) >= val:
                continue
            self.waited[(eng, key)] = val
            sem = self.sems[key]
            self.q[eng].append(lambda e, sem=sem, val=val: e.wait_ge(sem, val))

    @staticmethod
    def _merge(dst, src):
        for k, v in src.items():
            if dst.get(k, 0) < v:
                dst[k] = v

    def op(self, eng, fn, reads=(), writes=()):
        deps = {}
        for b in reads:
            self._merge(deps, b.writers)
        for b in writes:
            self._merge(deps, b.writers)
            self._merge(deps, b.readers)
        self._waits(eng, deps, skip_same_engine=(eng == "tensor"))
        self.ecnt[eng] += 1
        val = self.ecnt[eng]
        key = self.ekey[eng]
        if key not in self.sems:
            self.sems[key] = self.nc.alloc_semaphore(f"es_{key[1]}_{key[2]}")
        sem = self.sems[key]
        self.q[eng].append(lambda e, fn=fn, sem=sem: fn(e).then_inc(sem, 1))
        for b in writes:
            b.writers = {key: val}
            b.readers = {}
        for b in reads:
            if b.readers.get(key, 0) < val:
                b.readers[key] = val

    def dma(self, eng, dst, src, fn, reads=(), accumulate=False):
        key = self._dsem(dst)
        deps = {}
        if src is not None:
            self._merge(deps, src.writers)
        for b in reads:
            self._merge(deps, b.writers)
        w = dict(dst.writers)
        if accumulate:
            w.pop(key, None)
        self._merge(deps, w)
        self._merge(deps, dst.readers)
        self._waits(eng, deps)
        dst.dval += 16
        val = dst.dval
        sem = self.sems[key]
        self.q[eng].append(lambda e, fn=fn, sem=sem: fn(e).then_inc(sem, 16))
        dst.writers = {key: val}
        dst.readers = {}
        for b in ([src] if src is not None else []) + list(reads):
            if b.readers.get(key, 0) < val:
                b.readers[key] = val

    def finish(self, out_bufs):
        deps = {}
        for b in out_bufs:
            self._merge(deps, b.writers)
        self._waits("sync", deps)
        nc = self.nc
        q = self.q
        with nc.Block() as block:
            @block.sync
            def _(e):
                for f in q["sync"]:
                    f(e)

            @block.scalar
            def _(e):
                for f in q["scalar"]:
                    f(e)

            @block.vector
            def _(e):
                for f in q["vector"]:
                    f(e)

            @block.gpsimd
            def _(e):
                for f in q["gpsimd"]:
                    f(e)

            @block.tensor
            def _(e):
                for f in q["tensor"]:
                    f(e)


    def barrier(self):
        self._sets[self.cur]["cnt"] = dict(self.ecnt)
        deps = {}
        for st in self._sets.values():
            for e in ENGS:
                if st["cnt"][e] > 0:
                    deps[st["keys"][e]] = st["cnt"][e]
        for key, b in self._dbufs.items():
            deps[key] = b.dval
        for e in ENGS:
            self._waits(e, deps)


D = 2048
HALO = 512
KC = 16
IN_WIDTH = 4608
ALPHA = float(8 ** 0.25)
LN_EPS = 1e-5
ATT_SCALE = float(128 ** -0.5)
NEG = -30000.0
POOL_W = (2, 4, 8, 16)
NUV = 8


_UNIQ = [0]


def _emit_layer(nc, P, ps, A, NW_L, NINV, consts, dbufs, semset0, conv):
    from contextlib import ExitStack
    HB = HALO // 512
    ident, ones_bf, ones_f, flags, invc, ident_bf = consts
    psrot = [0]

    def nps():
        psrot[0] = (psrot[0] + 1) % 8
        return ps[psrot[0]]

    def sbs(es, name, shape, dt):
        _UNIQ[0] += 1
        t = es.enter_context(nc.sbuf_tensor(f"s{_UNIQ[0]}_{name}", list(shape), dt))
        return Buf(name, t)

    def OP(eng, f, reads, writes):
        P.op(eng, f, reads=reads, writes=writes)

    def MM(pb, out_ap, lb, l_ap, rb, r_ap, start, stop):
        P.op("tensor", lambda e: e.matmul(out_ap, lhsT=l_ap, rhs=r_ap, start=start, stop=stop),
             reads=[lb, rb], writes=[pb])

    def TR(pb, out_ap, sb_, in_ap):
        P.op("tensor", lambda e: e.transpose(out=out_ap, in_=in_ap, identity=ident.t[:, :]),
             reads=[sb_, ident], writes=[pb])

    def ACT(ob, out_ap, ib, in_ap, func, bias=None, scale=None, reads=()):
        kw = {}
        if bias is not None:
            kw["bias"] = bias
        if scale is not None:
            kw["scale"] = scale
        P.op("scalar", lambda e: e.activation(out=out_ap, in_=in_ap, func=func, **kw),
             reads=[ib] + list(reads), writes=[ob])

    cstate = [conv]

    def conv_step(n):
        for _ in range(n):
            if cstate[0] is None:
                return
            try:
                next(cstate[0])
            except StopIteration:
                cstate[0] = None

    def DMA(q, dst, out_ap, src, in_ap, reads=(), acc=False):
        P.dma(q, dst, src, lambda e: e.dma_start(out=out_ap, in_=in_ap), reads=reads, accumulate=acc)
        if q == "gpsimd":
            if cdefer[0]:
                cpend[0] += 3
            else:
                conv_step(3)

    cdefer = [False]
    cpend = [0]

    def defer_on():
        cdefer[0] = True

    def defer_off():
        cdefer[0] = False
        conv_step(cpend[0])
        cpend[0] = 0

    def TT(ob, out_ap, ab, a_ap, bb, b_ap, op, eng="vector"):
        P.op(eng, lambda e: e.tensor_tensor(out=out_ap, in0=a_ap, in1=b_ap, op=op), reads=[ab, bb], writes=[ob])

    def TS(ob, out_ap, ib, in_ap, s1, op0, s2=None, op1=None, reads=(), eng="vector"):
        if op1 is None:
            P.op(eng, lambda e: e.tensor_scalar(out=out_ap, in0=in_ap, scalar1=s1, scalar2=None, op0=op0),
                 reads=[ib] + list(reads), writes=[ob])
        else:
            P.op(eng, lambda e: e.tensor_scalar(out=out_ap, in0=in_ap, scalar1=s1, scalar2=s2, op0=op0, op1=op1),
                 reads=[ib] + list(reads), writes=[ob])

    def STT(ob, out_ap, ab, a_ap, scalar, bb, b_ap, op0, op1, reads=(), accum=None, accb=None):
        kw = {}
        w = [ob]
        if accum is not None:
            kw["accum_out"] = accum
            w.append(accb)
        P.op("vector", lambda e: e.scalar_tensor_tensor(out=out_ap, in0=a_ap, scalar=scalar, in1=b_ap,
                                                        op0=op0, op1=op1, **kw),
             reads=[ab, bb] + list(reads), writes=w)

    d_cat, d_x1, d_x1T, d_r2, d_top, d_xin, d_y, d_tab = dbufs
    catT_r = A["catT"].rearrange("(kc p) t -> p kc t", p=128)
    x1T_r = A["x1T"].rearrange("(kc p) t -> p kc t", p=128)
    w_in_r = A["w_in"].rearrange("(kc p) n -> p kc n", p=128)

    def mixer_segment(o0, NW, ninv):
        NWIN = HALO + NW
        NT = NW // 128
        NTW = NWIN // 128
        NB = NW // 512
        NBW = NWIN // 512
        fpos = NINV - HALO - o0
        with ExitStack() as es1:
            xT = sbs(es1, "xT", [128, KC, NWIN], BF16)
            wch = [sbs(es1, f"wch{i}", [128, KC, 128], BF16) for i in range(2)]
            b_in_fm = sbs(es1, "b_in_fm", [128, 36], F32)
            yfm = [sbs(es1, f"yfm{i}", [128, NW], BF16) for i in range(2)]
            DMA("sync", b_in_fm, b_in_fm.t[:, :], None, A["b_in_fm"])
            with ExitStack() as es0:
                xin = [sbs(es0, f"xin{i}", [128, D], F32) for i in range(2)]
                for i in range(NTW):
                    xb = xin[i % 2]
                    DMA("sync", xb, xb.t[:, :], d_xin, A["x_win"][o0 + i * 128:o0 + (i + 1) * 128, :])
                    for g in range(4):
                        pb = nps()
                        for j in range(4):
                            kc = g * 4 + j
                            TR(pb, pb.t[:, j * 128:(j + 1) * 128], xb, xb.t[:, kc * 128:(kc + 1) * 128])
                        o_ap = xT.t[:, g * 4:(g + 1) * 4, i * 128:(i + 1) * 128]
                        i_ap = pb.t[:, :].rearrange("p (a b) -> p a b", a=4)
                        if g % 2 == 0:
                            ACT(xT, o_ap, pb, i_ap, AF.Identity)
                        else:
                            P.op("vector", lambda e, o_ap=o_ap, i_ap=i_ap: e.tensor_copy(out=o_ap, in_=i_ap),
                                 reads=[pb], writes=[xT])
                P.barrier()

            wrot = [0]

            def proj_fm(c, blk_lo, evac):
                wb = wch[wrot[0] % 2]
                wrot[0] += 1
                DMA("gpsimd", wb, wb.t[:, :, :], None, w_in_r[:, :, c * 128:(c + 1) * 128])
                for tb in range(blk_lo, NBW):
                    pb = nps()
                    for kc in range(KC):
                        MM(pb, pb.t[:, :], wb, wb.t[:, kc, :], xT, xT.t[:, kc, tb * 512:(tb + 1) * 512], kc == 0, kc == KC - 1)
                    evac(tb, pb)

            def evac_f32(dst, c):
                def f(tb, pb):
                    ACT(dst, dst.t[:, tb * 512:(tb + 1) * 512], pb, pb.t[:, :], AF.Identity,
                        bias=b_in_fm.t[:, c:c + 1], scale=1.0, reads=[b_in_fm])
                return f

            with ExitStack() as es:
                hA = sbs(es, "hA", [128, NWIN], F32)
                hB = sbs(es, "hB", [128, NWIN], F32)
                hC = sbs(es, "hC", [128, NWIN], F32)
                convo = sbs(es, "convo", [128, 4, NW], F32)
                pooled = sbs(es, "pooled", [128, NW], BF16)
                wpool = sbs(es, "wpool", [128, 4, 128], BF16)
                psc = sbs(es, "psc", [128, 4], F32)
                dww = sbs(es, "dww", [128, 4, 31], F32)
                dwb = sbs(es, "dwb", [128, 4], F32)
                cng = sbs(es, "cng", [128, 4], F32)
                cnb = sbs(es, "cnb", [128, 4], F32)
                t16 = sbs(es, "t16", [128, 16], F32)
                sqb = sbs(es, "sqb", [128, 512], F32)
                meanb = sbs(es, "meanb", [128, 512], F32)
                varb = sbs(es, "varb", [128, 512], F32)
                tnb = sbs(es, "tnb", [128, 512], F32)
                DMA("gpsimd", wpool, wpool.t[:, :, :], None, A["w_pool"].rearrange("g c d -> c g d"))
                DMA("sync", psc, psc.t[:, :], None, A["psc_fm"])
                DMA("sync", dww, dww.t[:, :, :], None, A["dww_fm"])
                DMA("sync", dwb, dwb.t[:, :], None, A["dwb_fm"])
                DMA("sync", cng, cng.t[:, :], None, A["cng_fm"])
                DMA("sync", cnb, cnb.t[:, :], None, A["cnb_fm"])
                for g in range(4):
                    proj_fm(g, 0, evac_f32(hA, g))
                    if ninv > 0:
                        TS(hA, hA.t[:, 0:ninv], hA, hA.t[:, 0:ninv], flags.t[:, 0:1], ALU.mult, reads=[flags])
                    src = hA
                    bufs = [hB, hC]
                    for si, st in enumerate((1, 2, 4, 8)[:g + 1]):
                        dst = bufs[si % 2]
                        TT(dst, dst.t[:, 16:NWIN], src, src.t[:, 16:NWIN], src, src.t[:, 16 - st:NWIN - st], ALU.add)
                        src = dst
                    STT(pooled, pooled.t[:, :], src, src.t[:, HALO:NWIN], 1.0 / POOL_W[g], hA, hA.t[:, HALO:NWIN],
                        ALU.mult, ALU.subtract)
                    if 0 <= fpos < NW:
                        TT(t16, t16.t[:, :], src, src.t[:, HALO + fpos:HALO + fpos + 16], invc, invc.t[:, g * 16:(g + 1) * 16], ALU.mult)
                        TT(pooled, pooled.t[:, fpos:fpos + 16], t16, t16.t[:, :], hA, hA.t[:, HALO + fpos:HALO + fpos + 16], ALU.subtract)
                    yb = yfm[g % 2]
                    for blk in range(NB):
                        pb = nps()
                        MM(pb, pb.t[:, :], wpool, wpool.t[:, g, :], pooled, pooled.t[:, blk * 512:(blk + 1) * 512], True, True)
                        ACT(yb, yb.t[:, blk * 512:(blk + 1) * 512], pb, pb.t[:, :], AF.Identity,
                            scale=psc.t[:, g:g + 1], reads=[psc])
                    DMA("sync", d_cat, A["catT"][g * 128:(g + 1) * 128, o0:o0 + NW], yb, yb.t[:, :], acc=True)
                for j in range(4):
                    proj_fm(4 + j, 0, evac_f32(hA, 4 + j))
                    proj_fm(8 + j, 0, evac_f32(hB, 8 + j))
                    ACT(hB, hB.t[:, :], hB, hB.t[:, :], AF.Sigmoid)
                    TT(hA, hA.t[:, :], hA, hA.t[:, :], hB, hB.t[:, :], ALU.mult)
                    if ninv > 0:
                        TS(hA, hA.t[:, 0:ninv], hA, hA.t[:, 0:ninv], flags.t[:, 0:1], ALU.mult, reads=[flags])
                    acc = convo.t[:, j, :]
                    TS(convo, acc, hA, hA.t[:, HALO - 30:HALO - 30 + NW], dww.t[:, j, 0:1], ALU.mult,
                       dwb.t[:, j:j + 1], ALU.add, reads=[dww, dwb])
                    for k in range(1, 31):
                        STT(convo, acc, hA, hA.t[:, HALO - 30 + k:HALO - 30 + k + NW], dww.t[:, j, k:k + 1],
                            convo, acc, ALU.mult, ALU.add, reads=[dww])
                for blk in range(NB):
                    sl = slice(blk * 512, (blk + 1) * 512)
                    pm = nps()
                    pq = nps()
                    for j in range(4):
                        ACT(sqb, sqb.t[:, :], convo, convo.t[:, j, sl], AF.Square)
                        MM(pm, pm.t[:, :], ones_f, ones_f.t[:, :], convo, convo.t[:, j, sl], j == 0, j == 3)
                        MM(pq, pq.t[:, :], ones_f, ones_f.t[:, :], sqb, sqb.t[:, :], j == 0, j == 3)
                    ACT(meanb, meanb.t[:, :], pm, pm.t[:, :], AF.Identity)
                    TT(varb, varb.t[:, :], meanb, meanb.t[:, :], meanb, meanb.t[:, :], ALU.mult)
                    TT(varb, varb.t[:, :], pq, pq.t[:, :], varb, varb.t[:, :], ALU.subtract)
                    TS(varb, varb.t[:, :], varb, varb.t[:, :], LN_EPS, ALU.add)
                    ACT(varb, varb.t[:, :], varb, varb.t[:, :], AF.Sqrt)
                    P.op("vector", lambda e: e.reciprocal(out=varb.t[:, :], in_=varb.t[:, :]), reads=[varb], writes=[varb])
                    for j in range(4):
                        yb = yfm[j % 2]
                        TT(tnb, tnb.t[:, :], convo, convo.t[:, j, sl], meanb, meanb.t[:, :], ALU.subtract)
                        TT(tnb, tnb.t[:, :], tnb, tnb.t[:, :], varb, varb.t[:, :], ALU.mult)
                        ACT(yb, yb.t[:, 0:512], tnb, tnb.t[:, :], AF.Silu, bias=cnb.t[:, j:j + 1], scale=cng.t[:, j:j + 1],
                            reads=[cnb, cng])
                        DMA("sync", d_cat, A["catT"][(4 + j) * 128:(5 + j) * 128, o0 + blk * 512:o0 + (blk + 1) * 512], yb, yb.t[:, 0:512], acc=True)
                P.barrier()

            with ExitStack() as es:
                biasT = sbs(es, "biasT", [128, 8, 5, 128], F32)
                bvb = sbs(es, "bvb", [128, 1024], F32)
                qT = sbs(es, "qT", [128, NW], BF16)
                kT = sbs(es, "kT", [128, NWIN], BF16)
                Vh = sbs(es, "Vh", [128, NTW, 128], BF16)
                tmpS = [sbs(es, f"tmpS{i}", [128, 5, 128], F32) for i in range(2)]
                PT = [sbs(es, f"PT{i}", [128, 5, 128], BF16) for i in range(2)]
                rden = sbs(es, "rden", [128, 128], F32)
                DMA("sync", biasT, biasT.t[:, :, :, :], None, A["abias"].rearrange("p (h k q) -> p h k q", h=8, k=5))
                DMA("sync", bvb, bvb.t[:, :], None, A["bv_row"].partition_broadcast(128))
                SA = [ps[0], ps[1]]
                SB = [ps[2], ps[3]]
                psO = [ps[4], ps[5]]
                psD = psO
                for h in range(8):
                    cq = 12 + h
                    ck = 20 + h

                    def evq(tb, pb, cq=cq):
                        TS(qT, qT.t[:, (tb - HB) * 512:(tb - HB + 1) * 512], pb, pb.t[:, :], b_in_fm.t[:, cq:cq + 1], ALU.add,
                           ATT_SCALE, ALU.mult, reads=[b_in_fm])

                    def evk(tb, pb, ck=ck):
                        ACT(kT, kT.t[:, tb * 512:(tb + 1) * 512], pb, pb.t[:, :], AF.Identity,
                            bias=b_in_fm.t[:, ck:ck + 1], scale=1.0, reads=[b_in_fm])

                    psrot[0] = 5
                    def nps_proj():
                        psrot[0] = 6 if psrot[0] != 6 else 7
                        return ps[psrot[0]]

                    for (c, lo, ev) in ((cq, HB, evq), (ck, 0, evk)):
                        wb = wch[wrot[0] % 2]
                        wrot[0] += 1
                        DMA("gpsimd", wb, wb.t[:, :, :], None, w_in_r[:, :, c * 128:(c + 1) * 128])
                        for tb in range(lo, NBW):
                            pb = nps_proj()
                            for kc in range(KC):
                                MM(pb, pb.t[:, :], wb, wb.t[:, kc, :], xT, xT.t[:, kc, tb * 512:(tb + 1) * 512], kc == 0, kc == KC - 1)
                            ev(tb, pb)
                    wb = wch[wrot[0] % 2]
                    wrot[0] += 1
                    cv = 28 + h
                    DMA("gpsimd", wb, wb.t[:, :, :], None, w_in_r[:, :, cv * 128:(cv + 1) * 128])
                    for tg in range(NTW // 4):
                        pb = nps_proj()
                        for j in range(4):
                            ti = tg * 4 + j
                            for kc in range(KC):
                                MM(pb, pb.t[:, j * 128:(j + 1) * 128], xT, xT.t[:, kc, ti * 128:(ti + 1) * 128], wb, wb.t[:, kc, :],
                                   kc == 0, kc == KC - 1)
                        TT(Vh, Vh.t[:, tg * 4:(tg + 1) * 4, :], pb, pb.t[:, :].rearrange("p (a b) -> p a b", a=4),
                           bvb, bvb.t[:, h * 128:(h + 1) * 128].unsqueeze(1).to_broadcast([128, 4, 128]), ALU.add)
                    ya = yfm[h % 2]

                    def stA(pi):
                        s_ = pi % 2
                        for kb in range(5):
                            wt = pi + kb
                            pb = SA[s_] if kb < 4 else SB[s_]
                            o_ap = pb.t[:, (kb % 4) * 128:(kb % 4 + 1) * 128]
                            MM(pb, o_ap, kT, kT.t[:, wt * 128:(wt + 1) * 128], qT, qT.t[:, pi * 128:(pi + 1) * 128], True, True)

                    def stB(pi):
                        s_ = pi % 2
                        for kb in range(5):
                            wt = pi + kb
                            pb = SA[s_] if kb < 4 else SB[s_]
                            i_ap = pb.t[:, (kb % 4) * 128:(kb % 4 + 1) * 128]
                            sc_ = flags.t[:, 1:2] if wt * 128 < ninv else 0.0
                            STT(tmpS[s_], tmpS[s_].t[:, kb, :], pb, i_ap, sc_, biasT, biasT.t[:, h, kb, :], ALU.add, ALU.add,
                                reads=[flags])
                        ACT(PT[s_], PT[s_].t[:, :, :], tmpS[s_], tmpS[s_].t[:, :, :], AF.Exp)

                    def stC(pi):
                        s_ = pi % 2
                        for kb in range(5):
                            wt = pi + kb
                            MM(psO[s_], psO[s_].t[:, 128:256], Vh, Vh.t[:, wt, :], PT[s_], PT[s_].t[:, kb, :], kb == 0, kb == 4)
                        for kb in range(5):
                            MM(psD[s_], psD[s_].t[:, 256:384], ones_bf, ones_bf.t[:, :], PT[s_], PT[s_].t[:, kb, :], kb == 0, kb == 4)

                    def stE(pi):
                        s_ = pi % 2
                        TS(rden, rden.t[:, :], psD[s_], psD[s_].t[:, 256:384], 1e-30, ALU.add)
                        P.op("vector", lambda e: e.reciprocal(out=rden.t[:, :], in_=rden.t[:, :]), reads=[rden], writes=[rden])
                        TT(ya, ya.t[:, pi * 128:(pi + 1) * 128], psO[s_], psO[s_].t[:, 128:256], rden, rden.t[:, :], ALU.mult)

                    for t in range(NT + 3):
                        if t < NT:
                            stA(t)
                        if 0 <= t - 1 < NT:
                            stB(t - 1)
                        if 0 <= t - 2 < NT:
                            stC(t - 2)
                        if 0 <= t - 3 < NT:
                            stE(t - 3)
                    DMA("sync", d_cat, A["catT"][(8 + h) * 128:(9 + h) * 128, o0:o0 + NW], ya, ya.t[:, :], acc=True)
                P.barrier()


    NW = NW_L
    NT = NW // 128
    NB = NW // 512
    segs = []
    o = 0
    while o < NW_L:
        n = min(2048 if (NW_L - o) != 2560 else 1536, NW_L - o)
        if NW_L - o > 2048 and NW_L - o - n < 1024:
            n = NW_L - o - 1024
        segs.append((o, n))
        o += n
    for (o0, n) in segs:
        P.use_semset(semset0)
        mixer_segment(o0, n, max(0, min(NINV - o0, HALO + n)))

    P.use_semset(5)
    with ExitStack() as es:
        wout = sbs(es, "wout", [128, KC, D], BF16)
        g1b = sbs(es, "g1b", [128, D], F32)
        b1b = sbs(es, "b1b", [128, D], F32)
        catb = [sbs(es, f"catb{i}", [128, KC, 512], BF16) for i in range(2)]
        xin = [sbs(es, f"x2in{i}", [128, D], F32) for i in range(2)]
        rb = sbs(es, "rb", [128, D], F32)
        x1b = [sbs(es, f"x1b{i}", [128, D], F32) for i in range(2)]
        x1Tt = [sbs(es, f"x1Tt{i}", [128, KC, 128], BF16) for i in range(2)]
        stats = sbs(es, "stats", [128, 4, 6], F32)
        mv = sbs(es, "mv", [128, 2], F32)
        rstd = sbs(es, "rstd", [128, 1], F32)
        defer_on()
        for kc in range(KC):
            DMA("gpsimd", wout, wout.t[:, kc, :], None, A["w_out"][kc * 128:(kc + 1) * 128, :], acc=True)
        defer_off()
        DMA("sync", g1b, g1b.t[:, :], None, A["ln1g"].partition_broadcast(128))
        DMA("sync", b1b, b1b.t[:, :], None, A["ln1b"].partition_broadcast(128))
        def p2_mm(i):
            tb, tj = i // 4, i % 4
            cb = catb[tb % 2]
            if tj == 0:
                DMA("sync", cb, cb.t[:, :, :], d_cat, catT_r[:, :, tb * 512:(tb + 1) * 512])
            xb = xin[i % 2]
            DMA("sync", xb, xb.t[:, :], d_xin, A["x_win"][HALO + i * 128:HALO + (i + 1) * 128, :])
            for nb in range(4):
                pb = ps[nb]
                for kc in range(KC):
                    MM(pb, pb.t[:, :], cb, cb.t[:, kc, tj * 128:(tj + 1) * 128], wout, wout.t[:, kc, nb * 512:(nb + 1) * 512],
                       kc == 0, kc == KC - 1)

        def p2_evac(i):
            xb = xin[i % 2]
            for nb in range(4):
                pb = ps[nb]
                STT(rb, rb.t[:, nb * 512:(nb + 1) * 512], xb, xb.t[:, nb * 512:(nb + 1) * 512], ALPHA, pb, pb.t[:, :],
                    ALU.mult, ALU.add)

        def p2_ln(i):
            x1 = x1b[i % 2]
            _layernorm(P, rb, x1, g1b, b1b, stats, mv, rstd)
            DMA("sync", d_x1, A["x1"][i * 128:(i + 1) * 128, :], x1, x1.t[:, :], acc=True)

        def p2_tr(i):
            x1 = x1b[i % 2]
            xt = x1Tt[i % 2]
            for g in range(4):
                pb = ps[4 + g]
                for j in range(4):
                    kc = g * 4 + j
                    TR(pb, pb.t[:, j * 128:(j + 1) * 128], x1, x1.t[:, kc * 128:(kc + 1) * 128])
                o_ap = xt.t[:, g * 4:(g + 1) * 4, :]
                i_ap = pb.t[:, :].rearrange("p (a b) -> p a b", a=4)
                ACT(xt, o_ap, pb, i_ap, AF.Identity)
            DMA("sync", d_x1T, x1T_r[:, :, i * 128:(i + 1) * 128], xt, xt.t[:, :, :], acc=True)

        p2_mm(0)
        p2_evac(0)
        for i in range(NT):
            if i + 1 < NT:
                p2_mm(i + 1)
            p2_ln(i)
            p2_tr(i)
            if i + 1 < NT:
                p2_evac(i + 1)
        P.barrier()

    P.use_semset(5)
    if True:
        with ExitStack() as es:
            topSb = [sbs(es, f"topSb{i}", [128, 4, 16, 16], F32) for i in range(2)]
            topIb = [sbs(es, f"topIb{i}", [128, 4, 16, 16], U32) for i in range(2)]
            wq = sbs(es, "wq", [128, KC, D], BF16)
            skf = sbs(es, "skf", [128, 2, 128], F32)
            skT = sbs(es, "skT", [128, 2, 128], BF16)
            x1Tb = [sbs(es, f"x1Tb{i}", [128, KC, 512], BF16) for i in range(2)]
            qTc = [sbs(es, f"qTc{i}", [128, 512], BF16) for i in range(2)]
            scb = [sbs(es, f"scb{i}", [128, 4, 128], F32) for i in range(2)]
            scr = sbs(es, "scr", [128, 128], F32)
            defer_on()
            for kc in range(KC):
                DMA("gpsimd", wq, wq.t[:, kc, :], None, A["w_q"][kc * 128:(kc + 1) * 128, :], acc=True)
            defer_off()
            DMA("sync", skf, skf.t[:, :, :], None, A["sub_keys"].rearrange("p k c -> k p c"))
            for p_ in range(2):
                pb = nps()
                TR(pb, pb.t[:, 0:128], skf, skf.t[:, p_, :])
                ACT(skT, skT.t[:, p_, :], pb, pb.t[:, 0:128], AF.Identity)
            cnt = 0
            for tb in range(NB):
                xb = x1Tb[tb % 2]
                DMA("sync", xb, xb.t[:, :, :], d_x1T, x1T_r[:, :, tb * 512:(tb + 1) * 512])
                topS = topSb[tb % 2]
                topI = topIb[tb % 2]
                for hp in range(16):
                    pb = ps[cnt % 4]
                    for kc in range(KC):
                        MM(pb, pb.t[:, :], wq, wq.t[:, kc, hp * 128:(hp + 1) * 128], xb, xb.t[:, kc, :], kc == 0, kc == KC - 1)
                    qc = qTc[cnt % 2]
                    ACT(qc, qc.t[:, :], pb, pb.t[:, :], AF.Identity)
                    pb2 = ps[4 + cnt % 4]
                    for tj in range(4):
                        MM(pb2, pb2.t[:, tj * 128:(tj + 1) * 128], qc, qc.t[:, tj * 128:(tj + 1) * 128], skT, skT.t[:, hp % 2, :], True, True)
                    sc = scb[cnt % 2]
                    ACT(sc, sc.t[:, :, :], pb2, pb2.t[:, :].rearrange("p (a b) -> p a b", a=4), AF.Identity)
                    for tj in range(4):
                        v_ = sc.t[:, tj, :]
                        sa = topS.t[:, tj, hp, 0:8]
                        sb_ = topS.t[:, tj, hp, 8:16]
                        P.op("vector", lambda e, v_=v_, sa=sa: e.max(out=sa, in_=v_), reads=[sc], writes=[topS])
                        P.op("vector", lambda e, v_=v_, sa=sa: e.match_replace(out=scr.t[:, :], in_to_replace=sa, in_values=v_, imm_value=-1e30),
                             reads=[sc, topS], writes=[scr])
                        P.op("vector", lambda e, sb_=sb_: e.max(out=sb_, in_=scr.t[:, :]), reads=[scr], writes=[topS])
                        ia = topI.t[:, tj, hp, 0:8]
                        ib = topI.t[:, tj, hp, 8:16]
                        P.op("vector", lambda e, v_=v_, sa=sa, ia=ia: e.max_index(out=ia, in_max=sa, in_values=v_), reads=[sc, topS], writes=[topI])
                        P.op("vector", lambda e, v_=v_, sb_=sb_, ib=ib: e.max_index(out=ib, in_max=sb_, in_values=v_), reads=[sc, topS], writes=[topI])
                    cnt += 1
                DMA("sync", d_top, A["topS"][tb * 512:(tb + 1) * 512, :].rearrange("(a p) n -> p a n", p=128), topS,
                    topS.t[:, :, :, :].rearrange("p a h r -> p a (h r)"), acc=True)
                DMA("sync", d_top, A["topI"][tb * 512:(tb + 1) * 512, :].rearrange("(a p) n -> p a n", p=128), topI,
                    topI.t[:, :, :, :].rearrange("p a h r -> p a (h r)"), acc=True)
            P.barrier()
        P.use_semset(5)
        with ExitStack() as es:
            wpg = sbs(es, "wpg", [128, KC, D], BF16)
            wple = sbs(es, "wple", [128, 2, D], BF16)
            x1Tt = [sbs(es, f"x1Tu{i}", [128, KC, 128], BF16) for i in range(2)]
            pin = [sbs(es, f"pin{i}", [128, 256], F32) for i in range(2)]
            pT = [sbs(es, f"pT{i}", [128, 2, 128], BF16) for i in range(2)]
            x1t = [sbs(es, f"x1t{i}", [128, D], F32) for i in range(2)]
            sgb = [sbs(es, f"sgb{i}", [128, 512], F32) for i in range(2)]
            r2b = [sbs(es, f"r2b{i}", [128, D], F32) for i in range(2)]
            defer_on()
            for kc in range(KC):
                DMA("gpsimd", wpg, wpg.t[:, kc, :], None, A["w_pg"][kc * 128:(kc + 1) * 128, :], acc=True)
            for k2 in range(2):
                DMA("gpsimd", wple, wple.t[:, k2, :], None, A["w_ple"][k2 * 128:(k2 + 1) * 128, :], acc=True)
            defer_off()
            for i in range(NT):
                xt = x1Tt[i % 2]
                DMA("sync", xt, xt.t[:, :, :], d_x1T, x1T_r[:, :, i * 128:(i + 1) * 128])
                pi_ = pin[i % 2]
                DMA("sync", pi_, pi_.t[:, :], None, A["p_own"][i * 128:(i + 1) * 128, :])
                x1 = x1t[i % 2]
                DMA("sync", x1, x1.t[:, :], d_x1, A["x1"][i * 128:(i + 1) * 128, :])
                pb = ps[6 + i % 2]
                for k2 in range(2):
                    TR(pb, pb.t[:, k2 * 128:(k2 + 1) * 128], pi_, pi_.t[:, k2 * 128:(k2 + 1) * 128])
                pt = pT[i % 2]
                ACT(pt, pt.t[:, :, :], pb, pb.t[:, 0:256].rearrange("p (a b) -> p a b", a=2), AF.Identity)
                r2 = r2b[i % 2]
                for nb in range(4):
                    sl = slice(nb * 512, (nb + 1) * 512)
                    pg = ps[nb % 2]
                    for kc in range(KC):
                        MM(pg, pg.t[:, :], xt, xt.t[:, kc, :], wpg, wpg.t[:, kc, sl], kc == 0, kc == KC - 1)
                    sg = sgb[nb % 2]
                    ACT(sg, sg.t[:, :], pg, pg.t[:, :], AF.Sigmoid)
                    pp = ps[2 + nb % 2]
                    for k2 in range(2):
                        MM(pp, pp.t[:, :], pt, pt.t[:, k2, :], wple, wple.t[:, k2, sl], k2 == 0, k2 == 1)
                    TT(sg, sg.t[:, :], pp, pp.t[:, :], sg, sg.t[:, :], ALU.mult)
                    STT(r2, r2.t[:, sl], x1, x1.t[:, sl], ALPHA, sg, sg.t[:, :], ALU.mult, ALU.add)
                DMA("sync", d_r2, A["r2"][i * 128:(i + 1) * 128, :], r2, r2.t[:, :], acc=True)
            P.barrier()

        conv_step(10 ** 6)
        with ExitStack() as es:
            topSt = [sbs(es, f"topSt{i}", [128, 16, 16], F32) for i in range(2)]
            topIt = [sbs(es, f"topIt{i}", [128, 16, 16], U32) for i in range(2)]
            g2b = sbs(es, "g2b", [128, D], F32)
            b2b = sbs(es, "b2b", [128, D], F32)
            uvb = [sbs(es, f"uvb{i}", [128, 2 * D], BF16) for i in range(NUV)]
            dg = [sbs(es, f"dg{i}", [128, 128], BF16) for i in range(4)]
            zc = [sbs(es, f"zc{i}", [128, 1], F32) for i in range(4)]
            gelc = [sbs(es, f"gelc{i}", [128, 1], F32) for i in range(4)]
            x1t = [sbs(es, f"x4t{i}", [128, D], F32) for i in range(2)]
            yac = [sbs(es, f"yac{i}", [128, D], F32) for i in range(3)]
            outb = [sbs(es, f"outb{i}", [128, D], F32) for i in range(2)]
            junk = sbs(es, "junk", [128, D], BF16)
            topIf = sbs(es, "topIf", [128, 16, 16], F32)
            cs2 = [sbs(es, f"cs2_{i}", [128, 256], F32) for i in range(2)]
            tS = sbs(es, "tS", [128, 8, 16], F32)
            scr2 = sbs(es, "scr2", [128, 256], F32)
            posu = sbs(es, "posu", [128, 8, 16], U32)
            pa_u = sbs(es, "pa_u", [128, 8, 16], U32)
            pb_u = sbs(es, "pb_u", [128, 8, 16], U32)
            pa_f = sbs(es, "pa_f", [128, 8, 16], F32)
            pb_f = sbs(es, "pb_f", [128, 8, 16], F32)
            oh = sbs(es, "oh", [128, 128, 16], F32)
            iota_rep = sbs(es, "iota_rep", [128, 128, 16], F32)
            sel1 = sbs(es, "sel1", [128, 128], F32)
            sel2 = sbs(es, "sel2", [128, 128], F32)
            acol = [sbs(es, f"acol{i}", [128, 1], F32) for i in range(4)]
            cint = sbs(es, "cint", [128, 2], U32)
            eidf = sbs(es, "eidf", [128, 128], F32)
            eidx = [sbs(es, f"eidx{i}", [128, 128], U32) for i in range(2)]
            nmax = sbs(es, "nmax", [128, 8], F32)
            ge = sbs(es, "ge", [128, 8, 16], F32)
            gsum = sbs(es, "gsum", [128, 8], F32)
            gates = [sbs(es, f"gate{i}", [128, 128], F32) for i in range(2)]
            stats = sbs(es, "stats4", [128, 4, 6], F32)
            mv = sbs(es, "mv4", [128, 2], F32)
            rstd = sbs(es, "rstd4", [128, 1], F32)
            DMA("sync", g2b, g2b.t[:, :], None, A["ln2g"].partition_broadcast(128))
            DMA("sync", b2b, b2b.t[:, :], None, A["ln2b"].partition_broadcast(128))
            P.op("gpsimd", lambda e: e.iota(iota_rep.t[:, :, :], pattern=[[0, 128], [1, 16]], base=0, channel_multiplier=0,
                                            allow_small_or_imprecise_dtypes=True), writes=[iota_rep])
            P.op("vector", lambda e: e.memset(cint.t[:, 0:1], 4), writes=[cint])
            P.op("vector", lambda e: e.memset(cint.t[:, 1:2], 15), writes=[cint])
            gcnt = [0]

            def idx_gen(i):
                tSt = topSt[i % 2]
                tIt = topIt[i % 2]
                ex = eidx[i % 2]
                gate = gates[i % 2]
                DMA("sync", tSt, tSt.t[:, :, :].rearrange("p h r -> p (h r)"), d_top, A["topS"][i * 128:(i + 1) * 128, :])
                DMA("sync", tIt, tIt.t[:, :, :].rearrange("p h r -> p (h r)"), d_top, A["topI"][i * 128:(i + 1) * 128, :])
                yield
                P.op("vector", lambda e: e.tensor_copy(out=topIf.t[:, :, :], in_=tIt.t[:, :, :]), reads=[tIt], writes=[topIf])
                yield
                for h in range(8):
                    s1 = tSt.t[:, 2 * h, :].unsqueeze(2).to_broadcast([128, 16, 16])
                    s2 = tSt.t[:, 2 * h + 1, :].unsqueeze(1).to_broadcast([128, 16, 16])
                    csb = cs2[h % 2]
                    csh = csb.t[:, :].rearrange("p (a b) -> p a b", a=16)
                    TT(csb, csh, tSt, s1, tSt, s2, ALU.add)
                    yield
                    P.op("vector", lambda e, h=h, csb=csb: e.max(out=tS.t[:, h, 0:8], in_=csb.t[:, :]), reads=[csb], writes=[tS])
                    yield
                    P.op("vector", lambda e, h=h, csb=csb: e.match_replace(out=scr2.t[:, :], in_to_replace=tS.t[:, h, 0:8], in_values=csb.t[:, :], imm_value=-1e30),
                         reads=[csb, tS], writes=[scr2])
                    yield
                    P.op("vector", lambda e, h=h: e.max(out=tS.t[:, h, 8:16], in_=scr2.t[:, :]), reads=[scr2], writes=[tS])
                    yield
                    P.op("vector", lambda e, h=h, csb=csb: e.max_index(out=posu.t[:, h, 0:8], in_max=tS.t[:, h, 0:8], in_values=csb.t[:, :]),
                         reads=[csb, tS], writes=[posu])
                    yield
                    P.op("vector", lambda e, h=h, csb=csb: e.max_index(out=posu.t[:, h, 8:16], in_max=tS.t[:, h, 8:16], in_values=csb.t[:, :]),
                         reads=[csb, tS], writes=[posu])
                    yield
                P.op("vector", lambda e: e.tensor_scalar(out=pa_u.t[:, :, :], in0=posu.t[:, :, :], scalar1=cint.t[:, 0:1], scalar2=None,
                                                         op0=ALU.logical_shift_right), reads=[posu, cint], writes=[pa_u])
                yield
                P.op("vector", lambda e: e.tensor_scalar(out=pb_u.t[:, :, :], in0=posu.t[:, :, :], scalar1=cint.t[:, 1:2], scalar2=None,
                                                         op0=ALU.bitwise_and), reads=[posu, cint], writes=[pb_u])
                yield
                P.op("vector", lambda e: e.tensor_copy(out=pa_f.t[:, :, :], in_=pa_u.t[:, :, :]), reads=[pa_u], writes=[pa_f])
                yield
                P.op("vector", lambda e: e.tensor_copy(out=pb_f.t[:, :, :], in_=pb_u.t[:, :, :]), reads=[pb_u], writes=[pb_f])
                yield
                tI4 = topIf.t[:, :, :].rearrange("p (h two) r -> p h two r", two=2)
                for (pf, which, dst) in ((pa_f, 0, sel1), (pb_f, 1, sel2)):
                    TT(oh, oh.t[:, :, :], pf, pf.t[:, :, :].rearrange("p h r -> p (h r)").unsqueeze(2).to_broadcast([128, 128, 16]),
                       iota_rep, iota_rep.t[:, :, :], ALU.is_equal)
                    yield
                    TT(oh, oh.t[:, :, :].rearrange("p (h r) a -> p h r a", h=8), oh, oh.t[:, :, :].rearrange("p (h r) a -> p h r a", h=8),
                       topIf, tI4[:, :, which, :].unsqueeze(2).to_broadcast([128, 8, 16, 16]), ALU.mult)
                    yield
                    P.op("vector", lambda e, dst=dst: e.tensor_reduce(out=dst.t[:, :], in_=oh.t[:, :, :], axis=mybir.AxisListType.X, op=ALU.add),
                         reads=[oh], writes=[dst])
                    yield
                STT(eidf, eidf.t[:, :], sel1, sel1.t[:, :], 128.0, sel2, sel2.t[:, :], ALU.mult, ALU.add)
                yield
                TS(eidf, eidf.t[:, :], eidf, eidf.t[:, :], 16383.0, ALU.min, 0.0, ALU.max)
                yield
                if (i + 1) * 128 <= NINV - HALO:
                    TS(eidf, eidf.t[:, :], eidf, eidf.t[:, :], flags.t[:, 0:1], ALU.mult, reads=[flags])
                    yield
                P.op("vector", lambda e: e.tensor_copy(out=ex.t[:, :], in_=eidf.t[:, :]), reads=[eidf], writes=[ex])
                yield
                TS(nmax, nmax.t[:, :], tS, tS.t[:, :, 0], -1.0, ALU.mult)
                yield
                for h in range(8):
                    ACT(ge, ge.t[:, h, :], tS, tS.t[:, h, :], AF.Exp, bias=nmax.t[:, h:h + 1], scale=1.0, reads=[nmax])
                yield
                P.op("vector", lambda e: e.tensor_reduce(out=gsum.t[:, :], in_=ge.t[:, :, :], axis=mybir.AxisListType.X, op=ALU.add),
                     reads=[ge], writes=[gsum])
                yield
                P.op("vector", lambda e: e.reciprocal(out=gsum.t[:, :], in_=gsum.t[:, :]), reads=[gsum], writes=[gsum])
                yield
                TT(gate, gate.t[:, :].rearrange("p (h r) -> p h r", h=8), ge, ge.t[:, :, :],
                   gsum, gsum.t[:, :].unsqueeze(2).to_broadcast([128, 8, 16]), ALU.mult)
                yield

            def finish_gen(k):
                ya = yac[k % 3]
                psb = ps[(k % 2) * 4:(k % 2) * 4 + 4]
                for nb in range(4):
                    sl = slice(nb * 512, (nb + 1) * 512)
                    TT(ya, ya.t[:, sl], psb[nb], psb[nb].t[:, :], ya, ya.t[:, sl], ALU.add)
                    yield
                ob = outb[k % 2]
                for k4 in range(4):
                    P.op("vector", lambda e, k4=k4: e.bn_stats(out=stats.t[:, k4, :], in_=ya.t[:, k4 * 512:(k4 + 1) * 512]),
                         reads=[ya], writes=[stats])
                    yield
                P.op("vector", lambda e: e.bn_aggr(out=mv.t[:, :], in_=stats.t[:, :, :].rearrange("p a b -> p (a b)")), reads=[stats], writes=[mv])
                yield
                P.op("vector", lambda e: e.tensor_scalar(out=rstd.t[:, :], in0=mv.t[:, 1:2], scalar1=LN_EPS, scalar2=None, op0=ALU.add),
                     reads=[mv], writes=[rstd])
                P.op("scalar", lambda e: e.activation(out=rstd.t[:, :], in_=rstd.t[:, :], func=AF.Sqrt), reads=[rstd], writes=[rstd])
                yield
                P.op("vector", lambda e: e.reciprocal(out=rstd.t[:, :], in_=rstd.t[:, :]), reads=[rstd], writes=[rstd])
                yield
                P.op("vector", lambda e: e.tensor_scalar(out=ob.t[:, :], in0=ya.t[:, :], scalar1=mv.t[:, 0:1], scalar2=rstd.t[:, 0:1],
                                                         op0=ALU.subtract, op1=ALU.mult), reads=[ya, mv, rstd], writes=[ob])
                yield
                P.op("vector", lambda e: e.tensor_tensor(out=ob.t[:, :], in0=ob.t[:, :], in1=g2b.t[:, :], op=ALU.mult), reads=[ob, g2b], writes=[ob])
                yield
                P.op("vector", lambda e: e.tensor_tensor(out=ob.t[:, :], in0=ob.t[:, :], in1=b2b.t[:, :], op=ALU.add), reads=[ob, b2b], writes=[ob])
                DMA("sync", d_y, A["y"][k * 128:(k + 1) * 128, :], ob, ob.t[:, :], acc=True)
                yield

            def tile_loads(i):
                x1 = x1t[i % 2]
                ya = yac[i % 3]
                DMA("sync", x1, x1.t[:, :], d_x1, A["x1"][i * 128:(i + 1) * 128, :])
                DMA("sync", ya, ya.t[:, :], d_r2, A["r2"][i * 128:(i + 1) * 128, :])

            def step(g, n):
                if g is None:
                    return None
                for _ in range(n):
                    try:
                        next(g)
                    except StopIteration:
                        return None
                return g

            def drain(g):
                while g is not None:
                    g = step(g, 1)

            P.use_semset(6)
            drain(idx_gen(0))
            tile_loads(0)
            for i in range(NT):
                P.use_semset(6 + i % 2)
                x1 = x1t[i % 2]
                ex = eidx[i % 2]
                gate = gates[i % 2]
                psb = ps[(i % 2) * 4:(i % 2) * 4 + 4]
                gi = idx_gen(i + 1) if i + 1 < NT else None
                gf = finish_gen(i - 1) if i >= 1 else None
                if i + 1 < NT:
                    tile_loads(i + 1)
                bufs = {}

                def consume(j):
                    buf = bufs.pop(j)
                    dj = dg[j % 4]
                    ac = acol[j % 4]
                    ACT(ac, ac.t[:, 0:1], gelc[j % 4], gelc[j % 4].t[:, 0:1], AF.Identity, scale=gate.t[:, j:j + 1], reads=[gate])
                    ACT(dj, dj.t[:, :], ident_bf, ident_bf.t[:, :], AF.Identity, scale=ac.t[:, 0:1], reads=[ac])
                    for nb in range(4):
                        MM(psb[nb], psb[nb].t[:, :], dj, dj.t[:, :], buf, buf.t[:, D + nb * 512:D + (nb + 1) * 512], j == 0, j == 127)

                for j in range(128):
                    buf = uvb[gcnt[0] % NUV]
                    gcnt[0] += 1
                    bufs[j] = buf
                    P.dma("gpsimd", buf, d_tab, lambda e, buf=buf, ex=ex, j=j: e.indirect_dma_start(
                        out=buf.t[:, :], out_offset=None, in_=A["uv_bf"],
                        in_offset=bass.IndirectOffsetOnAxis(ap=ex.t[:, j:j + 1], axis=0)), reads=[ex])
                    zj = zc[j % 4]
                    STT(junk, junk.t[:, :], buf, buf.t[:, 0:D], 1.0, x1, x1.t[:, :], ALU.mult, ALU.mult,
                        accum=zj.t[:, 0:1], accb=zj)
                    ACT(gelc[j % 4], gelc[j % 4].t[:, 0:1], zj, zj.t[:, 0:1], AF.Gelu)
                    if j >= 2:
                        consume(j - 2)
                    gi = step(gi, 2)
                    gf = step(gf, 1)
                consume(126)
                consume(127)
                drain(gf)
                drain(gi)
            drain(finish_gen(NT - 1))
            P.barrier()


def _layernorm(P, rb, ob, gb, bb, stats, mv, rstd):
    for k in range(4):
        P.op("vector", lambda e, k=k: e.bn_stats(out=stats.t[:, k, :], in_=rb.t[:, k * 512:(k + 1) * 512]),
             reads=[rb], writes=[stats])
    P.op("vector", lambda e: e.bn_aggr(out=mv.t[:, :], in_=stats.t[:, :, :].rearrange("p a b -> p (a b)")), reads=[stats], writes=[mv])
    P.op("vector", lambda e: e.tensor_scalar(out=rstd.t[:, :], in0=mv.t[:, 1:2], scalar1=LN_EPS, scalar2=None, op0=ALU.add),
         reads=[mv], writes=[rstd])
    P.op("scalar", lambda e: e.activation(out=rstd.t[:, :], in_=rstd.t[:, :], func=AF.Sqrt), reads=[rstd], writes=[rstd])
    P.op("vector", lambda e: e.reciprocal(out=rstd.t[:, :], in_=rstd.t[:, :]), reads=[rstd], writes=[rstd])
    P.op("vector", lambda e: e.tensor_scalar(out=ob.t[:, :], in0=rb.t[:, :], scalar1=mv.t[:, 0:1], scalar2=rstd.t[:, 0:1],
                                             op0=ALU.subtract, op1=ALU.mult), reads=[rb, mv, rstd], writes=[ob])
    P.op("vector", lambda e: e.tensor_tensor(out=ob.t[:, :], in0=ob.t[:, :], in1=gb.t[:, :], op=ALU.mult), reads=[ob, gb], writes=[ob])
    P.op("vector", lambda e: e.tensor_tensor(out=ob.t[:, :], in0=ob.t[:, :], in1=bb.t[:, :], op=ALU.add), reads=[ob, bb], writes=[ob])


_W_SPECS = [
    ("w_in", [D, IN_WIDTH]), ("b_in_fm", [128, 36]), ("bv_row", [1, 1024]), ("w_pool", [4, 128, 128]),
    ("psc_fm", [128, 4]), ("dww_fm", [128, 4, 31]), ("dwb_fm", [128, 4]), ("cng_fm", [128, 4]), ("cnb_fm", [128, 4]),
    ("abias", [128, 8 * 5 * 128]), ("w_out", [D, D]), ("ln1g", [1, D]), ("ln1b", [1, D]), ("w_q", [D, D]),
    ("sub_keys", [2, 128, 128]), ("u_tab", [16384, D]), ("v_tab", [16384, D]), ("w_ple", [256, D]), ("w_pg", [D, D]),
    ("ln2g", [1, D]), ("ln2b", [1, D]),
]


DEPTH = 4
NW_FINAL = 2048


def _layer_dims(l):
    nw = NW_FINAL + HALO * (DEPTH - 1 - l)
    ninv = HALO * (DEPTH - l)
    return nw, ninv


def build_program():
    nc = bass.Bass("TRN2", target_bir_lowering=False)
    NWMAX = _layer_dims(0)[0]
    x0 = nc.dram_tensor("x_win0", [NWMAX + HALO, D], F32, kind="ExternalInput").ap()
    flags_d = nc.dram_tensor("flags", [1, 2], F32, kind="ExternalInput").ap()
    invc_d = nc.dram_tensor("invc", [1, 64], F32, kind="ExternalInput").ap()
    ident_d = nc.dram_tensor("ident", [128, 128], F32, kind="ExternalInput").ap()
    y_out = nc.dram_tensor("y", [NW_FINAL, D], F32, kind="ExternalOutput").ap()
    scr = {
        "catT": nc.dram_tensor("catT", [D, NWMAX], BF16, kind="Internal").ap(),
        "x1T": nc.dram_tensor("x1T", [D, NWMAX], BF16, kind="Internal").ap(),
        "x1": nc.dram_tensor("x1", [NWMAX, D], F32, kind="Internal").ap(),
        "r2": nc.dram_tensor("r2", [NWMAX, D], F32, kind="Internal").ap(),
        "topS": nc.dram_tensor("topS", [NWMAX, 256], F32, kind="Internal").ap(),
        "topI": nc.dram_tensor("topI", [NWMAX, 256], U32, kind="Internal").ap(),
    }
    ybuf = [nc.dram_tensor(f"ybuf{i}", [NWMAX, D], F32, kind="Internal").ap() for i in range(2)]
    P = Prog(nc)
    ps = [P.ps(f"ps{i}", [128, 512], F32) for i in range(8)]
    ident = P.sb("ident", [128, 128], F32)
    ones_bf = P.sb("ones_bf", [128, 128], BF16)
    ones_f = P.sb("ones_f", [128, 128], F32)
    flags = P.sb("flags_sb", [128, 2], F32)
    invc = P.sb("invc_sb", [128, 64], F32)
    P.dma("sync", ident, None, lambda e: e.dma_start(out=ident.t[:, :], in_=ident_d))
    P.dma("sync", flags, None, lambda e: e.dma_start(out=flags.t[:, :], in_=flags_d.partition_broadcast(128)))
    P.dma("sync", invc, None, lambda e: e.dma_start(out=invc.t[:, :], in_=invc_d.partition_broadcast(128)))
    P.op("vector", lambda e: e.memset(ones_bf.t[:, :], 1.0), writes=[ones_bf])
    P.op("vector", lambda e: e.memset(ones_f.t[:, :], 1.0 / 512.0), writes=[ones_f])
    ident_bf = P.sb("ident_bf", [128, 128], BF16)
    P.op("vector", lambda e: e.tensor_copy(out=ident_bf.t[:, :], in_=ident.t[:, :]), reads=[ident], writes=[ident_bf])
    consts = (ident, ones_bf, ones_f, flags, invc, ident_bf)
    d_cat, d_x1, d_x1T, d_r2, d_top, d_tab = (P.dram(n) for n in ("d_cat", "d_x1", "d_x1T", "d_r2", "d_top", "d_tab"))
    d_ys = [P.dram("d_y0"), P.dram("d_y1")]
    d_in = P.dram("d_in")
    d_out = P.dram("d_out")
    As = []
    for l in range(DEPTH):
        nw, ninv = _layer_dims(l)
        A = dict(scr)
        for name, shape in _W_SPECS:
            A[name] = nc.dram_tensor(f"{name}_{l}", shape, F32, kind="ExternalInput").ap()
        A["p_own"] = nc.dram_tensor(f"p_own_{l}", [nw, 256], F32, kind="ExternalInput").ap()
        A["uv_bf"] = nc.dram_tensor(f"uv_bf_{l}", [16384, 2 * D], BF16, kind="Internal").ap()
        As.append(A)
    cv = [P.sb(f"cv{i}", [128, 2, D], BF16) for i in range(2)]

    def conv_gen(l):
        jobs = []
        for nm, off in (("u_tab", 0), ("v_tab", D)):
            src = As[l][nm].rearrange("(p r) d -> p r d", r=128)
            dst = As[l]["uv_bf"].rearrange("(p r) d -> p r d", r=128)[:, :, off:off + D]
            for c in range(64):
                jobs.append((src[:, c * 2:(c + 1) * 2, :], dst[:, c * 2:(c + 1) * 2, :]))

        def load(k):
            b_, (sa, _) = cv[k % 2], jobs[k]
            P.dma("gpsimd", b_, None, lambda e: e.dma_start(out=b_.t[:, :, :], in_=sa))

        def store(k):
            b_, (_, da) = cv[k % 2], jobs[k]
            P.dma("gpsimd", d_tab, b_, lambda e: e.dma_start(out=da, in_=b_.t[:, :, :]), accumulate=True)

        load(0)
        yield
        for k in range(len(jobs)):
            if k + 1 < len(jobs):
                load(k + 1)
                yield
            store(k)
            yield

    x_cur, d_cur = x0, d_in
    for l in range(DEPTH):
        nw, ninv = _layer_dims(l)
        A = As[l]
        A["x_win"] = x_cur
        if l == DEPTH - 1:
            A["y"], d_y = y_out, d_out
        else:
            A["y"], d_y = ybuf[l % 2], d_ys[l % 2]
        _emit_layer(nc, P, ps, A, nw, ninv, consts, (d_cat, d_x1, d_x1T, d_r2, d_top, d_cur, d_y, d_tab), 1 + l, conv_gen(l))
        x_cur, d_cur = A["y"], d_y
    P.finish([d_out])
    return nc


def _attn_bias_layout(rel_bias_l):
    q = np.arange(128)[None, :]
    j = np.arange(640)[:, None]
    dist = q - j + 512
    idx = np.clip(dist, -63, 128) + 63
    g = rel_bias_l[:, idx]
    masked = ((q < 64) & (j >= 576)) | ((q >= 64) & (j < 64))
    g = np.where(masked[None], np.float32(NEG), g).astype(np.float32)
    g = g.reshape(8, 5, 128, 128).transpose(2, 0, 1, 3)
    return np.ascontiguousarray(g.reshape(128, 8 * 5 * 128))


def _fm(v, nchunk):
    return np.ascontiguousarray(v.reshape(nchunk, 128).T)


def layer_weights(inp, l):
    w = {}
    w["w_in"] = np.ascontiguousarray(inp["w_in"][l])
    w["b_in_fm"] = _fm(inp["b_in"][l], 36)
    w["bv_row"] = np.ascontiguousarray(inp["b_in"][l][3584:4608].reshape(1, 1024))
    w["w_pool"] = np.ascontiguousarray(inp["w_pool"][l])
    w["psc_fm"] = _fm(inp["pool_scale"][l], 4)
    w["dww_fm"] = np.ascontiguousarray(inp["dw_w"][l].reshape(31, 4, 128).transpose(2, 1, 0))
    w["dwb_fm"] = _fm(inp["dw_b"][l], 4)
    w["cng_fm"] = _fm(inp["cn_g"][l], 4)
    w["cnb_fm"] = _fm(inp["cn_b"][l], 4)
    w["abias"] = _attn_bias_layout(inp["rel_bias"][l])
    w["w_out"] = np.ascontiguousarray(inp["w_out"][l])
    w["ln1g"] = np.ascontiguousarray(inp["ln1_g"][l].reshape(1, D))
    w["ln1b"] = np.ascontiguousarray(inp["ln1_b"][l].reshape(1, D))
    w["w_q"] = np.ascontiguousarray(inp["w_q"][l])
    w["sub_keys"] = np.ascontiguousarray(inp["sub_keys"][l])
    w["u_tab"] = np.ascontiguousarray(inp["u_tab"][l])
    w["v_tab"] = np.ascontiguousarray(inp["v_tab"][l])
    w["w_ple"] = np.ascontiguousarray(inp["w_ple"][l])
    w["w_pg"] = np.ascontiguousarray(inp["w_pg"][l])
    w["ln2g"] = np.ascontiguousarray(inp["ln2_g"][l].reshape(1, D))
    w["ln2b"] = np.ascontiguousarray(inp["ln2_b"][l].reshape(1, D))
    return w


def core_consts(first_half):
    flags = np.array([[0.0, NEG]] if first_half else [[1.0, 0.0]], np.float32)
    invc = np.zeros((4, 16), np.float32)
    t = np.arange(16)
    for g, wv in enumerate(POOL_W):
        cnt = np.minimum(t + 1, wv) if first_half else np.full(16, wv)
        invc[g] = (1.0 / cnt).astype(np.float32)
    return flags, invc.reshape(1, 64)


_PROG_CACHE = {}


def kernel(**inputs):
    inp = {k: np.asarray(v) for k, v in inputs.items()}
    x = np.ascontiguousarray(inp["x"], dtype=np.float32)
    B, S, _ = x.shape
    assert S == 2 * NW_FINAL and inp["w_in"].shape[0] == DEPTH
    if "nc" not in _PROG_CACHE:
        _PROG_CACHE["nc"] = build_program()
    nc = _PROG_CACHE["nc"]
    ident = np.eye(128, dtype=np.float32)
    wl = [layer_weights(inp, l) for l in range(DEPTH)]
    in_maps = []
    for c in range(2 * B):
        b, half = c // 2, c % 2
        end = (half + 1) * NW_FINAL
        flags, invc = core_consts(half == 0)
        m = {"flags": flags, "invc": invc, "ident": ident}

        def window(arr, n):
            lo = end - n
            if lo >= 0:
                return np.ascontiguousarray(arr[lo:end])
            out = np.zeros((n,) + arr.shape[1:], np.float32)
            out[-lo:] = arr[0:end]
            return out

        m["x_win0"] = window(x[b], _layer_dims(0)[0] + HALO)
        for l in range(DEPTH):
            m[f"p_own_{l}"] = window(inp["p"][l, b], _layer_dims(l)[0])
            for k, v in wl[l].items():
                m[f"{k}_{l}"] = v
        in_maps.append(m)
    res = run_bass_kernel_spmd(nc, in_maps, core_ids=list(range(2 * B)))
    out = np.empty((B, S, D), np.float32)
    for c in range(2 * B):
        b, half = c // 2, c % 2
        out[b, half * NW_FINAL:(half + 1) * NW_FINAL] = res.results[c]["y"]
    return out
```

```python
import numpy as np
import concourse.bass as bass
import concourse.mybir as mybir
from concourse.bass_utils import run_bass_kernel_spmd

F32 = mybir.dt.float32
BF16 = mybir.dt.bfloat16
U32 = mybir.dt.uint32
I32 = mybir.dt.int32
AF = mybir.ActivationFunctionType
ALU = mybir.AluOpType

ENGS = ("sync", "scalar", "vector", "gpsimd", "tensor")


class Buf:
    def __init__(self, name, t=None):
        self.name = name
        self.t = t
        self.writers = {}
        self.readers = {}
        self.dsem = None
        self.dval = 0


class Prog:
    def __init__(self, nc):
        self.nc = nc
        self.q = {e: [] for e in ENGS}
        self.sems = {}
        self.waited = {}
        self._dbufs = {}
        self._named = {}
        self._sets = {}
        self.cur = None
        self.use_semset(0)

    def use_semset(self, k):
        if self.cur is not None:
            self._sets[self.cur]["cnt"] = dict(self.ecnt)
        if k not in self._sets:
            keys = {e: ("e", e, k) for e in ENGS}
            self._sets[k] = {"keys": keys, "cnt": {e: 0 for e in ENGS}}
        self.cur = k
        self.ekey = dict(self._sets[k]["keys"])
        self.ecnt = dict(self._sets[k]["cnt"])

    def sb(self, name, shape, dtype):
        return Buf(name, self.nc.alloc_sbuf_tensor("s_" + name, list(shape), dtype))

    def ps(self, name, shape, dtype=F32):
        return Buf(name, self.nc.alloc_psum_tensor(name, list(shape), dtype))

    def dram(self, name):
        return Buf(name, None)

    def _dsem(self, buf):
        if buf.dsem is None:
            key = ("d", buf.name)
            if key in self._named:
                buf.dval = self._named[key].dval
            else:
                self.sems[key] = self.nc.alloc_semaphore("ds_" + buf.name)
            self._named[key] = buf
            self._dbufs[key] = buf
            buf.dsem = key
        return buf.dsem

    def _waits(self, eng, deps, skip_same_engine=False):
        for key, val in deps.items():
            if skip_same_engine and key[0] == "e" and key[1] == eng:
                continue
            if self.waited.get((eng, key), 0) >= val:
                continue
            self.waited[(eng, key)] = val
            sem = self.sems[key]
            self.q[eng].append(lambda e, sem=sem, val=val: e.wait_ge(sem, val))

    @staticmethod
    def _merge(dst, src):
        for k, v in src.items():
            if dst.get(k, 0) < v:
                dst[k] = v

    def op(self, eng, fn, reads=(), writes=()):
        deps = {}
        for b in reads:
            self._merge(deps, b.writers)
        for b in writes:
            self._merge(deps, b.writers)
            self._merge(deps, b.readers)
        self._waits(eng, deps, skip_same_engine=(eng == "tensor"))
        self.ecnt[eng] += 1
        val = self.ecnt[eng]
        key = self.ekey[eng]
        if key not in self.sems:
            self.sems[key] = self.nc.alloc_semaphore(f"es_{key[1]}_{key[2]}")
        sem = self.sems[key]
        self.q[eng].append(lambda e, fn=fn, sem=sem: fn(e).then_inc(sem, 1))
        for b in writes:
            b.writers = {key: val}
            b.readers = {}
        for b in reads:
            if b.readers.get(key, 0) < val:
                b.readers[key] = val

    def dma(self, eng, dst, src, fn, reads=(), accumulate=False):
        key = self._dsem(dst)
        deps = {}
        if src is not None:
            self._merge(deps, src.writers)
        for b in reads:
            self._merge(deps, b.writers)
        w = dict(dst.writers)
        if accumulate:
            w.pop(key, None)
        self._merge(deps, w)
        self._merge(deps, dst.readers)
        self._waits(eng, deps)
        dst.dval += 16
        val = dst.dval
        sem = self.sems[key]
        self.q[eng].append(lambda e, fn=fn, sem=sem: fn(e).then_inc(sem, 16))
        dst.writers = {key: val}
        dst.readers = {}
        for b in ([src] if src is not None else []) + list(reads):
            if b.readers.get(key, 0) < val:
                b.readers[key] = val

    def finish(self, out_bufs):
        deps = {}
        for b in out_bufs:
            self._merge(deps, b.writers)
        self._waits("sync", deps)
        nc = self.nc
        q = self.q
        with nc.Block() as block:
            @block.sync
            def _(e):
                for f in q["sync"]:
                    f(e)

            @block.scalar
            def _(e):
                for f in q["scalar"]:
                    f(e)

            @block.vector
            def _(e):
                for f in q["vector"]:
                    f(e)

            @block.gpsimd
            def _(e):
                for f in q["gpsimd"]:
                    f(e)

            @block.tensor
            def _(e):
                for f in q["tensor"]:
                    f(e)


    def barrier(self):
        self._sets[self.cur]["cnt"] = dict(self.ecnt)
        deps = {}
        for st in self._sets.values():
            for e in ENGS:
                if st["cnt"][e] > 0:
                    deps[st["keys"][e]] = st["cnt"][e]
        for key, b in self._dbufs.items():
            deps[key] = b.dval
        for e in ENGS:
            self._waits(e, deps)


D = 2048
HALO = 512
KC = 16
IN_WIDTH = 4608
ALPHA = float(8 ** 0.25)
LN_EPS = 1e-5
ATT_SCALE = float(128 ** -0.5)
NEG = -30000.0
POOL_W = (2, 4, 8, 16)
NUV = 8


_UNIQ = [0]


def _emit_layer(nc, P, ps, A, NW_L, NINV, consts, dbufs, semset0, conv):
    from contextlib import ExitStack
    HB = HALO // 512
    ident, ones_bf, ones_f, flags, invc, ident_bf = consts
    psrot = [0]

    def nps():
        psrot[0] = (psrot[0] + 1) % 8
        return ps[psrot[0]]

    def sbs(es, name, shape, dt):
        _UNIQ[0] += 1
        t = es.enter_context(nc.sbuf_tensor(f"s{_UNIQ[0]}_{name}", list(shape), dt))
        return Buf(name, t)

    def OP(eng, f, reads, writes):
        P.op(eng, f, reads=reads, writes=writes)

    def MM(pb, out_ap, lb, l_ap, rb, r_ap, start, stop):
        P.op("tensor", lambda e: e.matmul(out_ap, lhsT=l_ap, rhs=r_ap, start=start, stop=stop),
             reads=[lb, rb], writes=[pb])

    def TR(pb, out_ap, sb_, in_ap):
        P.op("tensor", lambda e: e.transpose(out=out_ap, in_=in_ap, identity=ident.t[:, :]),
             reads=[sb_, ident], writes=[pb])

    def ACT(ob, out_ap, ib, in_ap, func, bias=None, scale=None, reads=()):
        kw = {}
        if bias is not None:
            kw["bias"] = bias
        if scale is not None:
            kw["scale"] = scale
        P.op("scalar", lambda e: e.activation(out=out_ap, in_=in_ap, func=func, **kw),
             reads=[ib] + list(reads), writes=[ob])

    cstate = [conv]

    def conv_step(n):
        for _ in range(n):
            if cstate[0] is None:
                return
            try:
                next(cstate[0])
            except StopIteration:
                cstate[0] = None

    def DMA(q, dst, out_ap, src, in_ap, reads=(), acc=False):
        P.dma(q, dst, src, lambda e: e.dma_start(out=out_ap, in_=in_ap), reads=reads, accumulate=acc)
        if q == "gpsimd":
            if cdefer[0]:
                cpend[0] += 3
            else:
                conv_step(3)

    cdefer = [False]
    cpend = [0]

    def defer_on():
        cdefer[0] = True

    def defer_off():
        cdefer[0] = False
        conv_step(cpend[0])
        cpend[0] = 0

    def TT(ob, out_ap, ab, a_ap, bb, b_ap, op, eng="vector"):
        P.op(eng, lambda e: e.tensor_tensor(out=out_ap, in0=a_ap, in1=b_ap, op=op), reads=[ab, bb], writes=[ob])

    def TS(ob, out_ap, ib, in_ap, s1, op0, s2=None, op1=None, reads=(), eng="vector"):
        if op1 is None:
            P.op(eng, lambda e: e.tensor_scalar(out=out_ap, in0=in_ap, scalar1=s1, scalar2=None, op0=op0),
                 reads=[ib] + list(reads), writes=[ob])
        else:
            P.op(eng, lambda e: e.tensor_scalar(out=out_ap, in0=in_ap, scalar1=s1, scalar2=s2, op0=op0, op1=op1),
                 reads=[ib] + list(reads), writes=[ob])

    def STT(ob, out_ap, ab, a_ap, scalar, bb, b_ap, op0, op1, reads=(), accum=None, accb=None):
        kw = {}
        w = [ob]
        if accum is not None:
            kw["accum_out"] = accum
            w.append(accb)
        P.op("vector", lambda e: e.scalar_tensor_tensor(out=out_ap, in0=a_ap, scalar=scalar, in1=b_ap,
                                                        op0=op0, op1=op1, **kw),
             reads=[ab, bb] + list(reads), writes=w)

    d_cat, d_x1, d_x1T, d_r2, d_top, d_xin, d_y, d_tab = dbufs
    catT_r = A["catT"].rearrange("(kc p) t -> p kc t", p=128)
    x1T_r = A["x1T"].rearrange("(kc p) t -> p kc t", p=128)
    w_in_r = A["w_in"].rearrange("(kc p) n -> p kc n", p=128)

    def mixer_segment(o0, NW, ninv):
        NWIN = HALO + NW
        NT = NW // 128
        NTW = NWIN // 128
        NB = NW // 512
        NBW = NWIN // 512
        fpos = NINV - HALO - o0
        with ExitStack() as es1:
            xT = sbs(es1, "xT", [128, KC, NWIN], BF16)
            wch = [sbs(es1, f"wch{i}", [128, KC, 128], BF16) for i in range(2)]
            b_in_fm = sbs(es1, "b_in_fm", [128, 36], F32)
            yfm = [sbs(es1, f"yfm{i}", [128, NW], BF16) for i in range(2)]
            DMA("sync", b_in_fm, b_in_fm.t[:, :], None, A["b_in_fm"])
            with ExitStack() as es0:
                xin = [sbs(es0, f"xin{i}", [128, D], F32) for i in range(2)]
                for i in range(NTW):
                    xb = xin[i % 2]
                    DMA("sync", xb, xb.t[:, :], d_xin, A["x_win"][o0 + i * 128:o0 + (i + 1) * 128, :])
                    for g in range(4):
                        pb = nps()
                        for j in range(4):
                            kc = g * 4 + j
                            TR(pb, pb.t[:, j * 128:(j + 1) * 128], xb, xb.t[:, kc * 128:(kc + 1) * 128])
                        o_ap = xT.t[:, g * 4:(g + 1) * 4, i * 128:(i + 1) * 128]
                        i_ap = pb.t[:, :].rearrange("p (a b) -> p a b", a=4)
                        if g % 2 == 0:
                            ACT(xT, o_ap, pb, i_ap, AF.Identity)
                        else:
                            P.op("vector", lambda e, o_ap=o_ap, i_ap=i_ap: e.tensor_copy(out=o_ap, in_=i_ap),
                                 reads=[pb], writes=[xT])
                P.barrier()

            wrot = [0]

            def proj_fm(c, blk_lo, evac):
                wb = wch[wrot[0] % 2]
                wrot[0] += 1
                DMA("gpsimd", wb, wb.t[:, :, :], None, w_in_r[:, :, c * 128:(c + 1) * 128])
                for tb in range(blk_lo, NBW):
                    pb = nps()
                    for kc in range(KC):
                        MM(pb, pb.t[:, :], wb, wb.t[:, kc, :], xT, xT.t[:, kc, tb * 512:(tb + 1) * 512], kc == 0, kc == KC - 1)
                    evac(tb, pb)

            def evac_f32(dst, c):
                def f(tb, pb):
                    ACT(dst, dst.t[:, tb * 512:(tb + 1) * 512], pb, pb.t[:, :], AF.Identity,
                        bias=b_in_fm.t[:, c:c + 1], scale=1.0, reads=[b_in_fm])
                return f

            with ExitStack() as es:
                hA = sbs(es, "hA", [128, NWIN], F32)
                hB = sbs(es, "hB", [128, NWIN], F32)
                hC = sbs(es, "hC", [128, NWIN], F32)
                convo = sbs(es, "convo", [128, 4, NW], F32)
                pooled = sbs(es, "pooled", [128, NW], BF16)
                wpool = sbs(es, "wpool", [128, 4, 128], BF16)
                psc = sbs(es, "psc", [128, 4], F32)
                dww = sbs(es, "dww", [128, 4, 31], F32)
                dwb = sbs(es, "dwb", [128, 4], F32)
                cng = sbs(es, "cng", [128, 4], F32)
                cnb = sbs(es, "cnb", [128, 4], F32)
                t16 = sbs(es, "t16", [128, 16], F32)
                sqb = sbs(es, "sqb", [128, 512], F32)
                meanb = sbs(es, "meanb", [128, 512], F32)
                varb = sbs(es, "varb", [128, 512], F32)
                tnb = sbs(es, "tnb", [128, 512], F32)
                DMA("gpsimd", wpool, wpool.t[:, :, :], None, A["w_pool"].rearrange("g c d -> c g d"))
                DMA("sync", psc, psc.t[:, :], None, A["psc_fm"])
                DMA("sync", dww, dww.t[:, :, :], None, A["dww_fm"])
                DMA("sync", dwb, dwb.t[:, :], None, A["dwb_fm"])
                DMA("sync", cng, cng.t[:, :], None, A["cng_fm"])
                DMA("sync", cnb, cnb.t[:, :], None, A["cnb_fm"])
                for g in range(4):
                    proj_fm(g, 0, evac_f32(hA, g))
                    if ninv > 0:
                        TS(hA, hA.t[:, 0:ninv], hA, hA.t[:, 0:ninv], flags.t[:, 0:1], ALU.mult, reads=[flags])
                    src = hA
                    bufs = [hB, hC]
                    for si, st in enumerate((1, 2, 4, 8)[:g + 1]):
                        dst = bufs[si % 2]
                        TT(dst, dst.t[:, 16:NWIN], src, src.t[:, 16:NWIN], src, src.t[:, 16 - st:NWIN - st], ALU.add)
                        src = dst
                    STT(pooled, pooled.t[:, :], src, src.t[:, HALO:NWIN], 1.0 / POOL_W[g], hA, hA.t[:, HALO:NWIN],
                        ALU.mult, ALU.subtract)
                    if 0 <= fpos < NW:
                        TT(t16, t16.t[:, :], src, src.t[:, HALO + fpos:HALO + fpos + 16], invc, invc.t[:, g * 16:(g + 1) * 16], ALU.mult)
                        TT(pooled, pooled.t[:, fpos:fpos + 16], t16, t16.t[:, :], hA, hA.t[:, HALO + fpos:HALO + fpos + 16], ALU.subtract)
                    yb = yfm[g % 2]
                    for blk in range(NB):
                        pb = nps()
                        MM(pb, pb.t[:, :], wpool, wpool.t[:, g, :], pooled, pooled.t[:, blk * 512:(blk + 1) * 512], True, True)
                        ACT(yb, yb.t[:, blk * 512:(blk + 1) * 512], pb, pb.t[:, :], AF.Identity,
                            scale=psc.t[:, g:g + 1], reads=[psc])
                    DMA("sync", d_cat, A["catT"][g * 128:(g + 1) * 128, o0:o0 + NW], yb, yb.t[:, :], acc=True)
                for j in range(4):
                    proj_fm(4 + j, 0, evac_f32(hA, 4 + j))
                    proj_fm(8 + j, 0, evac_f32(hB, 8 + j))
                    ACT(hB, hB.t[:, :], hB, hB.t[:, :], AF.Sigmoid)
                    TT(hA, hA.t[:, :], hA, hA.t[:, :], hB, hB.t[:, :], ALU.mult)
                    if ninv > 0:
                        TS(hA, hA.t[:, 0:ninv], hA, hA.t[:, 0:ninv], flags.t[:, 0:1], ALU.mult, reads=[flags])
                    acc = convo.t[:, j, :]
                    TS(convo, acc, hA, hA.t[:, HALO - 30:HALO - 30 + NW], dww.t[:, j, 0:1], ALU.mult,
                       dwb.t[:, j:j + 1], ALU.add, reads=[dww, dwb])
                    for k in range(1, 31):
                        STT(convo, acc, hA, hA.t[:, HALO - 30 + k:HALO - 30 + k + NW], dww.t[:, j, k:k + 1],
                            convo, acc, ALU.mult, ALU.add, reads=[dww])
                for blk in range(NB):
                    sl = slice(blk * 512, (blk + 1) * 512)
                    pm = nps()
                    pq = nps()
                    for j in range(4):
                        ACT(sqb, sqb.t[:, :], convo, convo.t[:, j, sl], AF.Square)
                        MM(pm, pm.t[:, :], ones_f, ones_f.t[:, :], convo, convo.t[:, j, sl], j == 0, j == 3)
                        MM(pq, pq.t[:, :], ones_f, ones_f.t[:, :], sqb, sqb.t[:, :], j == 0, j == 3)
                    ACT(meanb, meanb.t[:, :], pm, pm.t[:, :], AF.Identity)
                    TT(varb, varb.t[:, :], meanb, meanb.t[:, :], meanb, meanb.t[:, :], ALU.mult)
                    TT(varb, varb.t[:, :], pq, pq.t[:, :], varb, varb.t[:, :], ALU.subtract)
                    TS(varb, varb.t[:, :], varb, varb.t[:, :], LN_EPS, ALU.add)
                    ACT(varb, varb.t[:, :], varb, varb.t[:, :], AF.Sqrt)
                    P.op("vector", lambda e: e.reciprocal(out=varb.t[:, :], in_=varb.t[:, :]), reads=[varb], writes=[varb])
                    for j in range(4):
                        yb = yfm[j % 2]
                        TT(tnb, tnb.t[:, :], convo, convo.t[:, j, sl], meanb, meanb.t[:, :], ALU.subtract)
                        TT(tnb, tnb.t[:, :], tnb, tnb.t[:, :], varb, varb.t[:, :], ALU.mult)
                        ACT(yb, yb.t[:, 0:512], tnb, tnb.t[:, :], AF.Silu, bias=cnb.t[:, j:j + 1], scale=cng.t[:, j:j + 1],
                            reads=[cnb, cng])
                        DMA("sync", d_cat, A["catT"][(4 + j) * 128:(5 + j) * 128, o0 + blk * 512:o0 + (blk + 1) * 512], yb, yb.t[:, 0:512], acc=True)
                P.barrier()

            with ExitStack() as es:
                biasT = sbs(es, "biasT", [128, 8, 5, 128], F32)
                bvb = sbs(es, "bvb", [128, 1024], F32)
                qT = sbs(es, "qT", [128, NW], BF16)
                kT = sbs(es, "kT", [128, NWIN], BF16)
                Vh = sbs(es, "Vh", [128, NTW, 128], BF16)
                tmpS = [sbs(es, f"tmpS{i}", [128, 5, 128], F32) for i in range(2)]
                PT = [sbs(es, f"PT{i}", [128, 5, 128], BF16) for i in range(2)]
                rden = sbs(es, "rden", [128, 128], F32)
                DMA("sync", biasT, biasT.t[:, :, :, :], None, A["abias"].rearrange("p (h k q) -> p h k q", h=8, k=5))
                DMA("sync", bvb, bvb.t[:, :], None, A["bv_row"].partition_broadcast(128))
                SA = [ps[0], ps[1]]
                SB = [ps[2], ps[3]]
                psO = [ps[4], ps[5]]
                psD = psO
                for h in range(8):
                    cq = 12 + h
                    ck = 20 + h

                    def evq(tb, pb, cq=cq):
                        TS(qT, qT.t[:, (tb - HB) * 512:(tb - HB + 1) * 512], pb, pb.t[:, :], b_in_fm.t[:, cq:cq + 1], ALU.add,
                           ATT_SCALE, ALU.mult, reads=[b_in_fm])

                    def evk(tb, pb, ck=ck):
                        ACT(kT, kT.t[:, tb * 512:(tb + 1) * 512], pb, pb.t[:, :], AF.Identity,
                            bias=b_in_fm.t[:, ck:ck + 1], scale=1.0, reads=[b_in_fm])

                    psrot[0] = 5
                    def nps_proj():
                        psrot[0] = 6 if psrot[0] != 6 else 7
                        return ps[psrot[0]]

                    for (c, lo, ev) in ((cq, HB, evq), (ck, 0, evk)):
                        wb = wch[wrot[0] % 2]
                        wrot[0] += 1
                        DMA("gpsimd", wb, wb.t[:, :, :], None, w_in_r[:, :, c * 128:(c + 1) * 128])
                        for tb in range(lo, NBW):
                            pb = nps_proj()
                            for kc in range(KC):
                                MM(pb, pb.t[:, :], wb, wb.t[:, kc, :], xT, xT.t[:, kc, tb * 512:(tb + 1) * 512], kc == 0, kc == KC - 1)
                            ev(tb, pb)
                    wb = wch[wrot[0] % 2]
                    wrot[0] += 1
                    cv = 28 + h
                    DMA("gpsimd", wb, wb.t[:, :, :], None, w_in_r[:, :, cv * 128:(cv + 1) * 128])
                    for tg in range(NTW // 4):
                        pb = nps_proj()
                        for j in range(4):
                            ti = tg * 4 + j
                            for kc in range(KC):
                                MM(pb, pb.t[:, j * 128:(j + 1) * 128], xT, xT.t[:, kc, ti * 128:(ti + 1) * 128], wb, wb.t[:, kc, :],
                                   kc == 0, kc == KC - 1)
                        TT(Vh, Vh.t[:, tg * 4:(tg + 1) * 4, :], pb, pb.t[:, :].rearrange("p (a b) -> p a b", a=4),
                           bvb, bvb.t[:, h * 128:(h + 1) * 128].unsqueeze(1).to_broadcast([128, 4, 128]), ALU.add)
                    ya = yfm[h % 2]

                    def stA(pi):
                        s_ = pi % 2
                        for kb in range(5):
                            wt = pi + kb
                            pb = SA[s_] if kb < 4 else SB[s_]
                            o_ap = pb.t[:, (kb % 4) * 128:(kb % 4 + 1) * 128]
                            MM(pb, o_ap, kT, kT.t[:, wt * 128:(wt + 1) * 128], qT, qT.t[:, pi * 128:(pi + 1) * 128], True, True)

                    def stB(pi):
                        s_ = pi % 2
                        for kb in range(5):
                            wt = pi + kb
                            pb = SA[s_] if kb < 4 else SB[s_]
                            i_ap = pb.t[:, (kb % 4) * 128:(kb % 4 + 1) * 128]
                            sc_ = flags.t[:, 1:2] if wt * 128 < ninv else 0.0
                            STT(tmpS[s_], tmpS[s_].t[:, kb, :], pb, i_ap, sc_, biasT, biasT.t[:, h, kb, :], ALU.add, ALU.add,
                                reads=[flags])
                        ACT(PT[s_], PT[s_].t[:, :, :], tmpS[s_], tmpS[s_].t[:, :, :], AF.Exp)

                    def stC(pi):
                        s_ = pi % 2
                        for kb in range(5):
                            wt = pi + kb
                            MM(psO[s_], psO[s_].t[:, 128:256], Vh, Vh.t[:, wt, :], PT[s_], PT[s_].t[:, kb, :], kb == 0, kb == 4)
                        for kb in range(5):
                            MM(psD[s_], psD[s_].t[:, 256:384], ones_bf, ones_bf.t[:, :], PT[s_], PT[s_].t[:, kb, :], kb == 0, kb == 4)

                    def stE(pi):
                        s_ = pi % 2
                        TS(rden, rden.t[:, :], psD[s_], psD[s_].t[:, 256:384], 1e-30, ALU.add)
                        P.op("vector", lambda e: e.reciprocal(out=rden.t[:, :], in_=rden.t[:, :]), reads=[rden], writes=[rden])
                        TT(ya, ya.t[:, pi * 128:(pi + 1) * 128], psO[s_], psO[s_].t[:, 128:256], rden, rden.t[:, :], ALU.mult)

                    for t in range(NT + 3):
                        if t < NT:
                            stA(t)
                        if 0 <= t - 1 < NT:
                            stB(t - 1)
                        if 0 <= t - 2 < NT:
                            stC(t - 2)
                        if 0 <= t - 3 < NT:
                            stE(t - 3)
                    DMA("sync", d_cat, A["catT"][(8 + h) * 128:(9 + h) * 128, o0:o0 + NW], ya, ya.t[:, :], acc=True)
                P.barrier()


    NW = NW_L
    NT = NW // 128
    NB = NW // 512
    segs = []
    o = 0
    while o < NW_L:
        n = min(2048 if (NW_L - o) != 2560 else 1536, NW_L - o)
        if NW_L - o > 2048 and NW_L - o - n < 1024:
            n = NW_L - o - 1024
        segs.append((o, n))
        o += n
    for (o0, n) in segs:
        P.use_semset(semset0)
        mixer_segment(o0, n, max(0, min(NINV - o0, HALO + n)))

    P.use_semset(5)
    with ExitStack() as es:
        wout = sbs(es, "wout", [128, KC, D], BF16)
        g1b = sbs(es, "g1b", [128, D], F32)
        b1b = sbs(es, "b1b", [128, D], F32)
        catb = [sbs(es, f"catb{i}", [128, KC, 512], BF16) for i in range(2)]
        xin = [sbs(es, f"x2in{i}", [128, D], F32) for i in range(2)]
        rb = sbs(es, "rb", [128, D], F32)
        x1b = [sbs(es, f"x1b{i}", [128, D], F32) for i in range(2)]
        x1Tt = [sbs(es, f"x1Tt{i}", [128, KC, 128], BF16) for i in range(2)]
        stats = sbs(es, "stats", [128, 4, 6], F32)
        mv = sbs(es, "mv", [128, 2], F32)
        rstd = sbs(es, "rstd", [128, 1], F32)
        defer_on()
        for kc in range(KC):
            DMA("gpsimd", wout, wout.t[:, kc, :], None, A["w_out"][kc * 128:(kc + 1) * 128, :], acc=True)
        defer_off()
        DMA("sync", g1b, g1b.t[:, :], None, A["ln1g"].partition_broadcast(128))
        DMA("sync", b1b, b1b.t[:, :], None, A["ln1b"].partition_broadcast(128))
        def p2_mm(i):
            tb, tj = i // 4, i % 4
            cb = catb[tb % 2]
            if tj == 0:
                DMA("sync", cb, cb.t[:, :, :], d_cat, catT_r[:, :, tb * 512:(tb + 1) * 512])
            xb = xin[i % 2]
            DMA("sync", xb, xb.t[:, :], d_xin, A["x_win"][HALO + i * 128:HALO + (i + 1) * 128, :])
            for nb in range(4):
                pb = ps[nb]
                for kc in range(KC):
                    MM(pb, pb.t[:, :], cb, cb.t[:, kc, tj * 128:(tj + 1) * 128], wout, wout.t[:, kc, nb * 512:(nb + 1) * 512],
                       kc == 0, kc == KC - 1)

        def p2_evac(i):
            xb = xin[i % 2]
            for nb in range(4):
                pb = ps[nb]
                STT(rb, rb.t[:, nb * 512:(nb + 1) * 512], xb, xb.t[:, nb * 512:(nb + 1) * 512], ALPHA, pb, pb.t[:, :],
                    ALU.mult, ALU.add)

        def p2_ln(i):
            x1 = x1b[i % 2]
            _layernorm(P, rb, x1, g1b, b1b, stats, mv, rstd)
            DMA("sync", d_x1, A["x1"][i * 128:(i + 1) * 128, :], x1, x1.t[:, :], acc=True)

        def p2_tr(i):
            x1 = x1b[i % 2]
            xt = x1Tt[i % 2]
            for g in range(4):
                pb = ps[4 + g]
                for j in range(4):
                    kc = g * 4 + j
                    TR(pb, pb.t[:, j * 128:(j + 1) * 128], x1, x1.t[:, kc * 128:(kc + 1) * 128])
                o_ap = xt.t[:, g * 4:(g + 1) * 4, :]
                i_ap = pb.t[:, :].rearrange("p (a b) -> p a b", a=4)
                ACT(xt, o_ap, pb, i_ap, AF.Identity)
            DMA("sync", d_x1T, x1T_r[:, :, i * 128:(i + 1) * 128], xt, xt.t[:, :, :], acc=True)

        p2_mm(0)
        p2_evac(0)
        for i in range(NT):
            if i + 1 < NT:
                p2_mm(i + 1)
            p2_ln(i)
            p2_tr(i)
            if i + 1 < NT:
                p2_evac(i + 1)
        P.barrier()

    P.use_semset(5)
    if True:
        with ExitStack() as es:
            topSb = [sbs(es, f"topSb{i}", [128, 4, 16, 16], F32) for i in range(2)]
            topIb = [sbs(es, f"topIb{i}", [128, 4, 16, 16], U32) for i in range(2)]
            wq = sbs(es, "wq", [128, KC, D], BF16)
            skf = sbs(es, "skf", [128, 2, 128], F32)
            skT = sbs(es, "skT", [128, 2, 128], BF16)
            x1Tb = [sbs(es, f"x1Tb{i}", [128, KC, 512], BF16) for i in range(2)]
            qTc = [sbs(es, f"qTc{i}", [128, 512], BF16) for i in range(2)]
            scb = [sbs(es, f"scb{i}", [128, 4, 128], F32) for i in range(2)]
            scr = sbs(es, "scr", [128, 128], F32)
            defer_on()
            for kc in range(KC):
                DMA("gpsimd", wq, wq.t[:, kc, :], None, A["w_q"][kc * 128:(kc + 1) * 128, :], acc=True)
            defer_off()
            DMA("sync", skf, skf.t[:, :, :], None, A["sub_keys"].rearrange("p k c -> k p c"))
            for p_ in range(2):
                pb = nps()
                TR(pb, pb.t[:, 0:128], skf, skf.t[:, p_, :])
                ACT(skT, skT.t[:, p_, :], pb, pb.t[:, 0:128], AF.Identity)
            scr4 = [sbs(es, f"scr4_{i}", [128, 128], F32) for i in range(4)]
            tS_al = [[Buf(f"tSal{k}_{tj}", topSb[k].t) for tj in range(4)] for k in range(2)]
            tI_al = [[Buf(f"tIal{k}_{tj}", topIb[k].t) for tj in range(4)] for k in range(2)]

            def ld_blk(tb):
                xb_ = x1Tb[tb % 2]
                DMA("sync", xb_, xb_.t[:, :, :], d_x1T, x1T_r[:, :, tb * 512:(tb + 1) * 512])

            def st_M(c):
                tb, hp = c // 16, c % 16
                xb_ = x1Tb[tb % 2]
                pb = ps[c % 4]
                for kc in range(KC):
                    MM(pb, pb.t[:, :], wq, wq.t[:, kc, hp * 128:(hp + 1) * 128], xb_, xb_.t[:, kc, :], kc == 0, kc == KC - 1)
                qc = qTc[c % 2]
                ACT(qc, qc.t[:, :], pb, pb.t[:, :], AF.Identity)

            def st_S(c):
                hp = c % 16
                qc = qTc[c % 2]
                pb2 = ps[4 + c % 4]
                for tj in range(4):
                    MM(pb2, pb2.t[:, tj * 128:(tj + 1) * 128], qc, qc.t[:, tj * 128:(tj + 1) * 128], skT, skT.t[:, hp % 2, :], True, True)
                sc = scb[c % 2]
                ACT(sc, sc.t[:, :, :], pb2, pb2.t[:, :].rearrange("p (a b) -> p a b", a=4), AF.Identity)

            def st_T(c):
                tb, hp = c // 16, c % 16
                k = tb % 2
                topS, topI = topSb[k], topIb[k]
                sc = scb[c % 2]
                vs = [sc.t[:, tj, :] for tj in range(4)]
                sas = [topS.t[:, tj, hp, 0:8] for tj in range(4)]
                sbs_ = [topS.t[:, tj, hp, 8:16] for tj in range(4)]
                ias = [topI.t[:, tj, hp, 0:8] for tj in range(4)]
                ibs = [topI.t[:, tj, hp, 8:16] for tj in range(4)]
                for tj in range(4):
                    P.op("vector", lambda e, v_=vs[tj], sa=sas[tj]: e.max(out=sa, in_=v_), reads=[sc], writes=[tS_al[k][tj]])
                for tj in range(4):
                    P.op("vector", lambda e, v_=vs[tj], sa=sas[tj], sr=scr4[tj]: e.match_replace(out=sr.t[:, :], in_to_replace=sa, in_values=v_, imm_value=-1e30),
                         reads=[sc, tS_al[k][tj]], writes=[scr4[tj]])
                for tj in range(4):
                    P.op("vector", lambda e, sb_=sbs_[tj], sr=scr4[tj]: e.max(out=sb_, in_=sr.t[:, :]), reads=[scr4[tj]], writes=[tS_al[k][tj]])
                for tj in range(4):
                    P.op("vector", lambda e, v_=vs[tj], sa=sas[tj], ia=ias[tj]: e.max_index(out=ia, in_max=sa, in_values=v_),
                         reads=[sc, tS_al[k][tj]], writes=[tI_al[k][tj]])
                for tj in range(4):
                    P.op("vector", lambda e, v_=vs[tj], sb_=sbs_[tj], ib=ibs[tj]: e.max_index(out=ib, in_max=sb_, in_values=v_),
                         reads=[sc, tS_al[k][tj]], writes=[tI_al[k][tj]])
                if hp == 15:
                    DMA("sync", d_top, A["topS"][tb * 512:(tb + 1) * 512, :].rearrange("(a p) n -> p a n", p=128), None,
                        topS.t[:, :, :, :].rearrange("p a h r -> p a (h r)"), reads=tS_al[k], acc=True)
                    DMA("sync", d_top, A["topI"][tb * 512:(tb + 1) * 512, :].rearrange("(a p) n -> p a n", p=128), None,
                        topI.t[:, :, :, :].rearrange("p a h r -> p a (h r)"), reads=tI_al[k], acc=True)

            NC3 = NB * 16
            ld_blk(0)
            for c in range(NC3 + 2):
                if c < NC3:
                    if c % 16 == 0 and c // 16 + 1 < NB:
                        ld_blk(c // 16 + 1)
                    st_M(c)
                if 0 <= c - 1 < NC3:
                    st_S(c - 1)
                if 0 <= c - 2 < NC3:
                    st_T(c - 2)
            P.barrier()
        P.use_semset(5)
        with ExitStack() as es:
            wpg = sbs(es, "wpg", [128, KC, D], BF16)
            wple = sbs(es, "wple", [128, 2, D], BF16)
            x1Tt = [sbs(es, f"x1Tu{i}", [128, KC, 128], BF16) for i in range(2)]
            pin = [sbs(es, f"pin{i}", [128, 256], F32) for i in range(2)]
            pT = [sbs(es, f"pT{i}", [128, 2, 128], BF16) for i in range(2)]
            x1t = [sbs(es, f"x1t{i}", [128, D], F32) for i in range(2)]
            sgb = [sbs(es, f"sgb{i}", [128, 512], F32) for i in range(2)]
            r2b = [sbs(es, f"r2b{i}", [128, D], F32) for i in range(2)]
            defer_on()
            for kc in range(KC):
                DMA("gpsimd", wpg, wpg.t[:, kc, :], None, A["w_pg"][kc * 128:(kc + 1) * 128, :], acc=True)
            for k2 in range(2):
                DMA("gpsimd", wple, wple.t[:, k2, :], None, A["w_ple"][k2 * 128:(k2 + 1) * 128, :], acc=True)
            defer_off()
            def ld3b(i):
                xt_ = x1Tt[i % 2]
                DMA("sync", xt_, xt_.t[:, :, :], d_x1T, x1T_r[:, :, i * 128:(i + 1) * 128])
                DMA("sync", pin[i % 2], pin[i % 2].t[:, :], None, A["p_own"][i * 128:(i + 1) * 128, :])
                DMA("sync", x1t[i % 2], x1t[i % 2].t[:, :], d_x1, A["x1"][i * 128:(i + 1) * 128, :])

            ld3b(0)
            for i in range(NT):
                xt = x1Tt[i % 2]
                pi_ = pin[i % 2]
                x1 = x1t[i % 2]
                if i + 1 < NT:
                    ld3b(i + 1)
                pb = ps[6 + i % 2]
                for k2 in range(2):
                    TR(pb, pb.t[:, k2 * 128:(k2 + 1) * 128], pi_, pi_.t[:, k2 * 128:(k2 + 1) * 128])
                pt = pT[i % 2]
                ACT(pt, pt.t[:, :, :], pb, pb.t[:, 0:256].rearrange("p (a b) -> p a b", a=2), AF.Identity)
                r2 = r2b[i % 2]
                for nb in range(4):
                    sl = slice(nb * 512, (nb + 1) * 512)
                    pg = ps[nb % 2]
                    for kc in range(KC):
                        MM(pg, pg.t[:, :], xt, xt.t[:, kc, :], wpg, wpg.t[:, kc, sl], kc == 0, kc == KC - 1)
                    sg = sgb[nb % 2]
                    ACT(sg, sg.t[:, :], pg, pg.t[:, :], AF.Sigmoid)
                    pp = ps[2 + nb % 2]
                    for k2 in range(2):
                        MM(pp, pp.t[:, :], pt, pt.t[:, k2, :], wple, wple.t[:, k2, sl], k2 == 0, k2 == 1)
                    TT(sg, sg.t[:, :], pp, pp.t[:, :], sg, sg.t[:, :], ALU.mult)
                    STT(r2, r2.t[:, sl], x1, x1.t[:, sl], ALPHA, sg, sg.t[:, :], ALU.mult, ALU.add)
                DMA("sync", d_r2, A["r2"][i * 128:(i + 1) * 128, :], r2, r2.t[:, :], acc=True)
            P.barrier()

        conv_step(10 ** 6)
        with ExitStack() as es:
            topSt = [sbs(es, f"topSt{i}", [128, 16, 16], F32) for i in range(2)]
            topIt = [sbs(es, f"topIt{i}", [128, 16, 16], U32) for i in range(2)]
            g2b = sbs(es, "g2b", [128, D], F32)
            b2b = sbs(es, "b2b", [128, D], F32)
            uvb = [sbs(es, f"uvb{i}", [128, 2 * D], BF16) for i in range(NUV)]
            dg = [sbs(es, f"dg{i}", [128, 128], BF16) for i in range(4)]
            zc = [sbs(es, f"zc{i}", [128, 1], F32) for i in range(4)]
            gelc = [sbs(es, f"gelc{i}", [128, 1], F32) for i in range(4)]
            x1t = [sbs(es, f"x4t{i}", [128, D], F32) for i in range(2)]
            yac = [sbs(es, f"yac{i}", [128, D], F32) for i in range(3)]
            outb = [sbs(es, f"outb{i}", [128, D], F32) for i in range(2)]
            junk = sbs(es, "junk", [128, D], BF16)
            topIf = sbs(es, "topIf", [128, 16, 16], F32)
            cs2 = [sbs(es, f"cs2_{i}", [128, 256], F32) for i in range(2)]
            tS = sbs(es, "tS", [128, 8, 16], F32)
            scr2 = sbs(es, "scr2", [128, 256], F32)
            posu = sbs(es, "posu", [128, 8, 16], U32)
            pa_u = sbs(es, "pa_u", [128, 8, 16], U32)
            pb_u = sbs(es, "pb_u", [128, 8, 16], U32)
            pa_f = sbs(es, "pa_f", [128, 8, 16], F32)
            pb_f = sbs(es, "pb_f", [128, 8, 16], F32)
            oh = sbs(es, "oh", [128, 128, 16], F32)
            iota_rep = sbs(es, "iota_rep", [128, 128, 16], F32)
            sel1 = sbs(es, "sel1", [128, 128], F32)
            sel2 = sbs(es, "sel2", [128, 128], F32)
            acol = [sbs(es, f"acol{i}", [128, 1], F32) for i in range(4)]
            cint = sbs(es, "cint", [128, 2], U32)
            eidf = sbs(es, "eidf", [128, 128], F32)
            eidx = [sbs(es, f"eidx{i}", [128, 128], U32) for i in range(2)]
            nmax = sbs(es, "nmax", [128, 8], F32)
            ge = sbs(es, "ge", [128, 8, 16], F32)
            gsum = sbs(es, "gsum", [128, 8], F32)
            gates = [sbs(es, f"gate{i}", [128, 128], F32) for i in range(2)]
            stats = sbs(es, "stats4", [128, 4, 6], F32)
            mv = sbs(es, "mv4", [128, 2], F32)
            rstd = sbs(es, "rstd4", [128, 1], F32)
            DMA("sync", g2b, g2b.t[:, :], None, A["ln2g"].partition_broadcast(128))
            DMA("sync", b2b, b2b.t[:, :], None, A["ln2b"].partition_broadcast(128))
            P.op("gpsimd", lambda e: e.iota(iota_rep.t[:, :, :], pattern=[[0, 128], [1, 16]], base=0, channel_multiplier=0,
                                            allow_small_or_imprecise_dtypes=True), writes=[iota_rep])
            P.op("vector", lambda e: e.memset(cint.t[:, 0:1], 4), writes=[cint])
            P.op("vector", lambda e: e.memset(cint.t[:, 1:2], 15), writes=[cint])
            gcnt = [0]

            def idx_gen(i):
                tSt = topSt[i % 2]
                tIt = topIt[i % 2]
                ex = eidx[i % 2]
                gate = gates[i % 2]
                DMA("sync", tSt, tSt.t[:, :, :].rearrange("p h r -> p (h r)"), d_top, A["topS"][i * 128:(i + 1) * 128, :])
                DMA("sync", tIt, tIt.t[:, :, :].rearrange("p h r -> p (h r)"), d_top, A["topI"][i * 128:(i + 1) * 128, :])
                yield
                P.op("vector", lambda e: e.tensor_copy(out=topIf.t[:, :, :], in_=tIt.t[:, :, :]), reads=[tIt], writes=[topIf])
                yield
                for h in range(8):
                    s1 = tSt.t[:, 2 * h, :].unsqueeze(2).to_broadcast([128, 16, 16])
                    s2 = tSt.t[:, 2 * h + 1, :].unsqueeze(1).to_broadcast([128, 16, 16])
                    csb = cs2[h % 2]
                    csh = csb.t[:, :].rearrange("p (a b) -> p a b", a=16)
                    TT(csb, csh, tSt, s1, tSt, s2, ALU.add)
                    yield
                    P.op("vector", lambda e, h=h, csb=csb: e.max(out=tS.t[:, h, 0:8], in_=csb.t[:, :]), reads=[csb], writes=[tS])
                    yield
                    P.op("vector", lambda e, h=h, csb=csb: e.match_replace(out=scr2.t[:, :], in_to_replace=tS.t[:, h, 0:8], in_values=csb.t[:, :], imm_value=-1e30),
                         reads=[csb, tS], writes=[scr2])
                    yield
                    P.op("vector", lambda e, h=h: e.max(out=tS.t[:, h, 8:16], in_=scr2.t[:, :]), reads=[scr2], writes=[tS])
                    yield
                    P.op("vector", lambda e, h=h, csb=csb: e.max_index(out=posu.t[:, h, 0:8], in_max=tS.t[:, h, 0:8], in_values=csb.t[:, :]),
                         reads=[csb, tS], writes=[posu])
                    yield
                    P.op("vector", lambda e, h=h, csb=csb: e.max_index(out=posu.t[:, h, 8:16], in_max=tS.t[:, h, 8:16], in_values=csb.t[:, :]),
                         reads=[csb, tS], writes=[posu])
                    yield
                P.op("vector", lambda e: e.tensor_scalar(out=pa_u.t[:, :, :], in0=posu.t[:, :, :], scalar1=cint.t[:, 0:1], scalar2=None,
                                                         op0=ALU.logical_shift_right), reads=[posu, cint], writes=[pa_u])
                yield
                P.op("vector", lambda e: e.tensor_scalar(out=pb_u.t[:, :, :], in0=posu.t[:, :, :], scalar1=cint.t[:, 1:2], scalar2=None,
                                                         op0=ALU.bitwise_and), reads=[posu, cint], writes=[pb_u])
                yield
                P.op("vector", lambda e: e.tensor_copy(out=pa_f.t[:, :, :], in_=pa_u.t[:, :, :]), reads=[pa_u], writes=[pa_f])
                yield
                P.op("vector", lambda e: e.tensor_copy(out=pb_f.t[:, :, :], in_=pb_u.t[:, :, :]), reads=[pb_u], writes=[pb_f])
                yield
                tI4 = topIf.t[:, :, :].rearrange("p (h two) r -> p h two r", two=2)
                for (pf, which, dst) in ((pa_f, 0, sel1), (pb_f, 1, sel2)):
                    TT(oh, oh.t[:, :, :], pf, pf.t[:, :, :].rearrange("p h r -> p (h r)").unsqueeze(2).to_broadcast([128, 128, 16]),
                       iota_rep, iota_rep.t[:, :, :], ALU.is_equal)
                    yield
                    TT(oh, oh.t[:, :, :].rearrange("p (h r) a -> p h r a", h=8), oh, oh.t[:, :, :].rearrange("p (h r) a -> p h r a", h=8),
                       topIf, tI4[:, :, which, :].unsqueeze(2).to_broadcast([128, 8, 16, 16]), ALU.mult)
                    yield
                    P.op("vector", lambda e, dst=dst: e.tensor_reduce(out=dst.t[:, :], in_=oh.t[:, :, :], axis=mybir.AxisListType.X, op=ALU.add),
                         reads=[oh], writes=[dst])
                    yield
                STT(eidf, eidf.t[:, :], sel1, sel1.t[:, :], 128.0, sel2, sel2.t[:, :], ALU.mult, ALU.add)
                yield
                TS(eidf, eidf.t[:, :], eidf, eidf.t[:, :], 16383.0, ALU.min, 0.0, ALU.max)
                yield
                if (i + 1) * 128 <= NINV - HALO:
                    TS(eidf, eidf.t[:, :], eidf, eidf.t[:, :], flags.t[:, 0:1], ALU.mult, reads=[flags])
                    yield
                P.op("vector", lambda e: e.tensor_copy(out=ex.t[:, :], in_=eidf.t[:, :]), reads=[eidf], writes=[ex])
                yield
                TS(nmax, nmax.t[:, :], tS, tS.t[:, :, 0], -1.0, ALU.mult)
                yield
                for h in range(8):
                    ACT(ge, ge.t[:, h, :], tS, tS.t[:, h, :], AF.Exp, bias=nmax.t[:, h:h + 1], scale=1.0, reads=[nmax])
                yield
                P.op("vector", lambda e: e.tensor_reduce(out=gsum.t[:, :], in_=ge.t[:, :, :], axis=mybir.AxisListType.X, op=ALU.add),
                     reads=[ge], writes=[gsum])
                yield
                P.op("vector", lambda e: e.reciprocal(out=gsum.t[:, :], in_=gsum.t[:, :]), reads=[gsum], writes=[gsum])
                yield
                TT(gate, gate.t[:, :].rearrange("p (h r) -> p h r", h=8), ge, ge.t[:, :, :],
                   gsum, gsum.t[:, :].unsqueeze(2).to_broadcast([128, 8, 16]), ALU.mult)
                yield

            def finish_gen(k):
                ya = yac[k % 3]
                psb = ps[(k % 2) * 4:(k % 2) * 4 + 4]
                for nb in range(4):
                    sl = slice(nb * 512, (nb + 1) * 512)
                    TT(ya, ya.t[:, sl], psb[nb], psb[nb].t[:, :], ya, ya.t[:, sl], ALU.add)
                    yield
                ob = outb[k % 2]
                for k4 in range(4):
                    P.op("vector", lambda e, k4=k4: e.bn_stats(out=stats.t[:, k4, :], in_=ya.t[:, k4 * 512:(k4 + 1) * 512]),
                         reads=[ya], writes=[stats])
                    yield
                P.op("vector", lambda e: e.bn_aggr(out=mv.t[:, :], in_=stats.t[:, :, :].rearrange("p a b -> p (a b)")), reads=[stats], writes=[mv])
                yield
                P.op("vector", lambda e: e.tensor_scalar(out=rstd.t[:, :], in0=mv.t[:, 1:2], scalar1=LN_EPS, scalar2=None, op0=ALU.add),
                     reads=[mv], writes=[rstd])
                P.op("scalar", lambda e: e.activation(out=rstd.t[:, :], in_=rstd.t[:, :], func=AF.Sqrt), reads=[rstd], writes=[rstd])
                yield
                P.op("vector", lambda e: e.reciprocal(out=rstd.t[:, :], in_=rstd.t[:, :]), reads=[rstd], writes=[rstd])
                yield
                P.op("vector", lambda e: e.tensor_scalar(out=ob.t[:, :], in0=ya.t[:, :], scalar1=mv.t[:, 0:1], scalar2=rstd.t[:, 0:1],
                                                         op0=ALU.subtract, op1=ALU.mult), reads=[ya, mv, rstd], writes=[ob])
                yield
                P.op("vector", lambda e: e.tensor_tensor(out=ob.t[:, :], in0=ob.t[:, :], in1=g2b.t[:, :], op=ALU.mult), reads=[ob, g2b], writes=[ob])
                yield
                P.op("vector", lambda e: e.tensor_tensor(out=ob.t[:, :], in0=ob.t[:, :], in1=b2b.t[:, :], op=ALU.add), reads=[ob, b2b], writes=[ob])
                DMA("sync", d_y, A["y"][k * 128:(k + 1) * 128, :], ob, ob.t[:, :], acc=True)
                yield

            def tile_loads(i):
                x1 = x1t[i % 2]
                ya = yac[i % 3]
                DMA("sync", x1, x1.t[:, :], d_x1, A["x1"][i * 128:(i + 1) * 128, :])
                DMA("sync", ya, ya.t[:, :], d_r2, A["r2"][i * 128:(i + 1) * 128, :])

            def step(g, n):
                if g is None:
                    return None
                for _ in range(n):
                    try:
                        next(g)
                    except StopIteration:
                        return None
                return g

            def drain(g):
                while g is not None:
                    g = step(g, 1)

            P.use_semset(6)
            drain(idx_gen(0))
            tile_loads(0)
            for i in range(NT):
                P.use_semset(6 + i % 2)
                x1 = x1t[i % 2]
                ex = eidx[i % 2]
                gate = gates[i % 2]
                psb = ps[(i % 2) * 4:(i % 2) * 4 + 4]
                gi = idx_gen(i + 1) if i + 1 < NT else None
                gf = finish_gen(i - 1) if i >= 1 else None
                if i + 1 < NT:
                    tile_loads(i + 1)
                bufs = {}

                def consume(j):
                    buf = bufs.pop(j)
                    dj = dg[j % 4]
                    ac = acol[j % 4]
                    ACT(ac, ac.t[:, 0:1], gelc[j % 4], gelc[j % 4].t[:, 0:1], AF.Identity, scale=gate.t[:, j:j + 1], reads=[gate])
                    ACT(dj, dj.t[:, :], ident_bf, ident_bf.t[:, :], AF.Identity, scale=ac.t[:, 0:1], reads=[ac])
                    for nb in range(4):
                        MM(psb[nb], psb[nb].t[:, :], dj, dj.t[:, :], buf, buf.t[:, D + nb * 512:D + (nb + 1) * 512], j == 0, j == 127)

                for j in range(128):
                    buf = uvb[gcnt[0] % NUV]
                    gcnt[0] += 1
                    bufs[j] = buf
                    P.dma("gpsimd", buf, d_tab, lambda e, buf=buf, ex=ex, j=j: e.indirect_dma_start(
                        out=buf.t[:, :], out_offset=None, in_=A["uv_bf"],
                        in_offset=bass.IndirectOffsetOnAxis(ap=ex.t[:, j:j + 1], axis=0)), reads=[ex])
                    zj = zc[j % 4]
                    STT(junk, junk.t[:, :], buf, buf.t[:, 0:D], 1.0, x1, x1.t[:, :], ALU.mult, ALU.mult,
                        accum=zj.t[:, 0:1], accb=zj)
                    ACT(gelc[j % 4], gelc[j % 4].t[:, 0:1], zj, zj.t[:, 0:1], AF.Gelu)
                    if j >= 2:
                        consume(j - 2)
                    gi = step(gi, 2)
                    gf = step(gf, 1)
                consume(126)
                consume(127)
                drain(gf)
                drain(gi)
            drain(finish_gen(NT - 1))
            P.barrier()


def _layernorm(P, rb, ob, gb, bb, stats, mv, rstd):
    for k in range(4):
        P.op("vector", lambda e, k=k: e.bn_stats(out=stats.t[:, k, :], in_=rb.t[:, k * 512:(k + 1) * 512]),
             reads=[rb], writes=[stats])
    P.op("vector", lambda e: e.bn_aggr(out=mv.t[:, :], in_=stats.t[:, :, :].rearrange("p a b -> p (a b)")), reads=[stats], writes=[mv])
    P.op("vector", lambda e: e.tensor_scalar(out=rstd.t[:, :], in0=mv.t[:, 1:2], scalar1=LN_EPS, scalar2=None, op0=ALU.add),
         reads=[mv], writes=[rstd])
    P.op("scalar", lambda e: e.activation(out=rstd.t[:, :], in_=rstd.t[:, :], func=AF.Sqrt), reads=[rstd], writes=[rstd])
    P.op("vector", lambda e: e.reciprocal(out=rstd.t[:, :], in_=rstd.t[:, :]), reads=[rstd], writes=[rstd])
    P.op("vector", lambda e: e.tensor_scalar(out=ob.t[:, :], in0=rb.t[:, :], scalar1=mv.t[:, 0:1], scalar2=rstd.t[:, 0:1],
                                             op0=ALU.subtract, op1=ALU.mult), reads=[rb, mv, rstd], writes=[ob])
    P.op("vector", lambda e: e.tensor_tensor(out=ob.t[:, :], in0=ob.t[:, :], in1=gb.t[:, :], op=ALU.mult), reads=[ob, gb], writes=[ob])
    P.op("vector", lambda e: e.tensor_tensor(out=ob.t[:, :], in0=ob.t[:, :], in1=bb.t[:, :], op=ALU.add), reads=[ob, bb], writes=[ob])


_W_SPECS = [
    ("w_in", [D, IN_WIDTH]), ("b_in_fm", [128, 36]), ("bv_row", [1, 1024]), ("w_pool", [4, 128, 128]),
    ("psc_fm", [128, 4]), ("dww_fm", [128, 4, 31]), ("dwb_fm", [128, 4]), ("cng_fm", [128, 4]), ("cnb_fm", [128, 4]),
    ("abias", [128, 8 * 5 * 128]), ("w_out", [D, D]), ("ln1g", [1, D]), ("ln1b", [1, D]), ("w_q", [D, D]),
    ("sub_keys", [2, 128, 128]), ("u_tab", [16384, D]), ("v_tab", [16384, D]), ("w_ple", [256, D]), ("w_pg", [D, D]),
    ("ln2g", [1, D]), ("ln2b", [1, D]),
]


DEPTH = 4
NW_FINAL = 2048


def _layer_dims(l):
    nw = NW_FINAL + HALO * (DEPTH - 1 - l)
    ninv = HALO * (DEPTH - l)
    return nw, ninv


def build_program():
    nc = bass.Bass("TRN2", target_bir_lowering=False)
    NWMAX = _layer_dims(0)[0]
    x0 = nc.dram_tensor("x_win0", [NWMAX + HALO, D], F32, kind="ExternalInput").ap()
    flags_d = nc.dram_tensor("flags", [1, 2], F32, kind="ExternalInput").ap()
    invc_d = nc.dram_tensor("invc", [1, 64], F32, kind="ExternalInput").ap()
    ident_d = nc.dram_tensor("ident", [128, 128], F32, kind="ExternalInput").ap()
    y_out = nc.dram_tensor("y", [NW_FINAL, D], F32, kind="ExternalOutput").ap()
    scr = {
        "catT": nc.dram_tensor("catT", [D, NWMAX], BF16, kind="Internal").ap(),
        "x1T": nc.dram_tensor("x1T", [D, NWMAX], BF16, kind="Internal").ap(),
        "x1": nc.dram_tensor("x1", [NWMAX, D], F32, kind="Internal").ap(),
        "r2": nc.dram_tensor("r2", [NWMAX, D], F32, kind="Internal").ap(),
        "topS": nc.dram_tensor("topS", [NWMAX, 256], F32, kind="Internal").ap(),
        "topI": nc.dram_tensor("topI", [NWMAX, 256], U32, kind="Internal").ap(),
    }
    ybuf = [nc.dram_tensor(f"ybuf{i}", [NWMAX, D], F32, kind="Internal").ap() for i in range(2)]
    P = Prog(nc)
    ps = [P.ps(f"ps{i}", [128, 512], F32) for i in range(8)]
    ident = P.sb("ident", [128, 128], F32)
    ones_bf = P.sb("ones_bf", [128, 128], BF16)
    ones_f = P.sb("ones_f", [128, 128], F32)
    flags = P.sb("flags_sb", [128, 2], F32)
    invc = P.sb("invc_sb", [128, 64], F32)
    P.dma("sync", ident, None, lambda e: e.dma_start(out=ident.t[:, :], in_=ident_d))
    P.dma("sync", flags, None, lambda e: e.dma_start(out=flags.t[:, :], in_=flags_d.partition_broadcast(128)))
    P.dma("sync", invc, None, lambda e: e.dma_start(out=invc.t[:, :], in_=invc_d.partition_broadcast(128)))
    P.op("vector", lambda e: e.memset(ones_bf.t[:, :], 1.0), writes=[ones_bf])
    P.op("vector", lambda e: e.memset(ones_f.t[:, :], 1.0 / 512.0), writes=[ones_f])
    ident_bf = P.sb("ident_bf", [128, 128], BF16)
    P.op("vector", lambda e: e.tensor_copy(out=ident_bf.t[:, :], in_=ident.t[:, :]), reads=[ident], writes=[ident_bf])
    consts = (ident, ones_bf, ones_f, flags, invc, ident_bf)
    d_cat, d_x1, d_x1T, d_r2, d_top, d_tab = (P.dram(n) for n in ("d_cat", "d_x1", "d_x1T", "d_r2", "d_top", "d_tab"))
    d_ys = [P.dram("d_y0"), P.dram("d_y1")]
    d_in = P.dram("d_in")
    d_out = P.dram("d_out")
    As = []
    for l in range(DEPTH):
        nw, ninv = _layer_dims(l)
        A = dict(scr)
        for name, shape in _W_SPECS:
            A[name] = nc.dram_tensor(f"{name}_{l}", shape, F32, kind="ExternalInput").ap()
        A["p_own"] = nc.dram_tensor(f"p_own_{l}", [nw, 256], F32, kind="ExternalInput").ap()
        A["uv_bf"] = nc.dram_tensor(f"uv_bf_{l}", [16384, 2 * D], BF16, kind="Internal").ap()
        As.append(A)
    cv = [P.sb(f"cv{i}", [128, 2, D], BF16) for i in range(2)]

    def conv_gen(l):
        jobs = []
        for nm, off in (("u_tab", 0), ("v_tab", D)):
            src = As[l][nm].rearrange("(p r) d -> p r d", r=128)
            dst = As[l]["uv_bf"].rearrange("(p r) d -> p r d", r=128)[:, :, off:off + D]
            for c in range(64):
                jobs.append((src[:, c * 2:(c + 1) * 2, :], dst[:, c * 2:(c + 1) * 2, :]))

        def load(k):
            b_, (sa, _) = cv[k % 2], jobs[k]
            P.dma("gpsimd", b_, None, lambda e: e.dma_start(out=b_.t[:, :, :], in_=sa))

        def store(k):
            b_, (_, da) = cv[k % 2], jobs[k]
            P.dma("gpsimd", d_tab, b_, lambda e: e.dma_start(out=da, in_=b_.t[:, :, :]), accumulate=True)

        load(0)
        yield
        for k in range(len(jobs)):
            if k + 1 < len(jobs):
                load(k + 1)
                yield
            store(k)
            yield

    x_cur, d_cur = x0, d_in
    for l in range(DEPTH):
        nw, ninv = _layer_dims(l)
        A = As[l]
        A["x_win"] = x_cur
        if l == DEPTH - 1:
            A["y"], d_y = y_out, d_out
        else:
            A["y"], d_y = ybuf[l % 2], d_ys[l % 2]
        _emit_layer(nc, P, ps, A, nw, ninv, consts, (d_cat, d_x1, d_x1T, d_r2, d_top, d_cur, d_y, d_tab), 1 + l, conv_gen(l))
        x_cur, d_cur = A["y"], d_y
    P.finish([d_out])
    return nc


def _attn_bias_layout(rel_bias_l):
    q = np.arange(128)[None, :]
    j = np.arange(640)[:, None]
    dist = q - j + 512
    idx = np.clip(dist, -63, 128) + 63
    g = rel_bias_l[:, idx]
    masked = ((q < 64) & (j >= 576)) | ((q >= 64) & (j < 64))
    g = np.where(masked[None], np.float32(NEG), g).astype(np.float32)
    g = g.reshape(8, 5, 128, 128).transpose(2, 0, 1, 3)
    return np.ascontiguousarray(g.reshape(128, 8 * 5 * 128))


def _fm(v, nchunk):
    return np.ascontiguousarray(v.reshape(nchunk, 128).T)


def layer_weights(inp, l):
    w = {}
    w["w_in"] = np.ascontiguousarray(inp["w_in"][l])
    w["b_in_fm"] = _fm(inp["b_in"][l], 36)
    w["bv_row"] = np.ascontiguousarray(inp["b_in"][l][3584:4608].reshape(1, 1024))
    w["w_pool"] = np.ascontiguousarray(inp["w_pool"][l])
    w["psc_fm"] = _fm(inp["pool_scale"][l], 4)
    w["dww_fm"] = np.ascontiguousarray(inp["dw_w"][l].reshape(31, 4, 128).transpose(2, 1, 0))
    w["dwb_fm"] = _fm(inp["dw_b"][l], 4)
    w["cng_fm"] = _fm(inp["cn_g"][l], 4)
    w["cnb_fm"] = _fm(inp["cn_b"][l], 4)
    w["abias"] = _attn_bias_layout(inp["rel_bias"][l])
    w["w_out"] = np.ascontiguousarray(inp["w_out"][l])
    w["ln1g"] = np.ascontiguousarray(inp["ln1_g"][l].reshape(1, D))
    w["ln1b"] = np.ascontiguousarray(inp["ln1_b"][l].reshape(1, D))
    w["w_q"] = np.ascontiguousarray(inp["w_q"][l])
    w["sub_keys"] = np.ascontiguousarray(inp["sub_keys"][l])
    w["u_tab"] = np.ascontiguousarray(inp["u_tab"][l])
    w["v_tab"] = np.ascontiguousarray(inp["v_tab"][l])
    w["w_ple"] = np.ascontiguousarray(inp["w_ple"][l])
    w["w_pg"] = np.ascontiguousarray(inp["w_pg"][l])
    w["ln2g"] = np.ascontiguousarray(inp["ln2_g"][l].reshape(1, D))
    w["ln2b"] = np.ascontiguousarray(inp["ln2_b"][l].reshape(1, D))
    return w


def core_consts(first_half):
    flags = np.array([[0.0, NEG]] if first_half else [[1.0, 0.0]], np.float32)
    invc = np.zeros((4, 16), np.float32)
    t = np.arange(16)
    for g, wv in enumerate(POOL_W):
        cnt = np.minimum(t + 1, wv) if first_half else np.full(16, wv)
        invc[g] = (1.0 / cnt).astype(np.float32)
    return flags, invc.reshape(1, 64)


_PROG_CACHE = {}


def kernel(**inputs):
    inp = {k: np.asarray(v) for k, v in inputs.items()}
    x = np.ascontiguousarray(inp["x"], dtype=np.float32)
    B, S, _ = x.shape
    assert S == 2 * NW_FINAL and inp["w_in"].shape[0] == DEPTH
    if "nc" not in _PROG_CACHE:
        _PROG_CACHE["nc"] = build_program()
    nc = _PROG_CACHE["nc"]
    ident = np.eye(128, dtype=np.float32)
    wl = [layer_weights(inp, l) for l in range(DEPTH)]
    in_maps = []
    for c in range(2 * B):
        b, half = c // 2, c % 2
        end = (half + 1) * NW_FINAL
        flags, invc = core_consts(half == 0)
        m = {"flags": flags, "invc": invc, "ident": ident}

        def window(arr, n):
            lo = end - n
            if lo >= 0:
                return np.ascontiguousarray(arr[lo:end])
            out = np.zeros((n,) + arr.shape[1:], np.float32)
            out[-lo:] = arr[0:end]
            return out

        m["x_win0"] = window(x[b], _layer_dims(0)[0] + HALO)
        for l in range(DEPTH):
            m[f"p_own_{l}"] = window(inp["p"][l, b], _layer_dims(l)[0])
            for k, v in wl[l].items():
                m[f"{k}_{l}"] = v
        in_maps.append(m)
    res = run_bass_kernel_spmd(nc, in_maps, core_ids=list(range(2 * B)))
    out = np.empty((B, S, D), np.float32)
    for c in range(2 * B):
        b, half = c // 2, c % 2
        out[b, half * NW_FINAL:(half + 1) * NW_FINAL] = res.results[c]["y"]
    return out
```

```python
import numpy as np
import concourse.bass as bass
import concourse.mybir as mybir
from concourse.bass_utils import run_bass_kernel_spmd

F32 = mybir.dt.float32
BF16 = mybir.dt.bfloat16
U32 = mybir.dt.uint32
I32 = mybir.dt.int32
AF = mybir.ActivationFunctionType
ALU = mybir.AluOpType

ENGS = ("sync", "scalar", "vector", "gpsimd", "tensor")


class Buf:
    def __init__(self, name, t=None):
        self.name = name
        self.t = t
        self.writers = {}
        self.readers = {}
        self.dsem = None
        self.dval = 0


class Prog:
    def __init__(self, nc):
        self.nc = nc
        self.q = {e: [] for e in ENGS}
        self.sems = {}
        self.waited = {}
        self._dbufs = {}
        self._named = {}
        self._sets = {}
        self.cur = None
        self.use_semset(0)

    def use_semset(self, k):
        if self.cur is not None:
            self._sets[self.cur]["cnt"] = dict(self.ecnt)
        if k not in self._sets:
            keys = {e: ("e", e, k) for e in ENGS}
            self._sets[k] = {"keys": keys, "cnt": {e: 0 for e in ENGS}}
        self.cur = k
        self.ekey = dict(self._sets[k]["keys"])
        self.ecnt = dict(self._sets[k]["cnt"])

    def sb(self, name, shape, dtype):
        return Buf(name, self.nc.alloc_sbuf_tensor("s_" + name, list(shape), dtype))

    def ps(self, name, shape, dtype=F32):
        return Buf(name, self.nc.alloc_psum_tensor(name, list(shape), dtype))

    def dram(self, name):
        return Buf(name, None)

    def _dsem(self, buf):
        if buf.dsem is None:
            key = ("d", buf.name)
            if key in self._named:
                buf.dval = self._named[key].dval
            else:
                self.sems[key] = self.nc.alloc_semaphore("ds_" + buf.name)
            self._named[key] = buf
            self._dbufs[key] = buf
            buf.dsem = key
        return buf.dsem

    def _waits(self, eng, deps, skip_same_engine=False):
        for key, val in deps.items():
            if skip_same_engine and key[0] == "e" and key[1] == eng:
                continue
            if self.waited.get((eng, key), 0) >= val:
                continue
            self.waited[(eng, key)] = val
            sem = self.sems[key]
            self.q[eng].append(lambda e, sem=sem, val=val: e.wait_ge(sem, val))

    @staticmethod
    def _merge(dst, src):
        for k, v in src.items():
            if dst.get(k, 0) < v:
                dst[k] = v

    def op(self, eng, fn, reads=(), writes=()):
        deps = {}
        for b in reads:
            self._merge(deps, b.writers)
        for b in writes:
            self._merge(deps, b.writers)
            self._merge(deps, b.readers)
        self._waits(eng, deps, skip_same_engine=(eng == "tensor"))
        self.ecnt[eng] += 1
        val = self.ecnt[eng]
        key = self.ekey[eng]
        if key not in self.sems:
            self.sems[key] = self.nc.alloc_semaphore(f"es_{key[1]}_{key[2]}")
        sem = self.sems[key]
        self.q[eng].append(lambda e, fn=fn, sem=sem: fn(e).then_inc(sem, 1))
        for b in writes:
            b.writers = {key: val}
            b.readers = {}
        for b in reads:
            if b.readers.get(key, 0) < val:
                b.readers[key] = val

    def dma(self, eng, dst, src, fn, reads=(), accumulate=False):
        key = self._dsem(dst)
        deps = {}
        if src is not None:
            self._merge(deps, src.writers)
        for b in reads:
            self._merge(deps, b.writers)
        w = dict(dst.writers)
        if accumulate:
            w.pop(key, None)
        self._merge(deps, w)
        self._merge(deps, dst.readers)
        self._waits(eng, deps)
        dst.dval += 16
        val = dst.dval
        sem = self.sems[key]
        self.q[eng].append(lambda e, fn=fn, sem=sem: fn(e).then_inc(sem, 16))
        dst.writers = {key: val}
        dst.readers = {}
        for b in ([src] if src is not None else []) + list(reads):
            if b.readers.get(key, 0) < val:
                b.readers[key] = val

    def finish(self, out_bufs):
        deps = {}
        for b in out_bufs:
            self._merge(deps, b.writers)
        self._waits("sync", deps)
        nc = self.nc
        q = self.q
        with nc.Block() as block:
            @block.sync
            def _(e):
                for f in q["sync"]:
                    f(e)

            @block.scalar
            def _(e):
                for f in q["scalar"]:
                    f(e)

            @block.vector
            def _(e):
                for f in q["vector"]:
                    f(e)

            @block.gpsimd
            def _(e):
                for f in q["gpsimd"]:
                    f(e)

            @block.tensor
            def _(e):
                for f in q["tensor"]:
                    f(e)


    def barrier(self):
        self._sets[self.cur]["cnt"] = dict(self.ecnt)
        deps = {}
        for st in self._sets.values():
            for e in ENGS:
                if st["cnt"][e] > 0:
                    deps[st["keys"][e]] = st["cnt"][e]
        for key, b in self._dbufs.items():
            deps[key] = b.dval
        for e in ENGS:
            self._waits(e, deps)


D = 2048
HALO = 512
KC = 16
IN_WIDTH = 4608
ALPHA = float(8 ** 0.25)
LN_EPS = 1e-5
ATT_SCALE = float(128 ** -0.5)
NEG = -30000.0
POOL_W = (2, 4, 8, 16)
NUV = 8


_UNIQ = [0]


def _emit_layer(nc, P, ps, A, NW_L, NINV, consts, dbufs, semset0, conv):
    from contextlib import ExitStack
    HB = HALO // 512
    ident, ones_bf, ones_f, flags, invc, ident_bf = consts
    psrot = [0]

    def nps():
        psrot[0] = (psrot[0] + 1) % 8
        return ps[psrot[0]]

    def sbs(es, name, shape, dt):
        _UNIQ[0] += 1
        t = es.enter_context(nc.sbuf_tensor(f"s{_UNIQ[0]}_{name}", list(shape), dt))
        return Buf(name, t)

    def OP(eng, f, reads, writes):
        P.op(eng, f, reads=reads, writes=writes)

    def MM(pb, out_ap, lb, l_ap, rb, r_ap, start, stop):
        P.op("tensor", lambda e: e.matmul(out_ap, lhsT=l_ap, rhs=r_ap, start=start, stop=stop),
             reads=[lb, rb], writes=[pb])

    def TR(pb, out_ap, sb_, in_ap):
        P.op("tensor", lambda e: e.transpose(out=out_ap, in_=in_ap, identity=ident.t[:, :]),
             reads=[sb_, ident], writes=[pb])

    def ACT(ob, out_ap, ib, in_ap, func, bias=None, scale=None, reads=()):
        kw = {}
        if bias is not None:
            kw["bias"] = bias
        if scale is not None:
            kw["scale"] = scale
        P.op("scalar", lambda e: e.activation(out=out_ap, in_=in_ap, func=func, **kw),
             reads=[ib] + list(reads), writes=[ob])

    cstate = [conv]

    def conv_step(n):
        for _ in range(n):
            if cstate[0] is None:
                return
            try:
                next(cstate[0])
            except StopIteration:
                cstate[0] = None

    def DMA(q, dst, out_ap, src, in_ap, reads=(), acc=False):
        P.dma(q, dst, src, lambda e: e.dma_start(out=out_ap, in_=in_ap), reads=reads, accumulate=acc)
        if q == "gpsimd":
            if cdefer[0]:
                cpend[0] += 3
            else:
                conv_step(3)

    cdefer = [False]
    cpend = [0]

    def defer_on():
        cdefer[0] = True

    def defer_off():
        cdefer[0] = False
        conv_step(cpend[0])
        cpend[0] = 0

    def TT(ob, out_ap, ab, a_ap, bb, b_ap, op, eng="vector"):
        P.op(eng, lambda e: e.tensor_tensor(out=out_ap, in0=a_ap, in1=b_ap, op=op), reads=[ab, bb], writes=[ob])

    def TS(ob, out_ap, ib, in_ap, s1, op0, s2=None, op1=None, reads=(), eng="vector"):
        if op1 is None:
            P.op(eng, lambda e: e.tensor_scalar(out=out_ap, in0=in_ap, scalar1=s1, scalar2=None, op0=op0),
                 reads=[ib] + list(reads), writes=[ob])
        else:
            P.op(eng, lambda e: e.tensor_scalar(out=out_ap, in0=in_ap, scalar1=s1, scalar2=s2, op0=op0, op1=op1),
                 reads=[ib] + list(reads), writes=[ob])

    def STT(ob, out_ap, ab, a_ap, scalar, bb, b_ap, op0, op1, reads=(), accum=None, accb=None):
        kw = {}
        w = [ob]
        if accum is not None:
            kw["accum_out"] = accum
            w.append(accb)
        P.op("vector", lambda e: e.scalar_tensor_tensor(out=out_ap, in0=a_ap, scalar=scalar, in1=b_ap,
                                                        op0=op0, op1=op1, **kw),
             reads=[ab, bb] + list(reads), writes=w)

    d_cat, d_x1, d_x1T, d_r2, d_top, d_xin, d_y, d_tab = dbufs
    catT_r = A["catT"].rearrange("(kc p) t -> p kc t", p=128)
    x1T_r = A["x1T"].rearrange("(kc p) t -> p kc t", p=128)
    w_in_r = A["w_in"].rearrange("(kc p) n -> p kc n", p=128)

    def mixer_segment(o0, NW, ninv):
        NWIN = HALO + NW
        NT = NW // 128
        NTW = NWIN // 128
        NB = NW // 512
        NBW = NWIN // 512
        fpos = NINV - HALO - o0
        with ExitStack() as es1:
            xT = sbs(es1, "xT", [128, KC, NWIN], BF16)
            wch = [sbs(es1, f"wch{i}", [128, KC, 128], BF16) for i in range(2)]
            b_in_fm = sbs(es1, "b_in_fm", [128, 36], F32)
            yfm = [sbs(es1, f"yfm{i}", [128, NW], BF16) for i in range(2)]
            DMA("sync", b_in_fm, b_in_fm.t[:, :], None, A["b_in_fm"])
            with ExitStack() as es0:
                xin = [sbs(es0, f"xin{i}", [128, D], F32) for i in range(2)]
                for i in range(NTW):
                    xb = xin[i % 2]
                    DMA("sync", xb, xb.t[:, :], d_xin, A["x_win"][o0 + i * 128:o0 + (i + 1) * 128, :])
                    for g in range(4):
                        pb = nps()
                        for j in range(4):
                            kc = g * 4 + j
                            TR(pb, pb.t[:, j * 128:(j + 1) * 128], xb, xb.t[:, kc * 128:(kc + 1) * 128])
                        o_ap = xT.t[:, g * 4:(g + 1) * 4, i * 128:(i + 1) * 128]
                        i_ap = pb.t[:, :].rearrange("p (a b) -> p a b", a=4)
                        if g % 2 == 0:
                            ACT(xT, o_ap, pb, i_ap, AF.Identity)
                        else:
                            P.op("vector", lambda e, o_ap=o_ap, i_ap=i_ap: e.tensor_copy(out=o_ap, in_=i_ap),
                                 reads=[pb], writes=[xT])
                P.barrier()

            wrot = [0]

            def proj_fm(c, blk_lo, evac):
                wb = wch[wrot[0] % 2]
                wrot[0] += 1
                DMA("gpsimd", wb, wb.t[:, :, :], None, w_in_r[:, :, c * 128:(c + 1) * 128])
                for tb in range(blk_lo, NBW):
                    pb = nps()
                    for kc in range(KC):
                        MM(pb, pb.t[:, :], wb, wb.t[:, kc, :], xT, xT.t[:, kc, tb * 512:(tb + 1) * 512], kc == 0, kc == KC - 1)
                    evac(tb, pb)

            def evac_f32(dst, c):
                def f(tb, pb):
                    ACT(dst, dst.t[:, tb * 512:(tb + 1) * 512], pb, pb.t[:, :], AF.Identity,
                        bias=b_in_fm.t[:, c:c + 1], scale=1.0, reads=[b_in_fm])
                return f

            with ExitStack() as es:
                hA = sbs(es, "hA", [128, NWIN], F32)
                hB = sbs(es, "hB", [128, NWIN], F32)
                hC = sbs(es, "hC", [128, NWIN], F32)
                hD = sbs(es, "hD", [128, NWIN], F32)
                convo = sbs(es, "convo", [128, 4, NW], F32)
                pooled = sbs(es, "pooled", [128, NW], BF16)
                wpool = sbs(es, "wpool", [128, 4, 128], BF16)
                psc = sbs(es, "psc", [128, 4], F32)
                dww = sbs(es, "dww", [128, 4, 31], F32)
                dwb = sbs(es, "dwb", [128, 4], F32)
                cng = sbs(es, "cng", [128, 4], F32)
                cnb = sbs(es, "cnb", [128, 4], F32)
                t16 = sbs(es, "t16", [128, 16], F32)
                sqb = sbs(es, "sqb", [128, 512], F32)
                meanb = sbs(es, "meanb", [128, 512], F32)
                varb = sbs(es, "varb", [128, 512], F32)
                tnb = sbs(es, "tnb", [128, 512], F32)
                DMA("gpsimd", wpool, wpool.t[:, :, :], None, A["w_pool"].rearrange("g c d -> c g d"))
                DMA("sync", psc, psc.t[:, :], None, A["psc_fm"])
                DMA("sync", dww, dww.t[:, :, :], None, A["dww_fm"])
                DMA("sync", dwb, dwb.t[:, :], None, A["dwb_fm"])
                DMA("sync", cng, cng.t[:, :], None, A["cng_fm"])
                DMA("sync", cnb, cnb.t[:, :], None, A["cnb_fm"])
                for g in range(4):
                    proj_fm(g, 0, evac_f32(hA, g))
                    if ninv > 0:
                        TS(hA, hA.t[:, 0:ninv], hA, hA.t[:, 0:ninv], flags.t[:, 0:1], ALU.mult, reads=[flags])
                    src = hA
                    bufs = [hB, hC]
                    for si, st in enumerate((1, 2, 4, 8)[:g + 1]):
                        dst = bufs[si % 2]
                        TT(dst, dst.t[:, 16:NWIN], src, src.t[:, 16:NWIN], src, src.t[:, 16 - st:NWIN - st], ALU.add)
                        src = dst
                    STT(pooled, pooled.t[:, :], src, src.t[:, HALO:NWIN], 1.0 / POOL_W[g], hA, hA.t[:, HALO:NWIN],
                        ALU.mult, ALU.subtract)
                    if 0 <= fpos < NW:
                        TT(t16, t16.t[:, :], src, src.t[:, HALO + fpos:HALO + fpos + 16], invc, invc.t[:, g * 16:(g + 1) * 16], ALU.mult)
                        TT(pooled, pooled.t[:, fpos:fpos + 16], t16, t16.t[:, :], hA, hA.t[:, HALO + fpos:HALO + fpos + 16], ALU.subtract)
                    yb = yfm[g % 2]
                    for blk in range(NB):
                        pb = nps()
                        MM(pb, pb.t[:, :], wpool, wpool.t[:, g, :], pooled, pooled.t[:, blk * 512:(blk + 1) * 512], True, True)
                        ACT(yb, yb.t[:, blk * 512:(blk + 1) * 512], pb, pb.t[:, :], AF.Identity,
                            scale=psc.t[:, g:g + 1], reads=[psc])
                    DMA("sync", d_cat, A["catT"][g * 128:(g + 1) * 128, o0:o0 + NW], yb, yb.t[:, :], acc=True)
                hA0, hB0 = hA, hB
                for j in range(4):
                    hA = (hA0, hC)[j % 2]
                    hB = (hB0, hD)[j % 2]
                    proj_fm(4 + j, 0, evac_f32(hA, 4 + j))
                    proj_fm(8 + j, 0, evac_f32(hB, 8 + j))
                    ACT(hB, hB.t[:, :], hB, hB.t[:, :], AF.Sigmoid)
                    TT(hA, hA.t[:, :], hA, hA.t[:, :], hB, hB.t[:, :], ALU.mult)
                    if ninv > 0:
                        TS(hA, hA.t[:, 0:ninv], hA, hA.t[:, 0:ninv], flags.t[:, 0:1], ALU.mult, reads=[flags])
                    acc = convo.t[:, j, :]
                    TS(convo, acc, hA, hA.t[:, HALO - 30:HALO - 30 + NW], dww.t[:, j, 0:1], ALU.mult,
                       dwb.t[:, j:j + 1], ALU.add, reads=[dww, dwb])
                    for k in range(1, 31):
                        STT(convo, acc, hA, hA.t[:, HALO - 30 + k:HALO - 30 + k + NW], dww.t[:, j, k:k + 1],
                            convo, acc, ALU.mult, ALU.add, reads=[dww])
                for blk in range(NB):
                    sl = slice(blk * 512, (blk + 1) * 512)
                    pm = nps()
                    pq = nps()
                    for j in range(4):
                        ACT(sqb, sqb.t[:, :], convo, convo.t[:, j, sl], AF.Square)
                        MM(pm, pm.t[:, :], ones_f, ones_f.t[:, :], convo, convo.t[:, j, sl], j == 0, j == 3)
                        MM(pq, pq.t[:, :], ones_f, ones_f.t[:, :], sqb, sqb.t[:, :], j == 0, j == 3)
                    ACT(meanb, meanb.t[:, :], pm, pm.t[:, :], AF.Identity)
                    TT(varb, varb.t[:, :], meanb, meanb.t[:, :], meanb, meanb.t[:, :], ALU.mult)
                    TT(varb, varb.t[:, :], pq, pq.t[:, :], varb, varb.t[:, :], ALU.subtract)
                    TS(varb, varb.t[:, :], varb, varb.t[:, :], LN_EPS, ALU.add)
                    ACT(varb, varb.t[:, :], varb, varb.t[:, :], AF.Sqrt)
                    P.op("vector", lambda e: e.reciprocal(out=varb.t[:, :], in_=varb.t[:, :]), reads=[varb], writes=[varb])
                    for j in range(4):
                        yb = yfm[j % 2]
                        TT(tnb, tnb.t[:, :], convo, convo.t[:, j, sl], meanb, meanb.t[:, :], ALU.subtract)
                        TT(tnb, tnb.t[:, :], tnb, tnb.t[:, :], varb, varb.t[:, :], ALU.mult)
                        ACT(yb, yb.t[:, 0:512], tnb, tnb.t[:, :], AF.Silu, bias=cnb.t[:, j:j + 1], scale=cng.t[:, j:j + 1],
                            reads=[cnb, cng])
                        DMA("sync", d_cat, A["catT"][(4 + j) * 128:(5 + j) * 128, o0 + blk * 512:o0 + (blk + 1) * 512], yb, yb.t[:, 0:512], acc=True)
                P.barrier()

            with ExitStack() as es:
                biasT = sbs(es, "biasT", [128, 8, 5, 128], F32)
                bvb = sbs(es, "bvb", [128, 1024], F32)
                qT = sbs(es, "qT", [128, NW], BF16)
                kT = sbs(es, "kT", [128, NWIN], BF16)
                Vh = sbs(es, "Vh", [128, NTW, 128], BF16)
                tmpS = [sbs(es, f"tmpS{i}", [128, 5, 128], F32) for i in range(2)]
                PT = [sbs(es, f"PT{i}", [128, 5, 128], BF16) for i in range(2)]
                rden = sbs(es, "rden", [128, 128], F32)
                DMA("sync", biasT, biasT.t[:, :, :, :], None, A["abias"].rearrange("p (h k q) -> p h k q", h=8, k=5))
                DMA("sync", bvb, bvb.t[:, :], None, A["bv_row"].partition_broadcast(128))
                SA = [ps[0], ps[1]]
                SB = [ps[2], ps[3]]
                psO = [ps[4], ps[5]]
                psD = psO
                for h in range(8):
                    cq = 12 + h
                    ck = 20 + h

                    def evq(tb, pb, cq=cq):
                        TS(qT, qT.t[:, (tb - HB) * 512:(tb - HB + 1) * 512], pb, pb.t[:, :], b_in_fm.t[:, cq:cq + 1], ALU.add,
                           ATT_SCALE, ALU.mult, reads=[b_in_fm])

                    def evk(tb, pb, ck=ck):
                        ACT(kT, kT.t[:, tb * 512:(tb + 1) * 512], pb, pb.t[:, :], AF.Identity,
                            bias=b_in_fm.t[:, ck:ck + 1], scale=1.0, reads=[b_in_fm])

                    psrot[0] = 5
                    def nps_proj():
                        psrot[0] = 6 if psrot[0] != 6 else 7
                        return ps[psrot[0]]

                    for (c, lo, ev) in ((cq, HB, evq), (ck, 0, evk)):
                        wb = wch[wrot[0] % 2]
                        wrot[0] += 1
                        DMA("gpsimd", wb, wb.t[:, :, :], None, w_in_r[:, :, c * 128:(c + 1) * 128])
                        for tb in range(lo, NBW):
                            pb = nps_proj()
                            for kc in range(KC):
                                MM(pb, pb.t[:, :], wb, wb.t[:, kc, :], xT, xT.t[:, kc, tb * 512:(tb + 1) * 512], kc == 0, kc == KC - 1)
                            ev(tb, pb)
                    wb = wch[wrot[0] % 2]
                    wrot[0] += 1
                    cv = 28 + h
                    DMA("gpsimd", wb, wb.t[:, :, :], None, w_in_r[:, :, cv * 128:(cv + 1) * 128])
                    for tg in range(NTW // 4):
                        pb = nps_proj()
                        for j in range(4):
                            ti = tg * 4 + j
                            for kc in range(KC):
                                MM(pb, pb.t[:, j * 128:(j + 1) * 128], xT, xT.t[:, kc, ti * 128:(ti + 1) * 128], wb, wb.t[:, kc, :],
                                   kc == 0, kc == KC - 1)
                        TT(Vh, Vh.t[:, tg * 4:(tg + 1) * 4, :], pb, pb.t[:, :].rearrange("p (a b) -> p a b", a=4),
                           bvb, bvb.t[:, h * 128:(h + 1) * 128].unsqueeze(1).to_broadcast([128, 4, 128]), ALU.add)
                    ya = yfm[h % 2]

                    def stA(pi):
                        s_ = pi % 2
                        for kb in range(5):
                            wt = pi + kb
                            pb = SA[s_] if kb < 4 else SB[s_]
                            o_ap = pb.t[:, (kb % 4) * 128:(kb % 4 + 1) * 128]
                            MM(pb, o_ap, kT, kT.t[:, wt * 128:(wt + 1) * 128], qT, qT.t[:, pi * 128:(pi + 1) * 128], True, True)

                    def stB(pi):
                        s_ = pi % 2
                        for kb in range(5):
                            wt = pi + kb
                            pb = SA[s_] if kb < 4 else SB[s_]
                            i_ap = pb.t[:, (kb % 4) * 128:(kb % 4 + 1) * 128]
                            sc_ = flags.t[:, 1:2] if wt * 128 < ninv else 0.0
                            STT(tmpS[s_], tmpS[s_].t[:, kb, :], pb, i_ap, sc_, biasT, biasT.t[:, h, kb, :], ALU.add, ALU.add,
                                reads=[flags])
                        ACT(PT[s_], PT[s_].t[:, :, :], tmpS[s_], tmpS[s_].t[:, :, :], AF.Exp)

                    def stC(pi):
                        s_ = pi % 2
                        for kb in range(5):
                            wt = pi + kb
                            MM(psO[s_], psO[s_].t[:, 128:256], Vh, Vh.t[:, wt, :], PT[s_], PT[s_].t[:, kb, :], kb == 0, kb == 4)
                        for kb in range(5):
                            MM(psD[s_], psD[s_].t[:, 256:384], ones_bf, ones_bf.t[:, :], PT[s_], PT[s_].t[:, kb, :], kb == 0, kb == 4)

                    def stE(pi):
                        s_ = pi % 2
                        TS(rden, rden.t[:, :], psD[s_], psD[s_].t[:, 256:384], 1e-30, ALU.add)
                        P.op("vector", lambda e: e.reciprocal(out=rden.t[:, :], in_=rden.t[:, :]), reads=[rden], writes=[rden])
                        TT(ya, ya.t[:, pi * 128:(pi + 1) * 128], psO[s_], psO[s_].t[:, 128:256], rden, rden.t[:, :], ALU.mult)

                    for t in range(NT + 3):
                        if t < NT:
                            stA(t)
                        if 0 <= t - 1 < NT:
                            stB(t - 1)
                        if 0 <= t - 2 < NT:
                            stC(t - 2)
                        if 0 <= t - 3 < NT:
                            stE(t - 3)
                    DMA("sync", d_cat, A["catT"][(8 + h) * 128:(9 + h) * 128, o0:o0 + NW], ya, ya.t[:, :], acc=True)
                P.barrier()


    NW = NW_L
    NT = NW // 128
    NB = NW // 512
    segs = []
    o = 0
    while o < NW_L:
        n = min(2048 if (NW_L - o) != 2560 else 1536, NW_L - o)
        if NW_L - o > 2048 and NW_L - o - n < 1024:
            n = NW_L - o - 1024
        segs.append((o, n))
        o += n
    for (o0, n) in segs:
        P.use_semset(semset0)
        mixer_segment(o0, n, max(0, min(NINV - o0, HALO + n)))

    P.use_semset(5)
    with ExitStack() as es:
        wout = sbs(es, "wout", [128, KC, D], BF16)
        g1b = sbs(es, "g1b", [128, D], F32)
        b1b = sbs(es, "b1b", [128, D], F32)
        catb = [sbs(es, f"catb{i}", [128, KC, 512], BF16) for i in range(2)]
        xin = [sbs(es, f"x2in{i}", [128, D], F32) for i in range(2)]
        rb = sbs(es, "rb", [128, D], F32)
        x1b = [sbs(es, f"x1b{i}", [128, D], F32) for i in range(2)]
        x1Tt = [sbs(es, f"x1Tt{i}", [128, KC, 128], BF16) for i in range(2)]
        stats = sbs(es, "stats", [128, 4, 6], F32)
        mv = sbs(es, "mv", [128, 2], F32)
        rstd = sbs(es, "rstd", [128, 1], F32)
        defer_on()
        for kc in range(KC):
            DMA("gpsimd", wout, wout.t[:, kc, :], None, A["w_out"][kc * 128:(kc + 1) * 128, :], acc=True)
        defer_off()
        DMA("sync", g1b, g1b.t[:, :], None, A["ln1g"].partition_broadcast(128))
        DMA("sync", b1b, b1b.t[:, :], None, A["ln1b"].partition_broadcast(128))
        def p2_mm(i):
            tb, tj = i // 4, i % 4
            cb = catb[tb % 2]
            if tj == 0:
                DMA("sync", cb, cb.t[:, :, :], d_cat, catT_r[:, :, tb * 512:(tb + 1) * 512])
            xb = xin[i % 2]
            DMA("sync", xb, xb.t[:, :], d_xin, A["x_win"][HALO + i * 128:HALO + (i + 1) * 128, :])
            for nb in range(4):
                pb = ps[nb]
                for kc in range(KC):
                    MM(pb, pb.t[:, :], cb, cb.t[:, kc, tj * 128:(tj + 1) * 128], wout, wout.t[:, kc, nb * 512:(nb + 1) * 512],
                       kc == 0, kc == KC - 1)

        def p2_evac(i):
            xb = xin[i % 2]
            for nb in range(4):
                pb = ps[nb]
                STT(rb, rb.t[:, nb * 512:(nb + 1) * 512], xb, xb.t[:, nb * 512:(nb + 1) * 512], ALPHA, pb, pb.t[:, :],
                    ALU.mult, ALU.add)

        def p2_ln(i):
            x1 = x1b[i % 2]
            _layernorm(P, rb, x1, g1b, b1b, stats, mv, rstd)
            DMA("sync", d_x1, A["x1"][i * 128:(i + 1) * 128, :], x1, x1.t[:, :], acc=True)

        def p2_tr(i):
            x1 = x1b[i % 2]
            xt = x1Tt[i % 2]
            for g in range(4):
                pb = ps[4 + g]
                for j in range(4):
                    kc = g * 4 + j
                    TR(pb, pb.t[:, j * 128:(j + 1) * 128], x1, x1.t[:, kc * 128:(kc + 1) * 128])
                o_ap = xt.t[:, g * 4:(g + 1) * 4, :]
                i_ap = pb.t[:, :].rearrange("p (a b) -> p a b", a=4)
                ACT(xt, o_ap, pb, i_ap, AF.Identity)
            DMA("sync", d_x1T, x1T_r[:, :, i * 128:(i + 1) * 128], xt, xt.t[:, :, :], acc=True)

        p2_mm(0)
        p2_evac(0)
        for i in range(NT):
            if i + 1 < NT:
                p2_mm(i + 1)
            p2_ln(i)
            p2_tr(i)
            if i + 1 < NT:
                p2_evac(i + 1)
        P.barrier()

    P.use_semset(5)
    if True:
        with ExitStack() as es:
            topSb = [sbs(es, f"topSb{i}", [128, 4, 16, 16], F32) for i in range(2)]
            topIb = [sbs(es, f"topIb{i}", [128, 4, 16, 16], U32) for i in range(2)]
            wq = sbs(es, "wq", [128, KC, D], BF16)
            skf = sbs(es, "skf", [128, 2, 128], F32)
            skT = sbs(es, "skT", [128, 2, 128], BF16)
            x1Tb = [sbs(es, f"x1Tb{i}", [128, KC, 512], BF16) for i in range(2)]
            qTc = [sbs(es, f"qTc{i}", [128, 512], BF16) for i in range(2)]
            scb = [sbs(es, f"scb{i}", [128, 4, 128], F32) for i in range(2)]
            scr = sbs(es, "scr", [128, 128], F32)
            defer_on()
            for kc in range(KC):
                DMA("gpsimd", wq, wq.t[:, kc, :], None, A["w_q"][kc * 128:(kc + 1) * 128, :], acc=True)
            defer_off()
            DMA("sync", skf, skf.t[:, :, :], None, A["sub_keys"].rearrange("p k c -> k p c"))
            for p_ in range(2):
                pb = nps()
                TR(pb, pb.t[:, 0:128], skf, skf.t[:, p_, :])
                ACT(skT, skT.t[:, p_, :], pb, pb.t[:, 0:128], AF.Identity)
            scr4 = [sbs(es, f"scr4_{i}", [128, 128], F32) for i in range(4)]
            tS_al = [[Buf(f"tSal{k}_{tj}", topSb[k].t) for tj in range(4)] for k in range(2)]
            tI_al = [[Buf(f"tIal{k}_{tj}", topIb[k].t) for tj in range(4)] for k in range(2)]

            def ld_blk(tb):
                xb_ = x1Tb[tb % 2]
                DMA("sync", xb_, xb_.t[:, :, :], d_x1T, x1T_r[:, :, tb * 512:(tb + 1) * 512])

            def st_M(c):
                tb, hp = c // 16, c % 16
                xb_ = x1Tb[tb % 2]
                pb = ps[c % 4]
                for kc in range(KC):
                    MM(pb, pb.t[:, :], wq, wq.t[:, kc, hp * 128:(hp + 1) * 128], xb_, xb_.t[:, kc, :], kc == 0, kc == KC - 1)
                qc = qTc[c % 2]
                ACT(qc, qc.t[:, :], pb, pb.t[:, :], AF.Identity)

            def st_S(c):
                hp = c % 16
                qc = qTc[c % 2]
                pb2 = ps[4 + c % 4]
                for tj in range(4):
                    MM(pb2, pb2.t[:, tj * 128:(tj + 1) * 128], qc, qc.t[:, tj * 128:(tj + 1) * 128], skT, skT.t[:, hp % 2, :], True, True)
                sc = scb[c % 2]
                ACT(sc, sc.t[:, :, :], pb2, pb2.t[:, :].rearrange("p (a b) -> p a b", a=4), AF.Identity)

            def st_T(c):
                tb, hp = c // 16, c % 16
                k = tb % 2
                topS, topI = topSb[k], topIb[k]
                sc = scb[c % 2]
                vs = [sc.t[:, tj, :] for tj in range(4)]
                sas = [topS.t[:, tj, hp, 0:8] for tj in range(4)]
                sbs_ = [topS.t[:, tj, hp, 8:16] for tj in range(4)]
                ias = [topI.t[:, tj, hp, 0:8] for tj in range(4)]
                ibs = [topI.t[:, tj, hp, 8:16] for tj in range(4)]
                for tj in range(4):
                    P.op("vector", lambda e, v_=vs[tj], sa=sas[tj]: e.max(out=sa, in_=v_), reads=[sc], writes=[tS_al[k][tj]])
                for tj in range(4):
                    P.op("vector", lambda e, v_=vs[tj], sa=sas[tj], sr=scr4[tj]: e.match_replace(out=sr.t[:, :], in_to_replace=sa, in_values=v_, imm_value=-1e30),
                         reads=[sc, tS_al[k][tj]], writes=[scr4[tj]])
                for tj in range(4):
                    P.op("vector", lambda e, sb_=sbs_[tj], sr=scr4[tj]: e.max(out=sb_, in_=sr.t[:, :]), reads=[scr4[tj]], writes=[tS_al[k][tj]])
                for tj in range(4):
                    P.op("vector", lambda e, v_=vs[tj], sa=sas[tj], ia=ias[tj]: e.max_index(out=ia, in_max=sa, in_values=v_),
                         reads=[sc, tS_al[k][tj]], writes=[tI_al[k][tj]])
                for tj in range(4):
                    P.op("vector", lambda e, v_=vs[tj], sb_=sbs_[tj], ib=ibs[tj]: e.max_index(out=ib, in_max=sb_, in_values=v_),
                         reads=[sc, tS_al[k][tj]], writes=[tI_al[k][tj]])
                if hp == 15:
                    DMA("sync", d_top, A["topS"][tb * 512:(tb + 1) * 512, :].rearrange("(a p) n -> p a n", p=128), None,
                        topS.t[:, :, :, :].rearrange("p a h r -> p a (h r)"), reads=tS_al[k], acc=True)
                    DMA("sync", d_top, A["topI"][tb * 512:(tb + 1) * 512, :].rearrange("(a p) n -> p a n", p=128), None,
                        topI.t[:, :, :, :].rearrange("p a h r -> p a (h r)"), reads=tI_al[k], acc=True)

            NC3 = NB * 16
            ld_blk(0)
            for c in range(NC3 + 2):
                if c < NC3:
                    if c % 16 == 0 and c // 16 + 1 < NB:
                        ld_blk(c // 16 + 1)
                    st_M(c)
                if 0 <= c - 1 < NC3:
                    st_S(c - 1)
                if 0 <= c - 2 < NC3:
                    st_T(c - 2)
            P.barrier()
        P.use_semset(5)
        with ExitStack() as es:
            wpg = sbs(es, "wpg", [128, KC, D], BF16)
            wple = sbs(es, "wple", [128, 2, D], BF16)
            x1Tt = [sbs(es, f"x1Tu{i}", [128, KC, 128], BF16) for i in range(2)]
            pin = [sbs(es, f"pin{i}", [128, 256], F32) for i in range(2)]
            pT = [sbs(es, f"pT{i}", [128, 2, 128], BF16) for i in range(2)]
            x1t = [sbs(es, f"x1t{i}", [128, D], F32) for i in range(2)]
            sgb = [sbs(es, f"sgb{i}", [128, 512], F32) for i in range(2)]
            r2b = [sbs(es, f"r2b{i}", [128, D], F32) for i in range(2)]
            defer_on()
            for kc in range(KC):
                DMA("gpsimd", wpg, wpg.t[:, kc, :], None, A["w_pg"][kc * 128:(kc + 1) * 128, :], acc=True)
            for k2 in range(2):
                DMA("gpsimd", wple, wple.t[:, k2, :], None, A["w_ple"][k2 * 128:(k2 + 1) * 128, :], acc=True)
            defer_off()
            def ld3b(i):
                xt_ = x1Tt[i % 2]
                DMA("sync", xt_, xt_.t[:, :, :], d_x1T, x1T_r[:, :, i * 128:(i + 1) * 128])
                DMA("sync", pin[i % 2], pin[i % 2].t[:, :], None, A["p_own"][i * 128:(i + 1) * 128, :])
                DMA("sync", x1t[i % 2], x1t[i % 2].t[:, :], d_x1, A["x1"][i * 128:(i + 1) * 128, :])

            ld3b(0)
            for i in range(NT):
                xt = x1Tt[i % 2]
                pi_ = pin[i % 2]
                x1 = x1t[i % 2]
                if i + 1 < NT:
                    ld3b(i + 1)
                pb = ps[6 + i % 2]
                for k2 in range(2):
                    TR(pb, pb.t[:, k2 * 128:(k2 + 1) * 128], pi_, pi_.t[:, k2 * 128:(k2 + 1) * 128])
                pt = pT[i % 2]
                ACT(pt, pt.t[:, :, :], pb, pb.t[:, 0:256].rearrange("p (a b) -> p a b", a=2), AF.Identity)
                r2 = r2b[i % 2]
                for nb in range(4):
                    sl = slice(nb * 512, (nb + 1) * 512)
                    pg = ps[nb % 2]
                    for kc in range(KC):
                        MM(pg, pg.t[:, :], xt, xt.t[:, kc, :], wpg, wpg.t[:, kc, sl], kc == 0, kc == KC - 1)
                    sg = sgb[nb % 2]
                    ACT(sg, sg.t[:, :], pg, pg.t[:, :], AF.Sigmoid)
                    pp = ps[2 + nb % 2]
                    for k2 in range(2):
                        MM(pp, pp.t[:, :], pt, pt.t[:, k2, :], wple, wple.t[:, k2, sl], k2 == 0, k2 == 1)
                    TT(sg, sg.t[:, :], pp, pp.t[:, :], sg, sg.t[:, :], ALU.mult)
                    STT(r2, r2.t[:, sl], x1, x1.t[:, sl], ALPHA, sg, sg.t[:, :], ALU.mult, ALU.add)
                DMA("sync", d_r2, A["r2"][i * 128:(i + 1) * 128, :], r2, r2.t[:, :], acc=True)
            P.barrier()

        conv_step(10 ** 6)
        with ExitStack() as es:
            topSt = [sbs(es, f"topSt{i}", [128, 16, 16], F32) for i in range(2)]
            topIt = [sbs(es, f"topIt{i}", [128, 16, 16], U32) for i in range(2)]
            g2b = sbs(es, "g2b", [128, D], F32)
            b2b = sbs(es, "b2b", [128, D], F32)
            uvb = [sbs(es, f"uvb{i}", [128, 2 * D], BF16) for i in range(NUV)]
            dg = [sbs(es, f"dg{i}", [128, 128], BF16) for i in range(4)]
            zc = [sbs(es, f"zc{i}", [128, 1], F32) for i in range(4)]
            gelc = [sbs(es, f"gelc{i}", [128, 1], F32) for i in range(4)]
            x1t = [sbs(es, f"x4t{i}", [128, D], F32) for i in range(2)]
            yac = [sbs(es, f"yac{i}", [128, D], F32) for i in range(3)]
            outb = [sbs(es, f"outb{i}", [128, D], F32) for i in range(2)]
            junk = sbs(es, "junk", [128, D], BF16)
            topIf = sbs(es, "topIf", [128, 16, 16], F32)
            cs2 = [sbs(es, f"cs2_{i}", [128, 256], F32) for i in range(2)]
            tS = sbs(es, "tS", [128, 8, 16], F32)
            scr2 = sbs(es, "scr2", [128, 256], F32)
            posu = sbs(es, "posu", [128, 8, 16], U32)
            pa_u = sbs(es, "pa_u", [128, 8, 16], U32)
            pb_u = sbs(es, "pb_u", [128, 8, 16], U32)
            pa_f = sbs(es, "pa_f", [128, 8, 16], F32)
            pb_f = sbs(es, "pb_f", [128, 8, 16], F32)
            oh = sbs(es, "oh", [128, 128, 16], F32)
            iota_rep = sbs(es, "iota_rep", [128, 128, 16], F32)
            sel1 = sbs(es, "sel1", [128, 128], F32)
            sel2 = sbs(es, "sel2", [128, 128], F32)
            acol = [sbs(es, f"acol{i}", [128, 1], F32) for i in range(4)]
            cint = sbs(es, "cint", [128, 2], U32)
            eidf = sbs(es, "eidf", [128, 128], F32)
            eidx = [sbs(es, f"eidx{i}", [128, 128], U32) for i in range(2)]
            nmax = sbs(es, "nmax", [128, 8], F32)
            ge = sbs(es, "ge", [128, 8, 16], F32)
            gsum = sbs(es, "gsum", [128, 8], F32)
            gates = [sbs(es, f"gate{i}", [128, 128], F32) for i in range(2)]
            stats = sbs(es, "stats4", [128, 4, 6], F32)
            mv = sbs(es, "mv4", [128, 2], F32)
            rstd = sbs(es, "rstd4", [128, 1], F32)
            DMA("sync", g2b, g2b.t[:, :], None, A["ln2g"].partition_broadcast(128))
            DMA("sync", b2b, b2b.t[:, :], None, A["ln2b"].partition_broadcast(128))
            P.op("gpsimd", lambda e: e.iota(iota_rep.t[:, :, :], pattern=[[0, 128], [1, 16]], base=0, channel_multiplier=0,
                                            allow_small_or_imprecise_dtypes=True), writes=[iota_rep])
            P.op("vector", lambda e: e.memset(cint.t[:, 0:1], 4), writes=[cint])
            P.op("vector", lambda e: e.memset(cint.t[:, 1:2], 15), writes=[cint])
            gcnt = [0]

            def idx_gen(i):
                tSt = topSt[i % 2]
                tIt = topIt[i % 2]
                ex = eidx[i % 2]
                gate = gates[i % 2]
                DMA("sync", tSt, tSt.t[:, :, :].rearrange("p h r -> p (h r)"), d_top, A["topS"][i * 128:(i + 1) * 128, :])
                DMA("sync", tIt, tIt.t[:, :, :].rearrange("p h r -> p (h r)"), d_top, A["topI"][i * 128:(i + 1) * 128, :])
                yield
                P.op("vector", lambda e: e.tensor_copy(out=topIf.t[:, :, :], in_=tIt.t[:, :, :]), reads=[tIt], writes=[topIf])
                yield
                for h in range(8):
                    s1 = tSt.t[:, 2 * h, :].unsqueeze(2).to_broadcast([128, 16, 16])
                    s2 = tSt.t[:, 2 * h + 1, :].unsqueeze(1).to_broadcast([128, 16, 16])
                    csb = cs2[h % 2]
                    csh = csb.t[:, :].rearrange("p (a b) -> p a b", a=16)
                    TT(csb, csh, tSt, s1, tSt, s2, ALU.add)
                    yield
                    P.op("vector", lambda e, h=h, csb=csb: e.max(out=tS.t[:, h, 0:8], in_=csb.t[:, :]), reads=[csb], writes=[tS])
                    yield
                    P.op("vector", lambda e, h=h, csb=csb: e.match_replace(out=scr2.t[:, :], in_to_replace=tS.t[:, h, 0:8], in_values=csb.t[:, :], imm_value=-1e30),
                         reads=[csb, tS], writes=[scr2])
                    yield
                    P.op("vector", lambda e, h=h: e.max(out=tS.t[:, h, 8:16], in_=scr2.t[:, :]), reads=[scr2], writes=[tS])
                    yield
                    P.op("vector", lambda e, h=h, csb=csb: e.max_index(out=posu.t[:, h, 0:8], in_max=tS.t[:, h, 0:8], in_values=csb.t[:, :]),
                         reads=[csb, tS], writes=[posu])
                    yield
                    P.op("vector", lambda e, h=h, csb=csb: e.max_index(out=posu.t[:, h, 8:16], in_max=tS.t[:, h, 8:16], in_values=csb.t[:, :]),
                         reads=[csb, tS], writes=[posu])
                    yield
                P.op("vector", lambda e: e.tensor_scalar(out=pa_u.t[:, :, :], in0=posu.t[:, :, :], scalar1=cint.t[:, 0:1], scalar2=None,
                                                         op0=ALU.logical_shift_right), reads=[posu, cint], writes=[pa_u])
                yield
                P.op("vector", lambda e: e.tensor_scalar(out=pb_u.t[:, :, :], in0=posu.t[:, :, :], scalar1=cint.t[:, 1:2], scalar2=None,
                                                         op0=ALU.bitwise_and), reads=[posu, cint], writes=[pb_u])
                yield
                P.op("vector", lambda e: e.tensor_copy(out=pa_f.t[:, :, :], in_=pa_u.t[:, :, :]), reads=[pa_u], writes=[pa_f])
                yield
                P.op("vector", lambda e: e.tensor_copy(out=pb_f.t[:, :, :], in_=pb_u.t[:, :, :]), reads=[pb_u], writes=[pb_f])
                yield
                tI4 = topIf.t[:, :, :].rearrange("p (h two) r -> p h two r", two=2)
                for (pf, which, dst) in ((pa_f, 0, sel1), (pb_f, 1, sel2)):
                    TT(oh, oh.t[:, :, :], pf, pf.t[:, :, :].rearrange("p h r -> p (h r)").unsqueeze(2).to_broadcast([128, 128, 16]),
                       iota_rep, iota_rep.t[:, :, :], ALU.is_equal)
                    yield
                    TT(oh, oh.t[:, :, :].rearrange("p (h r) a -> p h r a", h=8), oh, oh.t[:, :, :].rearrange("p (h r) a -> p h r a", h=8),
                       topIf, tI4[:, :, which, :].unsqueeze(2).to_broadcast([128, 8, 16, 16]), ALU.mult)
                    yield
                    P.op("vector", lambda e, dst=dst: e.tensor_reduce(out=dst.t[:, :], in_=oh.t[:, :, :], axis=mybir.AxisListType.X, op=ALU.add),
                         reads=[oh], writes=[dst])
                    yield
                STT(eidf, eidf.t[:, :], sel1, sel1.t[:, :], 128.0, sel2, sel2.t[:, :], ALU.mult, ALU.add)
                yield
                TS(eidf, eidf.t[:, :], eidf, eidf.t[:, :], 16383.0, ALU.min, 0.0, ALU.max)
                yield
                if (i + 1) * 128 <= NINV - HALO:
                    TS(eidf, eidf.t[:, :], eidf, eidf.t[:, :], flags.t[:, 0:1], ALU.mult, reads=[flags])
                    yield
                P.op("vector", lambda e: e.tensor_copy(out=ex.t[:, :], in_=eidf.t[:, :]), reads=[eidf], writes=[ex])
                yield
                TS(nmax, nmax.t[:, :], tS, tS.t[:, :, 0], -1.0, ALU.mult)
                yield
                for h in range(8):
                    ACT(ge, ge.t[:, h, :], tS, tS.t[:, h, :], AF.Exp, bias=nmax.t[:, h:h + 1], scale=1.0, reads=[nmax])
                yield
                P.op("vector", lambda e: e.tensor_reduce(out=gsum.t[:, :], in_=ge.t[:, :, :], axis=mybir.AxisListType.X, op=ALU.add),
                     reads=[ge], writes=[gsum])
                yield
                P.op("vector", lambda e: e.reciprocal(out=gsum.t[:, :], in_=gsum.t[:, :]), reads=[gsum], writes=[gsum])
                yield
                TT(gate, gate.t[:, :].rearrange("p (h r) -> p h r", h=8), ge, ge.t[:, :, :],
                   gsum, gsum.t[:, :].unsqueeze(2).to_broadcast([128, 8, 16]), ALU.mult)
                yield

            def finish_gen(k):
                ya = yac[k % 3]
                psb = ps[(k % 2) * 4:(k % 2) * 4 + 4]
                for nb in range(4):
                    sl = slice(nb * 512, (nb + 1) * 512)
                    TT(ya, ya.t[:, sl], psb[nb], psb[nb].t[:, :], ya, ya.t[:, sl], ALU.add)
                    yield
                ob = outb[k % 2]
                for k4 in range(4):
                    P.op("vector", lambda e, k4=k4: e.bn_stats(out=stats.t[:, k4, :], in_=ya.t[:, k4 * 512:(k4 + 1) * 512]),
                         reads=[ya], writes=[stats])
                    yield
                P.op("vector", lambda e: e.bn_aggr(out=mv.t[:, :], in_=stats.t[:, :, :].rearrange("p a b -> p (a b)")), reads=[stats], writes=[mv])
                yield
                P.op("vector", lambda e: e.tensor_scalar(out=rstd.t[:, :], in0=mv.t[:, 1:2], scalar1=LN_EPS, scalar2=None, op0=ALU.add),
                     reads=[mv], writes=[rstd])
                P.op("scalar", lambda e: e.activation(out=rstd.t[:, :], in_=rstd.t[:, :], func=AF.Sqrt), reads=[rstd], writes=[rstd])
                yield
                P.op("vector", lambda e: e.reciprocal(out=rstd.t[:, :], in_=rstd.t[:, :]), reads=[rstd], writes=[rstd])
                yield
                P.op("vector", lambda e: e.tensor_scalar(out=ob.t[:, :], in0=ya.t[:, :], scalar1=mv.t[:, 0:1], scalar2=rstd.t[:, 0:1],
                                                         op0=ALU.subtract, op1=ALU.mult), reads=[ya, mv, rstd], writes=[ob])
                yield
                P.op("vector", lambda e: e.tensor_tensor(out=ob.t[:, :], in0=ob.t[:, :], in1=g2b.t[:, :], op=ALU.mult), reads=[ob, g2b], writes=[ob])
                yield
                P.op("vector", lambda e: e.tensor_tensor(out=ob.t[:, :], in0=ob.t[:, :], in1=b2b.t[:, :], op=ALU.add), reads=[ob, b2b], writes=[ob])
                DMA("sync", d_y, A["y"][k * 128:(k + 1) * 128, :], ob, ob.t[:, :], acc=True)
                yield

            def tile_loads(i):
                x1 = x1t[i % 2]
                ya = yac[i % 3]
                DMA("sync", x1, x1.t[:, :], d_x1, A["x1"][i * 128:(i + 1) * 128, :])
                DMA("sync", ya, ya.t[:, :], d_r2, A["r2"][i * 128:(i + 1) * 128, :])

            def step(g, n):
                if g is None:
                    return None
                for _ in range(n):
                    try:
                        next(g)
                    except StopIteration:
                        return None
                return g

            def drain(g):
                while g is not None:
                    g = step(g, 1)

            P.use_semset(6)
            drain(idx_gen(0))
            tile_loads(0)
            for i in range(NT):
                P.use_semset(6 + i % 2)
                x1 = x1t[i % 2]
                ex = eidx[i % 2]
                gate = gates[i % 2]
                psb = ps[(i % 2) * 4:(i % 2) * 4 + 4]
                gi = idx_gen(i + 1) if i + 1 < NT else None
                gf = finish_gen(i - 1) if i >= 1 else None
                if i + 1 < NT:
                    tile_loads(i + 1)
                bufs = {}

                def consume(j):
                    buf = bufs.pop(j)
                    dj = dg[j % 4]
                    ac = acol[j % 4]
                    ACT(ac, ac.t[:, 0:1], gelc[j % 4], gelc[j % 4].t[:, 0:1], AF.Identity, scale=gate.t[:, j:j + 1], reads=[gate])
                    ACT(dj, dj.t[:, :], ident_bf, ident_bf.t[:, :], AF.Identity, scale=ac.t[:, 0:1], reads=[ac])
                    for nb in range(4):
                        MM(psb[nb], psb[nb].t[:, :], dj, dj.t[:, :], buf, buf.t[:, D + nb * 512:D + (nb + 1) * 512], j == 0, j == 127)

                for j in range(128):
                    buf = uvb[gcnt[0] % NUV]
                    gcnt[0] += 1
                    bufs[j] = buf
                    P.dma("gpsimd", buf, d_tab, lambda e, buf=buf, ex=ex, j=j: e.indirect_dma_start(
                        out=buf.t[:, :], out_offset=None, in_=A["uv_bf"],
                        in_offset=bass.IndirectOffsetOnAxis(ap=ex.t[:, j:j + 1], axis=0)), reads=[ex])
                    zj = zc[j % 4]
                    STT(junk, junk.t[:, :], buf, buf.t[:, 0:D], 1.0, x1, x1.t[:, :], ALU.mult, ALU.mult,
                        accum=zj.t[:, 0:1], accb=zj)
                    ACT(gelc[j % 4], gelc[j % 4].t[:, 0:1], zj, zj.t[:, 0:1], AF.Gelu)
                    if j >= 2:
                        consume(j - 2)
                    gi = step(gi, 2)
                    gf = step(gf, 1)
                consume(126)
                consume(127)
                drain(gf)
                drain(gi)
            drain(finish_gen(NT - 1))
            P.barrier()


def _layernorm(P, rb, ob, gb, bb, stats, mv, rstd):
    for k in range(4):
        P.op("vector", lambda e, k=k: e.bn_stats(out=stats.t[:, k, :], in_=rb.t[:, k * 512:(k + 1) * 512]),
             reads=[rb], writes=[stats])
    P.op("vector", lambda e: e.bn_aggr(out=mv.t[:, :], in_=stats.t[:, :, :].rearrange("p a b -> p (a b)")), reads=[stats], writes=[mv])
    P.op("vector", lambda e: e.tensor_scalar(out=rstd.t[:, :], in0=mv.t[:, 1:2], scalar1=LN_EPS, scalar2=None, op0=ALU.add),
         reads=[mv], writes=[rstd])
    P.op("scalar", lambda e: e.activation(out=rstd.t[:, :], in_=rstd.t[:, :], func=AF.Sqrt), reads=[rstd], writes=[rstd])
    P.op("vector", lambda e: e.reciprocal(out=rstd.t[:, :], in_=rstd.t[:, :]), reads=[rstd], writes=[rstd])
    P.op("vector", lambda e: e.tensor_scalar(out=ob.t[:, :], in0=rb.t[:, :], scalar1=mv.t[:, 0:1], scalar2=rstd.t[:, 0:1],
                                             op0=ALU.subtract, op1=ALU.mult), reads=[rb, mv, rstd], writes=[ob])
    P.op("vector", lambda e: e.tensor_tensor(out=ob.t[:, :], in0=ob.t[:, :], in1=gb.t[:, :], op=ALU.mult), reads=[ob, gb], writes=[ob])
    P.op("vector", lambda e: e.tensor_tensor(out=ob.t[:, :], in0=ob.t[:, :], in1=bb.t[:, :], op=ALU.add), reads=[ob, bb], writes=[ob])


_W_SPECS = [
    ("w_in", [D, IN_WIDTH]), ("b_in_fm", [128, 36]), ("bv_row", [1, 1024]), ("w_pool", [4, 128, 128]),
    ("psc_fm", [128, 4]), ("dww_fm", [128, 4, 31]), ("dwb_fm", [128, 4]), ("cng_fm", [128, 4]), ("cnb_fm", [128, 4]),
    ("abias", [128, 8 * 5 * 128]), ("w_out", [D, D]), ("ln1g", [1, D]), ("ln1b", [1, D]), ("w_q", [D, D]),
    ("sub_keys", [2, 128, 128]), ("u_tab", [16384, D]), ("v_tab", [16384, D]), ("w_ple", [256, D]), ("w_pg", [D, D]),
    ("ln2g", [1, D]), ("ln2b", [1, D]),
]


DEPTH = 4
NW_FINAL = 2048


def _layer_dims(l):
    nw = NW_FINAL + HALO * (DEPTH - 1 - l)
    ninv = HALO * (DEPTH - l)
    return nw, ninv


def build_program():
    nc = bass.Bass("TRN2", target_bir_lowering=False)
    NWMAX = _layer_dims(0)[0]
    x0 = nc.dram_tensor("x_win0", [NWMAX + HALO, D], F32, kind="ExternalInput").ap()
    flags_d = nc.dram_tensor("flags", [1, 2], F32, kind="ExternalInput").ap()
    invc_d = nc.dram_tensor("invc", [1, 64], F32, kind="ExternalInput").ap()
    ident_d = nc.dram_tensor("ident", [128, 128], F32, kind="ExternalInput").ap()
    y_out = nc.dram_tensor("y", [NW_FINAL, D], F32, kind="ExternalOutput").ap()
    scr = {
        "catT": nc.dram_tensor("catT", [D, NWMAX], BF16, kind="Internal").ap(),
        "x1T": nc.dram_tensor("x1T", [D, NWMAX], BF16, kind="Internal").ap(),
        "x1": nc.dram_tensor("x1", [NWMAX, D], F32, kind="Internal").ap(),
        "r2": nc.dram_tensor("r2", [NWMAX, D], F32, kind="Internal").ap(),
        "topS": nc.dram_tensor("topS", [NWMAX, 256], F32, kind="Internal").ap(),
        "topI": nc.dram_tensor("topI", [NWMAX, 256], U32, kind="Internal").ap(),
    }
    ybuf = [nc.dram_tensor(f"ybuf{i}", [NWMAX, D], F32, kind="Internal").ap() for i in range(2)]
    P = Prog(nc)
    ps = [P.ps(f"ps{i}", [128, 512], F32) for i in range(8)]
    ident = P.sb("ident", [128, 128], F32)
    ones_bf = P.sb("ones_bf", [128, 128], BF16)
    ones_f = P.sb("ones_f", [128, 128], F32)
    flags = P.sb("flags_sb", [128, 2], F32)
    invc = P.sb("invc_sb", [128, 64], F32)
    P.dma("sync", ident, None, lambda e: e.dma_start(out=ident.t[:, :], in_=ident_d))
    P.dma("sync", flags, None, lambda e: e.dma_start(out=flags.t[:, :], in_=flags_d.partition_broadcast(128)))
    P.dma("sync", invc, None, lambda e: e.dma_start(out=invc.t[:, :], in_=invc_d.partition_broadcast(128)))
    P.op("vector", lambda e: e.memset(ones_bf.t[:, :], 1.0), writes=[ones_bf])
    P.op("vector", lambda e: e.memset(ones_f.t[:, :], 1.0 / 512.0), writes=[ones_f])
    ident_bf = P.sb("ident_bf", [128, 128], BF16)
    P.op("vector", lambda e: e.tensor_copy(out=ident_bf.t[:, :], in_=ident.t[:, :]), reads=[ident], writes=[ident_bf])
    consts = (ident, ones_bf, ones_f, flags, invc, ident_bf)
    d_cat, d_x1, d_x1T, d_r2, d_top, d_tab = (P.dram(n) for n in ("d_cat", "d_x1", "d_x1T", "d_r2", "d_top", "d_tab"))
    d_ys = [P.dram("d_y0"), P.dram("d_y1")]
    d_in = P.dram("d_in")
    d_out = P.dram("d_out")
    As = []
    for l in range(DEPTH):
        nw, ninv = _layer_dims(l)
        A = dict(scr)
        for name, shape in _W_SPECS:
            A[name] = nc.dram_tensor(f"{name}_{l}", shape, F32, kind="ExternalInput").ap()
        A["p_own"] = nc.dram_tensor(f"p_own_{l}", [nw, 256], F32, kind="ExternalInput").ap()
        A["uv_bf"] = nc.dram_tensor(f"uv_bf_{l}", [16384, 2 * D], BF16, kind="Internal").ap()
        As.append(A)
    cv = [P.sb(f"cv{i}", [128, 2, D], BF16) for i in range(2)]

    def conv_gen(l):
        jobs = []
        for nm, off in (("u_tab", 0), ("v_tab", D)):
            src = As[l][nm].rearrange("(p r) d -> p r d", r=128)
            dst = As[l]["uv_bf"].rearrange("(p r) d -> p r d", r=128)[:, :, off:off + D]
            for c in range(64):
                jobs.append((src[:, c * 2:(c + 1) * 2, :], dst[:, c * 2:(c + 1) * 2, :]))

        def load(k):
            b_, (sa, _) = cv[k % 2], jobs[k]
            P.dma("gpsimd", b_, None, lambda e: e.dma_start(out=b_.t[:, :, :], in_=sa))

        def store(k):
            b_, (_, da) = cv[k % 2], jobs[k]
            P.dma("gpsimd", d_tab, b_, lambda e: e.dma_start(out=da, in_=b_.t[:, :, :]), accumulate=True)

        load(0)
        yield
        for k in range(len(jobs)):
            if k + 1 < len(jobs):
                load(k + 1)
                yield
            store(k)
            yield

    x_cur, d_cur = x0, d_in
    for l in range(DEPTH):
        nw, ninv = _layer_dims(l)
        A = As[l]
        A["x_win"] = x_cur
        if l == DEPTH - 1:
            A["y"], d_y = y_out, d_out
        else:
            A["y"], d_y = ybuf[l % 2], d_ys[l % 2]
        _emit_layer(nc, P, ps, A, nw, ninv, consts, (d_cat, d_x1, d_x1T, d_r2, d_top, d_cur, d_y, d_tab), 1 + l, conv_gen(l))
        x_cur, d_cur = A["y"], d_y
    P.finish([d_out])
    return nc


def _attn_bias_layout(rel_bias_l):
    q = np.arange(128)[None, :]
    j = np.arange(640)[:, None]
    dist = q - j + 512
    idx = np.clip(dist, -63, 128) + 63
    g = rel_bias_l[:, idx]
    masked = ((q < 64) & (j >= 576)) | ((q >= 64) & (j < 64))
    g = np.where(masked[None], np.float32(NEG), g).astype(np.float32)
    g = g.reshape(8, 5, 128, 128).transpose(2, 0, 1, 3)
    return np.ascontiguousarray(g.reshape(128, 8 * 5 * 128))


def _fm(v, nchunk):
    return np.ascontiguousarray(v.reshape(nchunk, 128).T)


def layer_weights(inp, l):
    w = {}
    w["w_in"] = np.ascontiguousarray(inp["w_in"][l])
    w["b_in_fm"] = _fm(inp["b_in"][l], 36)
    w["bv_row"] = np.ascontiguousarray(inp["b_in"][l][3584:4608].reshape(1, 1024))
    w["w_pool"] = np.ascontiguousarray(inp["w_pool"][l])
    w["psc_fm"] = _fm(inp["pool_scale"][l], 4)
    w["dww_fm"] = np.ascontiguousarray(inp["dw_w"][l].reshape(31, 4, 128).transpose(2, 1, 0))
    w["dwb_fm"] = _fm(inp["dw_b"][l], 4)
    w["cng_fm"] = _fm(inp["cn_g"][l], 4)
    w["cnb_fm"] = _fm(inp["cn_b"][l], 4)
    w["abias"] = _attn_bias_layout(inp["rel_bias"][l])
    w["w_out"] = np.ascontiguousarray(inp["w_out"][l])
    w["ln1g"] = np.ascontiguousarray(inp["ln1_g"][l].reshape(1, D))
    w["ln1b"] = np.ascontiguousarray(inp["ln1_b"][l].reshape(1, D))
    w["w_q"] = np.ascontiguousarray(inp["w_q"][l])
    w["sub_keys"] = np.ascontiguousarray(inp["sub_keys"][l])
    w["u_tab"] = np.ascontiguousarray(inp["u_tab"][l])
    w["v_tab"] = np.ascontiguousarray(inp["v_tab"][l])
    w["w_ple"] = np.ascontiguousarray(inp["w_ple"][l])
    w["w_pg"] = np.ascontiguousarray(inp["w_pg"][l])
    w["ln2g"] = np.ascontiguousarray(inp["ln2_g"][l].reshape(1, D))
    w["ln2b"] = np.ascontiguousarray(inp["ln2_b"][l].reshape(1, D))
    return w


def core_consts(first_half):
    flags = np.array([[0.0, NEG]] if first_half else [[1.0, 0.0]], np.float32)
    invc = np.zeros((4, 16), np.float32)
    t = np.arange(16)
    for g, wv in enumerate(POOL_W):
        cnt = np.minimum(t + 1, wv) if first_half else np.full(16, wv)
        invc[g] = (1.0 / cnt).astype(np.float32)
    return flags, invc.reshape(1, 64)


_PROG_CACHE = {}


def kernel(**inputs):
    inp = {k: np.asarray(v) for k, v in inputs.items()}
    x = np.ascontiguousarray(inp["x"], dtype=np.float32)
    B, S, _ = x.shape
    assert S == 2 * NW_FINAL and inp["w_in"].shape[0] == DEPTH
    if "nc" not in _PROG_CACHE:
        _PROG_CACHE["nc"] = build_program()
    nc = _PROG_CACHE["nc"]
    ident = np.eye(128, dtype=np.float32)
    wl = [layer_weights(inp, l) for l in range(DEPTH)]
    in_maps = []
    for c in range(2 * B):
        b, half = c // 2, c % 2
        end = (half + 1) * NW_FINAL
        flags, invc = core_consts(half == 0)
        m = {"flags": flags, "invc": invc, "ident": ident}

        def window(arr, n):
            lo = end - n
            if lo >= 0:
                return np.ascontiguousarray(arr[lo:end])
            out = np.zeros((n,) + arr.shape[1:], np.float32)
            out[-lo:] = arr[0:end]
            return out

        m["x_win0"] = window(x[b], _layer_dims(0)[0] + HALO)
        for l in range(DEPTH):
            m[f"p_own_{l}"] = window(inp["p"][l, b], _layer_dims(l)[0])
            for k, v in wl[l].items():
                m[f"{k}_{l}"] = v
        in_maps.append(m)
    res = run_bass_kernel_spmd(nc, in_maps, core_ids=list(range(2 * B)))
    out = np.empty((B, S, D), np.float32)
    for c in range(2 * B):
        b, half = c // 2, c % 2
        out[b, half * NW_FINAL:(half + 1) * NW_FINAL] = res.results[c]["y"]
    return out
```
